# Optimizing a Trainium2 kernel written in Bass

```python
import math
import jax, jax.numpy as jnp
from jax import lax
import numpy as np

D_MODEL = 1024
BATCH = 8
SEQ = 2048
DEPTH = 1
DEC_BATCH = 128
DEC_SEQ = 4
PAST_LEN = 2048
PAGE_SIZE = 128

N_ATT_HEADS = 8
ATT_HEAD_DIM = 64
D_ATT = N_ATT_HEADS * ATT_HEAD_DIM
DILATED_CFGS = ((128, 1), (512, 4), (2048, 16))
WIN_MAX = 2048
SSM_CH = 16
D_SSM = D_MODEL - D_ATT
N_SSM_GROUPS = D_SSM // SSM_CH
SSM_STATE = 64
D_IN = 3 * D_ATT + D_SSM
N_MEM = 256
MEM_HEADS = 4
MEM_HEAD_DIM = D_MODEL // MEM_HEADS
D_FF = -(-8 * D_MODEL // (3 * 256)) * 256
DEEPNORM_ALPHA = (2 * DEPTH) ** 0.25
DEEPNORM_BETA = (8 * DEPTH) ** -0.25
EPS = 1e-5
NEG = -1e30

kernel_name = "hymba_s5_longnet_memxattn_deepnorm_step"


def layer_norm(x, g, b):
    xf = x.astype(jnp.float32)
    mu = jnp.mean(xf, -1, keepdims=True)
    var = jnp.mean(jnp.square(xf - mu), -1, keepdims=True)
    return ((xf - mu) * lax.rsqrt(var + EPS) * g + b).astype(x.dtype)


def rms_norm(x, g):
    xf = x.astype(jnp.float32)
    return (xf * lax.rsqrt(jnp.mean(jnp.square(xf), -1, keepdims=True) + EPS) * g).astype(x.dtype)


def combine_by_denominators(outs, lses):
    lse = jnp.stack(lses, 0)
    w = jnp.exp(lse - jax.nn.logsumexp(lse, axis=0, keepdims=True))
    return jnp.sum(w[..., None] * jnp.stack(outs, 0), axis=0)


def dilated_attention_prompt(q, k, v):
    B, S, H, dh = q.shape
    scale = dh ** -0.5
    outs, lses = [], []
    for window, d in DILATED_CFGS:
        reach = window // d
        blk = reach
        span = d * blk
        Sp = -(-S // span) * span
        L = Sp // d
        nb = L // blk
        pad = ((0, 0), (0, Sp - S), (0, 0), (0, 0))

        def to_blocks(t):
            t = jnp.pad(t, pad).reshape(B, L, d, H, dh).transpose(0, 2, 1, 3, 4)
            return t.reshape(B, d, nb, blk, H, dh)

        qb, kb, vb = to_blocks(q), to_blocks(k), to_blocks(v)
        shift = ((0, 0), (0, 0), (1, 0), (0, 0), (0, 0), (0, 0))
        kk = jnp.concatenate([jnp.pad(kb, shift)[:, :, :-1], kb], axis=3)
        vv = jnp.concatenate([jnp.pad(vb, shift)[:, :, :-1], vb], axis=3)
        s = jnp.einsum("brnqhd,brnkhd->brnhqk", qb, kk).astype(jnp.float32) * scale
        qi = jnp.arange(blk)[:, None]
        kj = jnp.arange(2 * blk)[None, :]
        dist = qi + blk - kj
        band = (dist >= 0) & (dist <= reach)
        real = (jnp.arange(nb)[:, None, None] > 0) | (kj[None] >= blk)
        valid = (band[None] & real)[:, None]
        s = jnp.where(valid, s, NEG)
        m = jnp.max(s, -1, keepdims=True)
        p = jnp.exp(s - m)
        den = jnp.sum(p, -1, keepdims=True)
        o = jnp.einsum("brnhqk,brnkhd->brnqhd", p, vv.astype(jnp.float32))
        o = o / jnp.transpose(den, (0, 1, 2, 4, 3, 5))
        lse = (m + jnp.log(den))[..., 0]
        o = o.reshape(B, d, L, H, dh).transpose(0, 2, 1, 3, 4).reshape(B, Sp, H, dh)[:, :S]
        lse = lse.transpose(0, 1, 2, 4, 3).reshape(B, d, L, H).transpose(0, 2, 1, 3).reshape(B, Sp, H)[:, :S]
        outs.append(o)
        lses.append(lse)
    return combine_by_denominators(outs, lses).astype(q.dtype)


def dilated_attention_sample(q, k, v, cache_k, cache_v):
    W = cache_k.shape[1]
    T = q.shape[1]
    scale = q.shape[-1] ** -0.5
    kk = jnp.concatenate([cache_k.astype(k.dtype), k], axis=1)
    vv = jnp.concatenate([cache_v.astype(v.dtype), v], axis=1)
    outs, lses = [], []
    for window, d in DILATED_CFGS:
        steps = jnp.arange(window // d + 1)
        idx = W + jnp.arange(T)[:, None] - steps[None, :] * d
        valid = idx >= 0
        idxc = jnp.maximum(idx, 0)
        kg, vg = kk[:, idxc], vv[:, idxc]
        s = jnp.einsum("bthd,btmhd->bhtm", q, kg).astype(jnp.float32) * scale
        s = jnp.where(valid[None, None], s, NEG)
        m = jnp.max(s, -1, keepdims=True)
        p = jnp.exp(s - m)
        den = jnp.sum(p, -1, keepdims=True)
        o = jnp.einsum("bhtm,btmhd->bthd", p, vg.astype(jnp.float32))
        o = o / jnp.transpose(den, (0, 2, 1, 3))
        lse = jnp.transpose((m + jnp.log(den))[..., 0], (0, 2, 1))
        outs.append(o)
        lses.append(lse)
    return combine_by_denominators(outs, lses).astype(q.dtype)


def s5_mixer(u, h0_re, h0_im, a_re, a_im, log_dt, b_re, b_im, c_re, c_im, d_skip, w_glu, b_glu):
    Bn, L, _ = u.shape
    uf = u.astype(jnp.float32).reshape(Bn, L, N_SSM_GROUPS, SSM_CH)
    dt = jnp.exp(log_dt.astype(jnp.float32))[:, None]
    ar, ai = a_re.astype(jnp.float32), a_im.astype(jnp.float32)
    mag = jnp.exp(dt * ar)
    abar_re, abar_im = mag * jnp.cos(dt * ai), mag * jnp.sin(dt * ai)
    den = ar * ar + ai * ai
    nr, ni = abar_re - 1.0, abar_im
    coef_re = (nr * ar + ni * ai) / den
    coef_im = (ni * ar - nr * ai) / den
    br, bi = b_re.astype(jnp.float32), b_im.astype(jnp.float32)
    bbar_re = coef_re[..., None] * br - coef_im[..., None] * bi
    bbar_im = coef_re[..., None] * bi + coef_im[..., None] * br
    bu_re = jnp.einsum("blgh,gph->blgp", uf, bbar_re)
    bu_im = jnp.einsum("blgh,gph->blgp", uf, bbar_im)
    h0r, h0i = h0_re.astype(jnp.float32), h0_im.astype(jnp.float32)
    bu_re = bu_re.at[:, 0].add(abar_re * h0r - abar_im * h0i)
    bu_im = bu_im.at[:, 0].add(abar_re * h0i + abar_im * h0r)
    a_full_re = jnp.broadcast_to(abar_re, bu_re.shape)
    a_full_im = jnp.broadcast_to(abar_im, bu_im.shape)

    def combine(e1, e2):
        a1r, a1i, b1r, b1i = e1
        a2r, a2i, b2r, b2i = e2
        return (a2r * a1r - a2i * a1i, a2r * a1i + a2i * a1r,
                a2r * b1r - a2i * b1i + b2r, a2r * b1i + a2i * b1r + b2i)

    _, _, hr, hi = lax.associative_scan(combine, (a_full_re, a_full_im, bu_re, bu_im), axis=1)
    y = (jnp.einsum("ghp,blgp->blgh", c_re.astype(jnp.float32), hr)
         - jnp.einsum("ghp,blgp->blgh", c_im.astype(jnp.float32), hi)
         + d_skip.astype(jnp.float32) * uf).reshape(Bn, L, D_SSM)
    g = jax.nn.gelu(y)
    out = g * jax.nn.sigmoid(g @ w_glu.astype(jnp.float32) + b_glu.astype(jnp.float32))
    return out.astype(u.dtype), hr[:, -1], hi[:, -1]


def memory_attention(x, mem_k, mem_v, w_mem_q, w_mem_o):
    Bn, L, _ = x.shape
    q = (x @ w_mem_q).reshape(Bn, L, MEM_HEADS, MEM_HEAD_DIM)
    s = jnp.einsum("blhd,bmhd->bhlm", q, mem_k.astype(q.dtype)).astype(jnp.float32) * MEM_HEAD_DIM ** -0.5
    p = jax.nn.softmax(s, axis=-1)
    o = jnp.einsum("bhlm,bmhd->blhd", p, mem_v.astype(jnp.float32)).astype(x.dtype)
    return o.reshape(Bn, L, D_MODEL) @ w_mem_o


def trunk_layer(x, h0_re, h0_im, mem_k, mem_v, attend,
                w_in, g_att, g_ssm, a_re, a_im, log_dt, b_re, b_im, c_re, c_im, d_skip, w_glu, b_glu, w_out,
                ln1_g, ln1_b, w_mem_q, w_mem_o, ln2_g, ln2_b, w_gate, w_up, w_down, ln3_g, ln3_b):
    Bn, L, _ = x.shape
    proj = x @ w_in
    q, k, v, u = jnp.split(proj, [D_ATT, 2 * D_ATT, 3 * D_ATT], axis=-1)
    q = q.reshape(Bn, L, N_ATT_HEADS, ATT_HEAD_DIM)
    k = k.reshape(Bn, L, N_ATT_HEADS, ATT_HEAD_DIM)
    v = v.reshape(Bn, L, N_ATT_HEADS, ATT_HEAD_DIM)
    o_att = attend(q, k, v).reshape(Bn, L, D_ATT)
    y_ssm, h_re, h_im = s5_mixer(u, h0_re, h0_im, a_re, a_im, log_dt, b_re, b_im, c_re, c_im, d_skip, w_glu, b_glu)
    mixed = jnp.concatenate([rms_norm(o_att, g_att), rms_norm(y_ssm, g_ssm)], axis=-1) @ w_out
    x = layer_norm(DEEPNORM_ALPHA * x + mixed, ln1_g, ln1_b)
    x = layer_norm(DEEPNORM_ALPHA * x + memory_attention(x, mem_k, mem_v, w_mem_q, w_mem_o), ln2_g, ln2_b)
    ffn = (jax.nn.silu(x @ w_gate) * (x @ w_up)) @ w_down
    x = layer_norm(DEEPNORM_ALPHA * x + ffn, ln3_g, ln3_b)
    return x, k, v, h_re, h_im


def setup_inputs(seed: int = 0) -> dict:
    key = jax.random.key(seed)
    ks = iter(jax.random.split(key, 64))
    f32 = jnp.float32

    def nrm(shape, scale=1.0):
        return scale * jax.random.normal(next(ks), shape, f32)

    w_buf = min(WIN_MAX, PAST_LEN)
    G, P = N_SSM_GROUPS, SSM_STATE
    n = jnp.arange(P, dtype=f32)
    return {
        "x_prompt": nrm((BATCH, SEQ, D_MODEL)),
        "x_sample": nrm((DEC_BATCH, DEC_SEQ, D_MODEL)),
        "cache_win_k": nrm((DEPTH, DEC_BATCH, w_buf, N_ATT_HEADS, ATT_HEAD_DIM)),
        "cache_win_v": nrm((DEPTH, DEC_BATCH, w_buf, N_ATT_HEADS, ATT_HEAD_DIM)),
        "state_ssm_re": nrm((DEPTH, DEC_BATCH, G, P), 0.3),
        "state_ssm_im": nrm((DEPTH, DEC_BATCH, G, P), 0.3),
        "cache_mem_k": nrm((DEPTH, DEC_BATCH, N_MEM, MEM_HEADS, MEM_HEAD_DIM)),
        "cache_mem_v": nrm((DEPTH, DEC_BATCH, N_MEM, MEM_HEADS, MEM_HEAD_DIM)),
        "mem_prompt": nrm((BATCH, N_MEM, D_MODEL)),
        "w_in": nrm((DEPTH, D_MODEL, D_IN), D_MODEL ** -0.5),
        "g_att": 1.0 + nrm((DEPTH, D_ATT), 0.01),
        "g_ssm": 1.0 + nrm((DEPTH, D_SSM), 0.01),
        "ssm_a_re": -0.5 * jnp.exp(nrm((DEPTH, G, P), 0.05)),
        "ssm_a_im": math.pi * n + nrm((DEPTH, G, P), 0.01),
        "ssm_log_dt": jax.random.uniform(next(ks), (DEPTH, G), f32, math.log(1e-3), math.log(1e-1)),
        "ssm_b_re": nrm((DEPTH, G, P, SSM_CH), (2 * SSM_CH) ** -0.5),
        "ssm_b_im": nrm((DEPTH, G, P, SSM_CH), (2 * SSM_CH) ** -0.5),
        "ssm_c_re": nrm((DEPTH, G, SSM_CH, P), (2 * P) ** -0.5),
        "ssm_c_im": nrm((DEPTH, G, SSM_CH, P), (2 * P) ** -0.5),
        "ssm_d": nrm((DEPTH, G, SSM_CH)),
        "w_glu": nrm((DEPTH, D_SSM, D_SSM), D_SSM ** -0.5),
        "b_glu": nrm((DEPTH, D_SSM), 0.02),
        "w_out": nrm((DEPTH, D_ATT + D_SSM, D_MODEL), (D_ATT + D_SSM) ** -0.5 * DEEPNORM_BETA),
        "ln1_g": 1.0 + nrm((DEPTH, D_MODEL), 0.01),
        "ln1_b": nrm((DEPTH, D_MODEL), 0.01),
        "w_mem_q": nrm((DEPTH, D_MODEL, D_MODEL), D_MODEL ** -0.5),
        "w_mem_k": nrm((DEPTH, D_MODEL, D_MODEL), D_MODEL ** -0.5),
        "w_mem_v": nrm((DEPTH, D_MODEL, D_MODEL), D_MODEL ** -0.5),
        "w_mem_o": nrm((DEPTH, D_MODEL, D_MODEL), D_MODEL ** -0.5 * DEEPNORM_BETA),
        "ln2_g": 1.0 + nrm((DEPTH, D_MODEL), 0.01),
        "ln2_b": nrm((DEPTH, D_MODEL), 0.01),
        "w_gate": nrm((DEPTH, D_MODEL, D_FF), D_MODEL ** -0.5),
        "w_up": nrm((DEPTH, D_MODEL, D_FF), D_MODEL ** -0.5),
        "w_down": nrm((DEPTH, D_FF, D_MODEL), D_FF ** -0.5 * DEEPNORM_BETA),
        "ln3_g": 1.0 + nrm((DEPTH, D_MODEL), 0.01),
        "ln3_b": nrm((DEPTH, D_MODEL), 0.01),
    }


def reference(x_prompt, x_sample, cache_win_k, cache_win_v, state_ssm_re, state_ssm_im, cache_mem_k, cache_mem_v,
              mem_prompt, w_in, g_att, g_ssm, ssm_a_re, ssm_a_im, ssm_log_dt, ssm_b_re, ssm_b_im, ssm_c_re, ssm_c_im,
              ssm_d, w_glu, b_glu, w_out, ln1_g, ln1_b, w_mem_q, w_mem_k, w_mem_v, w_mem_o, ln2_g, ln2_b,
              w_gate, w_up, w_down, ln3_g, ln3_b):
    Bp, S, _ = x_prompt.shape
    keep = min(WIN_MAX, S)
    y_p, y_s = x_prompt, x_sample
    wk_p, wv_p, wk_s, wv_s = [], [], [], []
    hr_p, hi_p, hr_s, hi_s = [], [], [], []
    mk_p, mv_p = [], []
    for l in range(DEPTH):
        lw = (w_in[l], g_att[l], g_ssm[l], ssm_a_re[l], ssm_a_im[l], ssm_log_dt[l], ssm_b_re[l], ssm_b_im[l],
              ssm_c_re[l], ssm_c_im[l], ssm_d[l], w_glu[l], b_glu[l], w_out[l], ln1_g[l], ln1_b[l],
              w_mem_q[l], w_mem_o[l], ln2_g[l], ln2_b[l], w_gate[l], w_up[l], w_down[l], ln3_g[l], ln3_b[l])
        mem_k = (mem_prompt @ w_mem_k[l]).reshape(Bp, N_MEM, MEM_HEADS, MEM_HEAD_DIM)
        mem_v = (mem_prompt @ w_mem_v[l]).reshape(Bp, N_MEM, MEM_HEADS, MEM_HEAD_DIM)
        h0 = jnp.zeros((Bp, N_SSM_GROUPS, SSM_STATE), jnp.float32)
        y_p, k_p, v_p, h_re_p, h_im_p = trunk_layer(y_p, h0, h0, mem_k, mem_v, dilated_attention_prompt, *lw)
        ck, cv = cache_win_k[l], cache_win_v[l]
        attend_s = lambda q, k, v, ck=ck, cv=cv: dilated_attention_sample(q, k, v, ck, cv)
        y_s, k_s, v_s, h_re_s, h_im_s = trunk_layer(y_s, state_ssm_re[l], state_ssm_im[l],
                                                    cache_mem_k[l], cache_mem_v[l], attend_s, *lw)
        wk_p.append(k_p[:, S - keep:])
        wv_p.append(v_p[:, S - keep:])
        wk_s.append(k_s)
        wv_s.append(v_s)
        hr_p.append(h_re_p)
        hi_p.append(h_im_p)
        hr_s.append(h_re_s)
        hi_s.append(h_im_s)
        mk_p.append(mem_k)
        mv_p.append(mem_v)
    return (y_p, y_s,
            jnp.stack(wk_p), jnp.stack(wv_p), jnp.stack(wk_s), jnp.stack(wv_s),
            jnp.stack(hr_p), jnp.stack(hi_p), jnp.stack(hr_s), jnp.stack(hi_s),
            jnp.stack(mk_p), jnp.stack(mv_p))
```

```python
import math
from contextlib import ExitStack
import numpy as np
import concourse.bass as bass
import concourse.mybir as mybir
from concourse.bass_utils import run_bass_kernel_spmd

F32 = mybir.dt.float32
BF16 = mybir.dt.bfloat16
I32 = mybir.dt.int32
AF = mybir.ActivationFunctionType
ALU = mybir.AluOpType

NCORES = 8
D = 1024
S = 2048
SB = 16
ST = 64
T = S + ST
DIN = 2048
DATT = 512
DFF = 2816
NMEM = 256
ALPHA = 2.0 ** 0.25
EPS = 1e-5
STAGE = 99


class _Stop(Exception):
    pass


class Ctx:
    def __init__(self, nc, es):
        self.nc = nc
        self.eng = {"pe": nc.tensor, "act": nc.scalar, "dve": nc.vector, "pool": nc.gpsimd, "sp": nc.sync}
        self.sem = {}
        self.cnt = {}
        for e in self.eng:
            self.sem[e] = es.enter_context(nc.semaphore("s_" + e))
            self.cnt[e] = 0
        self.R = 12
        self.dsem = {}
        self.dcnt = {}
        self.didx = {}
        for q in ("sp", "act", "pool"):
            self.dsem[q] = [es.enter_context(nc.semaphore("d_%s%d" % (q, i))) for i in range(self.R)]
            self.dcnt[q] = [0] * self.R
            self.didx[q] = 0
        self.seen = {e: {} for e in self.eng}
        self.bufs = {}
        self.dma_tokens = []
        self.rr = 0
        self.enabled = True

    def _semof(self, key):
        if isinstance(key, tuple):
            return self.dsem[key[0]][key[1]]
        return self.sem[key]

    def wait(self, e, tok):
        key, val = tok
        if self.seen[e].get(key, 0) >= val:
            return
        self.eng[e].wait_ge(self._semof(key), val)
        self.seen[e][key] = val

    def _deps(self, reads, writes):
        toks = {}
        for k in reads:
            b = self.bufs.get(k)
            if b and b["w"]:
                key, val = b["w"]
                toks[key] = max(toks.get(key, 0), val)
        for k in writes:
            b = self.bufs.get(k)
            if b:
                if b["w"]:
                    key, val = b["w"]
                    toks[key] = max(toks.get(key, 0), val)
                for key, val in b["r"].items():
                    toks[key] = max(toks.get(key, 0), val)
        return toks

    def _record(self, tok, reads, writes):
        for k in reads:
            b = self.bufs.setdefault(k, {"w": None, "r": {}})
            b["r"][tok[0]] = max(b["r"].get(tok[0], 0), tok[1])
        for k in writes:
            self.bufs[k] = {"w": tok, "r": {}}

    def op(self, e, fn, reads=(), writes=()):
        if not self.enabled:
            return None
        deps = self._deps(reads, writes)
        if e == "pe":
            own = 0
            for k in reads:
                b = self.bufs.get(k)
                if b and b["w"] and b["w"][0] == "pe":
                    own = max(own, b["w"][1])
            for k in writes:
                b = self.bufs.get(k)
                if b and b["r"].get("pe"):
                    own = max(own, 0)
            if own == 0:
                deps.pop("pe", None)
            else:
                deps["pe"] = own
        for key, val in deps.items():
            self.wait(e, (key, val))
        inst = fn(self.eng[e])
        self.cnt[e] += 1
        inst.then_inc(self.sem[e], 1)
        tok = (e, self.cnt[e])
        self._record(tok, reads, writes)
        return tok

    def dma(self, q, out, in_, reads=(), writes=(), **kw):
        if not self.enabled:
            return None
        for key, val in self._deps(reads, writes).items():
            self.wait(q, (key, val))
        i = self.didx[q]
        self.didx[q] = (i + 1) % self.R
        inst = self.eng[q].dma_start(out=out, in_=in_, **kw)
        self.dcnt[q][i] += 16
        inst.then_inc(self.dsem[q][i], 16)
        tok = ((q, i), self.dcnt[q][i])
        self.dma_tokens.append(tok)
        self._record(tok, reads, writes)
        return tok

    def barrier(self):
        for e in self.eng:
            if e != "sp" and self.cnt[e] > 0:
                self.wait("sp", (e, self.cnt[e]))
        for tok in self.dma_tokens:
            self.wait("sp", tok)
        self.dma_tokens = []
        inst = self.eng["sp"].nop()
        self.cnt["sp"] += 1
        inst.then_inc(self.sem["sp"], 1)
        for e in self.eng:
            if e != "sp":
                self.wait(e, ("sp", self.cnt["sp"]))
        self.bufs = {}

    def finish(self):
        self.barrier()


class _Catch:
    def __init__(self, cx):
        self.cx = cx

    def __enter__(self):
        return self

    def __exit__(self, et, ev, tb):
        if et is _Stop:
            self.cx.barrier()
            return True
        return False


def build():
    STG = _DEV.get('stage', 99)
    nc = bass.Bass("TRN2", target_bir_lowering=False)

    def din(name, shape):
        return nc.dram_tensor(name, list(shape), F32, kind="ExternalInput").ap()

    def dout(name, shape):
        return nc.dram_tensor(name, list(shape), F32, kind="ExternalOutput").ap()

    x_p = din("x_p", [S, D])
    x_s = din("x_s", [ST, D])
    cwk = din("cwk", [SB, 2048, DATT])
    cwv = din("cwv", [SB, 2048, DATT])
    s_re = din("s_re", [SB, 2048])
    s_im = din("s_im", [SB, 2048])
    cmk = din("cmk", [SB, NMEM, D])
    cmv = din("cmv", [SB, NMEM, D])
    memp = din("memp", [NMEM, D])
    w_in = din("w_in", [D, DIN])
    g_att = din("g_att", [DATT])
    g_ssm = din("g_ssm", [DATT])
    a_re = din("a_re", [2048])
    a_im = din("a_im", [2048])
    log_dt = din("log_dt", [32])
    b_re = din("b_re", [2048, 16])
    b_im = din("b_im", [2048, 16])
    c_re = din("c_re", [512, 64])
    c_im = din("c_im", [512, 64])
    d_skip = din("d_skip", [512])
    w_glu = din("w_glu", [512, 512])
    b_glu = din("b_glu", [512])
    w_out = din("w_out", [D, D])
    ln_g = [din("ln%d_g" % i, [D]) for i in (1, 2, 3)]
    ln_b = [din("ln%d_b" % i, [D]) for i in (1, 2, 3)]
    w_mq = din("w_mq", [D, D])
    w_mk = din("w_mk", [D, D])
    w_mv = din("w_mv", [D, D])
    w_mo = din("w_mo", [D, D])
    w_gate = din("w_gate", [D, DFF])
    w_up = din("w_up", [D, DFF])
    w_down = din("w_down", [DFF, D])

    y_p = dout("y_p", [S, D])
    y_s = dout("y_s", [ST, D])
    wk_p = dout("wk_p", [S, DATT])
    wv_p = dout("wv_p", [S, DATT])
    wk_s = dout("wk_s", [ST, DATT])
    wv_s = dout("wv_s", [ST, DATT])
    hr_p = dout("hr_p", [2048])
    hi_p = dout("hi_p", [2048])
    hr_s = dout("hr_s", [SB, 2048])
    hi_s = dout("hi_s", [SB, 2048])
    mk_p = dout("mk_p", [NMEM, D])
    mv_p = dout("mv_p", [NMEM, D])

    with ExitStack() as es:
        cx = Ctx(nc, es)
        _UID = [0]

        def sb(name, shape, dt=F32, stack=es):
            return stack.enter_context(nc.sbuf_tensor(name, list(shape), dt))

        def ps(name, shape, dt=F32, stack=es):
            return stack.enter_context(nc.psum_tensor(name, list(shape), dt))

        ident_b = sb("ident_b", [128, 128], BF16)
        ident_f = sb("ident_f", [128, 128], F32)
        ones_b = sb("ones_b", [128, 128], BF16)
        eps_t = sb("eps_t", [128, 1], F32)
        yssm = sb("yssm", [128, 4, T], F32)

        with ExitStack() as ph:
            it = sb("it_i", [128, 128], I32, ph)
            itf = sb("it_f", [128, 128], F32, ph)
            cx.op("pool", lambda g: g.iota(it[:], pattern=[[1, 128]], base=0, channel_multiplier=-1), writes=["it"])
            cx.op("dve", lambda v: v.tensor_copy(out=itf[:], in_=it[:]), reads=["it"], writes=["itf"])
            cx.op("dve", lambda v: v.tensor_scalar(out=ident_f[:], in0=itf[:], scalar1=0.0, scalar2=None,
                                                   op0=ALU.is_equal), reads=["itf"], writes=["ident_f"])
            cx.op("dve", lambda v: v.tensor_copy(out=ident_b[:], in_=ident_f[:]), reads=["ident_f"], writes=["ident_b"])
            cx.op("dve", lambda v: v.memset(ones_b[:], 1.0), writes=["ones_b"])
            cx.op("dve", lambda v: v.memset(eps_t[:], EPS), writes=["eps_t"])
            cx.barrier()

        def phase1(xT, srcs=None):
          _UID[0] += 1
          u_ = 'a%d_' % _UID[0]
          with ExitStack() as ph:
              xs = [sb(u_ + "xs%d" % i, [128, D], F32, ph) for i in range(2)]
              xb = [sb(u_ + "xb%d" % i, [128, D], BF16, ph) for i in range(2)]
              pt = [ps(u_ + "pt%d" % i, [128, 8, 128], BF16, ph) for i in range(2)]
              if srcs is None:
                  srcs = [(x_p[tt * 128:(tt + 1) * 128, :], 128) for tt in range(16)] + [(x_s[:, :], ST)]
              for tt, (src, rows) in enumerate(srcs):
                  i = tt % 2
                  cx.dma("sp", xs[i][0:rows, :], src, writes=["xs%d" % i])
                  cx.op("act" if tt % 2 else "dve",
                        (lambda e, i=i, rows=rows: e.activation(out=xb[i][0:rows, :], in_=xs[i][0:rows, :], func=AF.Copy))
                        if tt % 2 else
                        (lambda e, i=i, rows=rows: e.tensor_copy(out=xb[i][0:rows, :], in_=xs[i][0:rows, :])),
                        reads=["xs%d" % i], writes=["xb%d" % i])
                  for kc in range(8):
                      cx.op("pe", lambda pe, i=i, kc=kc, rows=rows: pe.transpose(
                          out=pt[i][:, kc, 0:rows], in_=xb[i][0:rows, kc * 128:(kc + 1) * 128],
                          identity=ident_b[0:rows, 0:rows]),
                          reads=["xb%d" % i, "ident_b"], writes=["pt%d_%d" % (i, kc)])
                  cx.op("dve" if tt % 2 else "act",
                        (lambda e, i=i, rows=rows, tt=tt: e.tensor_copy(out=xT[:, :, tt * 128:tt * 128 + rows], in_=pt[i][:, :, 0:rows]))
                        if tt % 2 else
                        (lambda e, i=i, rows=rows, tt=tt: e.activation(out=xT[:, :, tt * 128:tt * 128 + rows], in_=pt[i][:, :, 0:rows], func=AF.Copy)),
                        reads=["pt%d_%d" % (i, kc) for kc in range(8)], writes=["xT"])
              cx.barrier()

        def phase2(xT, cbs, qT, kT, uT):
          _UID[0] += 1
          u_ = 'b%d_' % _UID[0]
          with ExitStack() as ph:
              wst = [sb(u_ + "wst%d" % i, [128, 8, 512], F32, ph) for i in range(2)]
              wbf = [sb(u_ + "wbf%d" % i, [128, 8, 512], BF16, ph) for i in range(2)]
              ost = [sb(u_ + "ost%d" % i, [128, 512], F32, ph) for i in range(2)]
              pp = [ps(u_ + "pp%d" % i, [128, 512], F32, ph) for i in range(4)]
              w_v = w_in.rearrange("(kc p) c -> p kc c", p=128)
              ppi = 0
              osi = 0
              for cb in cbs:
                  i = cb % 2
                  cx.dma("sp" if cb % 2 == 0 else "act", wst[i][:, :, :], w_v[:, :, cb * 512:(cb + 1) * 512], writes=["wst%d" % i])
                  cx.op("pool", lambda e, i=i: e.tensor_copy(out=wbf[i][:, 0:4, :], in_=wst[i][:, 0:4, :]),
                        reads=["wst%d" % i], writes=["wbf%d_a" % i])
                  cx.op("dve", lambda e, i=i: e.tensor_copy(out=wbf[i][:, 4:8, :], in_=wst[i][:, 4:8, :]),
                        reads=["wst%d" % i], writes=["wbf%d_b" % i])
                  wkeys = ["wbf%d_a" % i, "wbf%d_b" % i]
                  if cb in (0, 1, 3):
                      dst = {0: qT, 1: kT, 3: uT}[cb]
                      dkey = {0: "qT", 1: "kT", 3: "uT"}[cb]
                      for sbk in range(4):
                          for nt in range(5):
                              t0 = nt * 512
                              n = 512 if nt < 4 else ST
                              p = ppi % 4
                              ppi += 1
                              for kc in range(8):
                                  cx.op("pe", lambda pe, p=p, i=i, kc=kc, sbk=sbk, t0=t0, n=n: pe.matmul(
                                      pp[p][:, 0:n], lhsT=wbf[i][:, kc, sbk * 128:(sbk + 1) * 128], rhs=xT[:, kc, t0:t0 + n],
                                      start=(kc == 0), stop=(kc == 7)),
                                      reads=wkeys + ["xT"], writes=["pp%d" % p])
                              if ppi % 2:
                                  cx.op("act", lambda e, p=p, sbk=sbk, t0=t0, n=n, dst=dst: e.activation(
                                      out=dst[:, sbk, t0:t0 + n], in_=pp[p][:, 0:n], func=AF.Copy),
                                      reads=["pp%d" % p], writes=[dkey + "%d_%d" % (sbk, nt)])
                              else:
                                  cx.op("dve", lambda e, p=p, sbk=sbk, t0=t0, n=n, dst=dst: e.tensor_copy(
                                      out=dst[:, sbk, t0:t0 + n], in_=pp[p][:, 0:n]),
                                      reads=["pp%d" % p], writes=[dkey + "%d_%d" % (sbk, nt)])
                  if cb in (1, 2):
                      dp, dsm = (wk_p, wk_s) if cb == 1 else (wv_p, wv_s)
                      for tt in range(17):
                          rows = 128 if tt < 16 else ST
                          p = ppi % 4
                          ppi += 1
                          for kc in range(8):
                              cx.op("pe", lambda pe, p=p, i=i, kc=kc, tt=tt, rows=rows: pe.matmul(
                                  pp[p][0:rows, :], lhsT=xT[:, kc, tt * 128:tt * 128 + rows], rhs=wbf[i][:, kc, :],
                                  start=(kc == 0), stop=(kc == 7)),
                                  reads=wkeys + ["xT"], writes=["pp%d" % p])
                          o = osi % 2
                          osi += 1
                          if osi % 2:
                              cx.op("act", lambda e, p=p, o=o, rows=rows: e.activation(
                                  out=ost[o][0:rows, :], in_=pp[p][0:rows, :], func=AF.Copy),
                                  reads=["pp%d" % p], writes=["ost%d" % o])
                          else:
                              cx.op("dve", lambda e, p=p, o=o, rows=rows: e.tensor_copy(
                                  out=ost[o][0:rows, :], in_=pp[p][0:rows, :]),
                                  reads=["pp%d" % p], writes=["ost%d" % o])
                          dstd = dp[tt * 128:(tt + 1) * 128, :] if tt < 16 else dsm[:, :]
                          cx.dma("sp", dstd, ost[o][0:rows, :], reads=["ost%d" % o], writes=["wkv_out"])
              cx.barrier()

        pxu = ExitStack()
        uT = sb("uT", [128, 4, T], BF16, pxu)
        with ExitStack() as px:
            xT = sb("xT", [128, 8, T], BF16, px)
            phase1(xT)
            phase2(xT, [3], None, None, uT)
        TWO_PI = 2.0 * math.pi
        C1 = 6.28125
        C2 = round((TWO_PI - C1) * (1 << 24)) / float(1 << 24)
        C3 = TWO_PI - C1 - C2

        def V(fn, r=(), w=()):
            return cx.op("dve", fn, reads=r, writes=w)

        def A(fn, r=(), w=()):
            return cx.op("act", fn, reads=r, writes=w)

        def G(fn, r=(), w=()):
            return cx.op("pool", fn, reads=r, writes=w)

        def P(fn, r=(), w=()):
            return cx.op("pe", fn, reads=r, writes=w)

        def TT(e, out, in0, in1, op, r, w):
            return cx.op(e, lambda en: en.tensor_tensor(out=out, in0=in0, in1=in1, op=op), reads=r, writes=w)

        def cis(ph, x, n, cos_o, sin_o, tag, xk, ok):
            u = sb(tag + "u", [128, n], F32, ph)
            ki = sb(tag + "ki", [128, n], I32, ph)
            kf = sb(tag + "kf", [128, n], F32, ph)
            r = sb(tag + "r", [128, n], F32, ph)
            m = sb(tag + "m", [128, n], F32, ph)
            k = tag
            V(lambda e: e.tensor_scalar(out=u[:], in0=x, scalar1=1.0 / TWO_PI, scalar2=None, op0=ALU.mult), [xk], [k + "u"])
            V(lambda e: e.tensor_copy(out=ki[:], in_=u[:]), [k + "u"], [k + "ki"])
            V(lambda e: e.tensor_copy(out=kf[:], in_=ki[:]), [k + "ki"], [k + "kf"])
            V(lambda e: e.scalar_tensor_tensor(out=r[:], in0=kf[:], scalar=-C1, in1=x, op0=ALU.mult, op1=ALU.add), [k + "kf", xk], [k + "r"])
            V(lambda e: e.scalar_tensor_tensor(out=r[:], in0=kf[:], scalar=-C2, in1=r[:], op0=ALU.mult, op1=ALU.add), [k + "kf", k + "r"], [k + "r"])
            V(lambda e: e.scalar_tensor_tensor(out=r[:], in0=kf[:], scalar=-C3, in1=r[:], op0=ALU.mult, op1=ALU.add), [k + "kf", k + "r"], [k + "r"])
            V(lambda e: e.tensor_scalar(out=m[:], in0=r[:], scalar1=math.pi, scalar2=-TWO_PI, op0=ALU.is_gt, op1=ALU.mult), [k + "r"], [k + "m"])
            TT("dve", r[:], r[:], m[:], ALU.add, [k + "r", k + "m"], [k + "r"])
            V(lambda e: e.tensor_scalar(out=m[:], in0=r[:], scalar1=-math.pi, scalar2=TWO_PI, op0=ALU.is_lt, op1=ALU.mult), [k + "r"], [k + "m"])
            TT("dve", r[:], r[:], m[:], ALU.add, [k + "r", k + "m"], [k + "r"])
            A(lambda e: e.activation(out=sin_o, in_=r[:], func=AF.Sin), [k + "r"], [ok + "s"])
            V(lambda e: e.tensor_scalar(out=u[:], in0=r[:], scalar1=math.pi / 2, scalar2=None, op0=ALU.add), [k + "r"], [k + "u"])
            V(lambda e: e.tensor_scalar(out=m[:], in0=u[:], scalar1=math.pi, scalar2=-TWO_PI, op0=ALU.is_gt, op1=ALU.mult), [k + "u"], [k + "m"])
            TT("dve", u[:], u[:], m[:], ALU.add, [k + "u", k + "m"], [k + "u"])
            A(lambda e: e.activation(out=cos_o, in_=u[:], func=AF.Sin), [k + "u"], [ok + "c"])

        with ExitStack() as ph:
            T1r = sb("T1r", [128, 4, 16, 128], BF16, ph)
            T1i = sb("T1i", [128, 4, 16, 128], BF16, ph)
            T2r = sb("T2r", [128, 16, 16, 32], BF16, ph)
            T2n = sb("T2n", [128, 16, 16, 32], BF16, ph)
            Kbd = sb("Kbd", [128, 4, 16, 128], BF16, ph)
            Fr = sb("Fr", [128, 16, 17], F32, ph)
            Fi = sb("Fi", [128, 16, 17], F32, ph)
            h0r = sb("h0r", [128, 16, SB], F32, ph)
            h0i = sb("h0i", [128, 16, SB], F32, ph)

            with ExitStack() as p2:
                AR = sb("AR", [128, 16], F32, p2)
                AI = sb("AI", [128, 16], F32, p2)
                DT = sb("DT", [128, 16], F32, p2)
                dtar = sb("dtar", [128, 16], F32, p2)
                dtai = sb("dtai", [128, 16], F32, p2)
                ARG = sb("ARG", [128, 16, 17], F32, p2)
                ANG = sb("ANG", [128, 16, 17], F32, p2)
                MAG = sb("MAG", [128, 16, 17], F32, p2)
                COS = sb("COS", [128, 16, 17], F32, p2)
                SIN = sb("SIN", [128, 16, 17], F32, p2)
                cor = sb("cor", [128, 16], F32, p2)
                coi = sb("coi", [128, 16], F32, p2)
                t1 = sb("t1", [128, 16], F32, p2)
                t2 = sb("t2", [128, 16], F32, p2)
                t3 = sb("t3", [128, 16], F32, p2)
                FBr = sb("FBr", [128, 16, 16], F32, p2)
                FBi = sb("FBi", [128, 16, 16], F32, p2)
                f1 = sb("f1", [128, 16, 16], F32, p2)
                BDr = sb("BDr", [128, 16, 32], F32, p2)
                BDi = sb("BDi", [128, 16, 32], F32, p2)
                CDr = sb("CDr", [128, 16, 32], F32, p2)
                CDi = sb("CDi", [128, 16, 32], F32, p2)
                CDrb = sb("CDrb", [128, 16, 32], BF16, p2)
                CDnb = sb("CDnb", [128, 16, 32], BF16, p2)
                cn = sb("cn", [128, 2, 4, 2, 64], F32, p2)
                dsk = sb("dsk", [128, 4], F32, p2)
                h0n = sb("h0n", [SB, 2048], F32, p2)
                tA = sb("tA", [128, 4, 16, 32], F32, p2)
                tB = sb("tB", [128, 4, 16, 32], F32, p2)
                LBr = sb("LBr", [128, 4, 16, 32], BF16, p2)
                LBi = sb("LBi", [128, 4, 16, 32], BF16, p2)
                pT = ps("pT", [128, 2, 16, 128], BF16, p2)
                pK = ps("pK", [128, 16, 32], F32, p2)
                pC = ps("pC", [128, 4, 128], F32, p2)
                pH = ps("pH", [128, 2, 16, SB], F32, p2)

                cx.dma("sp", AR[:, :], a_re.rearrange("(j q) -> q j", q=128), writes=["AR"], allow_slow_non_contiguous=True)
                cx.dma("sp", AI[:, :], a_im.rearrange("(j q) -> q j", q=128), writes=["AI"], allow_slow_non_contiguous=True)
                ldv = log_dt.rearrange("(j a) -> a j", a=2)
                for a in range(2):
                    cx.dma("act", DT[64 * a:64 * a + 64, :], ldv[a:a + 1, :].broadcast_to([64, 16]), writes=["DT%d" % a],
                           allow_slow_non_contiguous=True)
                V(lambda e: e.memset(BDr[:], 0.0), [], ["BDr"])
                V(lambda e: e.memset(BDi[:], 0.0), [], ["BDi"])
                for (src, dst, key) in ((b_re, BDr, "BDr"), (b_im, BDi, "BDi")):
                    sv = src.rearrange("(j a p) h -> a p j h", a=2, p=64)
                    for a in range(2):
                        cx.dma("sp" if a == 0 else "act", dst[64 * a:64 * a + 64, :, 16 * a:16 * a + 16], sv[a], reads=[], writes=[key],
                               allow_slow_non_contiguous=True)
                for ri, src in enumerate((c_re, c_im)):
                    sv = src.rearrange("(jj r) p -> r jj p", r=128)
                    for dup in range(2):
                        cx.dma("sp" if dup == 0 else "act", cn[:, ri, :, dup, :], sv, writes=["cn"])
                cx.dma("sp", dsk[:, :], d_skip.rearrange("(jj r) -> r jj", r=128), writes=["dsk"], allow_slow_non_contiguous=True)

                if STG < 2:
                    cx.enabled = False
                A(lambda e: e.activation(out=DT[:], in_=DT[:], func=AF.Exp), ["DT0", "DT1"], ["DT"])
                TT("dve", dtar[:], DT[:], AR[:], ALU.mult, ["DT", "AR"], ["dtar"])
                TT("dve", dtai[:], DT[:], AI[:], ALU.mult, ["DT", "AI"], ["dtai"])
                for k in range(17):
                    V(lambda e, k=k: e.tensor_scalar(out=ARG[:, :, k], in0=dtar[:], scalar1=float(k), scalar2=None, op0=ALU.mult), ["dtar"], ["ARG"])
                    V(lambda e, k=k: e.tensor_scalar(out=ANG[:, :, k], in0=dtai[:], scalar1=float(k), scalar2=None, op0=ALU.mult), ["dtai"], ["ANG"])
                A(lambda e: e.activation(out=MAG[:], in_=ARG[:], func=AF.Exp), ["ARG"], ["MAG"])
                cis(p2, ANG[:].rearrange("p a b -> p (a b)"), 16 * 17, COS[:].rearrange("p a b -> p (a b)"),
                    SIN[:].rearrange("p a b -> p (a b)"), "c1", "ANG", "CS")
                TT("dve", Fr[:], MAG[:], COS[:], ALU.mult, ["MAG", "CSc"], ["Fr"])
                TT("dve", Fi[:], MAG[:], SIN[:], ALU.mult, ["MAG", "CSs"], ["Fi"])
                V(lambda e: e.tensor_scalar(out=t1[:], in0=Fr[:, :, 1], scalar1=-1.0, scalar2=None, op0=ALU.add), ["Fr"], ["t1"])
                TT("dve", t2[:], AR[:], AR[:], ALU.mult, ["AR"], ["t2"])
                TT("dve", t3[:], AI[:], AI[:], ALU.mult, ["AI"], ["t3"])
                TT("dve", t2[:], t2[:], t3[:], ALU.add, ["t2", "t3"], ["t2"])
                V(lambda e: e.reciprocal(out=t2[:], in_=t2[:]), ["t2"], ["t2"])
                TT("dve", cor[:], t1[:], AR[:], ALU.mult, ["t1", "AR"], ["cor"])
                TT("dve", t3[:], Fi[:, :, 1], AI[:], ALU.mult, ["Fi", "AI"], ["t3"])
                TT("dve", cor[:], cor[:], t3[:], ALU.add, ["cor", "t3"], ["cor"])
                TT("dve", cor[:], cor[:], t2[:], ALU.mult, ["cor", "t2"], ["cor"])
                TT("dve", coi[:], Fi[:, :, 1], AR[:], ALU.mult, ["Fi", "AR"], ["coi"])
                TT("dve", t3[:], t1[:], AI[:], ALU.mult, ["t1", "AI"], ["t3"])
                TT("dve", coi[:], coi[:], t3[:], ALU.subtract, ["coi", "t3"], ["coi"])
                TT("dve", coi[:], coi[:], t2[:], ALU.mult, ["coi", "t2"], ["coi"])
                corb = cor[:, :].unsqueeze(2).broadcast_to([128, 16, 16])
                coib = coi[:, :].unsqueeze(2).broadcast_to([128, 16, 16])
                TT("dve", FBr[:], Fr[:, :, 0:16], corb, ALU.mult, ["Fr", "cor"], ["FBr"])
                TT("dve", f1[:], Fi[:, :, 0:16], coib, ALU.mult, ["Fi", "coi"], ["f1"])
                TT("dve", FBr[:], FBr[:], f1[:], ALU.subtract, ["FBr", "f1"], ["FBr"])
                TT("dve", FBi[:], Fr[:, :, 0:16], coib, ALU.mult, ["Fr", "coi"], ["FBi"])
                TT("dve", f1[:], Fi[:, :, 0:16], corb, ALU.mult, ["Fi", "cor"], ["f1"])
                TT("dve", FBi[:], FBi[:], f1[:], ALU.add, ["FBi", "f1"], ["FBi"])

                if STG < 3:
                    cx.enabled = False
                V(lambda e: e.memset(CDr[:], 0.0), [], ["CDr"])
                V(lambda e: e.memset(CDi[:], 0.0), [], ["CDi"])
                for jj in range(4):
                    for ri, dst, key in ((0, CDr, "CDr"), (1, CDi, "CDi")):
                        P(lambda pe, jj=jj, ri=ri: pe.transpose(out=pC[:, ri, :], in_=cn[:, ri, jj, :, :].rearrange("p d q -> p (d q)"),
                                                                 identity=ident_f[:, :]), ["cn", "ident_f"], ["pC"])
                        pv = pC[:, ri, :].rearrange("p (m a h) -> p m a h", m=4, a=2)
                        for a in range(2):
                            V(lambda e, jj=jj, a=a, dst=dst, pv=pv: e.tensor_copy(
                                out=dst[64 * a:64 * a + 64, 4 * jj:4 * jj + 4, 16 * a:16 * a + 16], in_=pv[64 * a:64 * a + 64, :, a, :]),
                                [], [key, "pC"])
                if STG < 3.2:
                    cx.enabled = False
                V(lambda e: e.tensor_copy(out=CDrb[:], in_=CDr[:]), ["CDr"], ["CDrb"])
                V(lambda e: e.tensor_scalar(out=CDnb[:], in0=CDi[:], scalar1=-1.0, scalar2=None, op0=ALU.mult), ["CDi"], ["CDnb"])

                if STG < 3.5:
                    cx.enabled = False
                for ri, dst, key in ((0, h0r, "h0r"), (1, h0i, "h0i")):
                    cx.dma("sp", h0n[:, :], (s_re if ri == 0 else s_im)[:, :], writes=["h0n"])
                    for j in range(16):
                        P(lambda pe, ri=ri, j=j: pe.transpose(out=pH[:, ri, j, :], in_=h0n[:, 128 * j:128 * j + 128],
                                                               identity=ident_f[0:SB, 0:SB]), ["h0n", "ident_f"], ["pH"])
                    V(lambda e, ri=ri, dst=dst: e.tensor_copy(out=dst[:], in_=pH[:, ri, :, :]), [], [key, "pH"])

                if STG < 4:
                    cx.enabled = False
                V(lambda e: e.memset(Kbd[:], 0.0), [], ["Kbd"])
                for jj in range(4):
                    js = slice(4 * jj, 4 * jj + 4)
                    fbr = FBr[:, js, :].unsqueeze(3).broadcast_to([128, 4, 16, 32])
                    fbi = FBi[:, js, :].unsqueeze(3).broadcast_to([128, 4, 16, 32])
                    bdr = BDr[:, js, :].unsqueeze(2).broadcast_to([128, 4, 16, 32])
                    bdi = BDi[:, js, :].unsqueeze(2).broadcast_to([128, 4, 16, 32])
                    TT("dve", tA[:], fbr, bdr, ALU.mult, ["FBr", "BDr"], ["tA"])
                    TT("pool", tB[:], fbi, bdi, ALU.mult, ["FBi", "BDi"], ["tB"])
                    TT("dve", LBr[:], tA[:], tB[:], ALU.subtract, ["tA", "tB"], ["LBr"])
                    TT("dve", tA[:], fbr, bdi, ALU.mult, ["FBr", "BDi", "LBr"], ["tA"])
                    TT("pool", tB[:], fbi, bdr, ALU.mult, ["FBi", "BDr", "LBr"], ["tB"])
                    TT("dve", LBi[:], tA[:], tB[:], ALU.add, ["tA", "tB"], ["LBi"])
                    for ri, src, dst, skey, dkey in ((0, LBr, T1r, "LBr", "T1r"), (1, LBi, T1i, "LBi", "T1i")):
                        for k in range(16):
                            for m in range(4):
                                P(lambda pe, ri=ri, k=k, m=m, src=src: pe.transpose(
                                    out=pT[32 * m:32 * m + 32, ri, k, :], in_=src[:, m, k, :], identity=ident_b[:, :], tile_position=(0, 32 * m)),
                                    [skey, "ident_b"], ["pT%d" % ri])
                        A(lambda e, ri=ri, dst=dst, jj=jj: e.activation(out=dst[:, jj, :, :], in_=pT[:, ri, :, :], func=AF.Copy),
                          ["pT%d" % ri], [dkey])
                    for k in range(16):
                        for m in range(4):
                            P(lambda pe, k=k, m=m, jj=jj: pe.matmul(pK[32 * m:32 * m + 32, k, :], lhsT=LBr[:, m, k, :],
                                                                    rhs=CDrb[:, 4 * jj + m, :], start=True, stop=False, tile_position=(0, 32 * m)),
                              ["LBr", "CDrb"], ["pK"])
                            P(lambda pe, k=k, m=m, jj=jj: pe.matmul(pK[32 * m:32 * m + 32, k, :], lhsT=LBi[:, m, k, :],
                                                                    rhs=CDnb[:, 4 * jj + m, :], start=False, stop=True, tile_position=(0, 32 * m)),
                              ["LBi", "CDnb"], ["pK"])
                    for m in range(4):
                        V(lambda e, m=m, jj=jj: e.tensor_copy(out=Kbd[32 * m:32 * m + 32, jj, :, 32 * m:32 * m + 32],
                                                              in_=pK[32 * m:32 * m + 32, :, :]), ["pK"], ["Kbd"])
                    V(lambda e, jj=jj: e.scalar_tensor_tensor(out=Kbd[:, jj, 0, :], in0=ident_f[:, :], scalar=dsk[:, jj:jj + 1],
                                                              in1=Kbd[:, jj, 0, :], op0=ALU.mult, op1=ALU.add),
                      ["Kbd", "dsk", "ident_f"], ["Kbd"])
                    fr = Fr[:, js, 1:17].unsqueeze(3).broadcast_to([128, 4, 16, 32])
                    fi = Fi[:, js, 1:17].unsqueeze(3).broadcast_to([128, 4, 16, 32])
                    cdr = CDr[:, js, :].unsqueeze(2).broadcast_to([128, 4, 16, 32])
                    cdi = CDi[:, js, :].unsqueeze(2).broadcast_to([128, 4, 16, 32])
                    TT("dve", tA[:], fr, cdr, ALU.mult, ["Fr", "CDr", "LBi"], ["tA"])
                    TT("pool", tB[:], fi, cdi, ALU.mult, ["Fi", "CDi", "LBi"], ["tB"])
                    TT("dve", T2r[:, js, :, :], tA[:], tB[:], ALU.subtract, ["tA", "tB"], ["T2r"])
                    TT("dve", tA[:], fr, cdi, ALU.mult, ["Fr", "CDi", "T2r"], ["tA"])
                    TT("pool", tB[:], fi, cdr, ALU.mult, ["Fi", "CDr", "T2r"], ["tB"])
                    V(lambda e, js=js: e.scalar_tensor_tensor(out=T2n[:, js, :, :], in0=tA[:], scalar=-1.0, in1=tB[:],
                                                              op0=ALU.mult, op1=ALU.subtract), ["tA", "tB"], ["T2n"])
                cx.barrier()

            if STG < 5:
                cx.enabled = False
            with ExitStack() as p3:
                Zr = sb("Zr", [128, 16, 128], F32, p3)
                Zi = sb("Zi", [128, 16, 128], F32, p3)
                Yr = sb("Yr", [128, 16, 128], F32, p3)
                Yi = sb("Yi", [128, 16, 128], F32, p3)
                tq = sb("tq", [128, 16, 128], F32, p3)
                tw = sb("tw", [128, 16, 128], F32, p3)
                Mr = sb("Mr", [128, 16], F32, p3)
                Mi = sb("Mi", [128, 16], F32, p3)
                m1 = sb("m1", [128, 16], F32, p3)
                m2 = sb("m2", [128, 16], F32, p3)
                Hxr = sb("Hxr", [128, 16, 128], BF16, p3)
                Hxi = sb("Hxi", [128, 16, 128], BF16, p3)
                h0rb = sb("h0rb", [128, 16, SB], BF16, p3)
                h0ib = sb("h0ib", [128, 16, SB], BF16, p3)
                hsr = sb("hsr", [128, 16, SB], F32, p3)
                hsi = sb("hsi", [128, 16, SB], F32, p3)
                hsn = sb("hsn", [SB, 2048], F32, p3)
                pZ = ps("pZ", [128, 2, 8, 128], F32, p3)
                pY = ps("pY", [128, 16, 128], F32, p3)

                for half in range(2):
                    for j8 in range(8):
                        j = 8 * half + j8
                        jj, m = j // 4, j % 4
                        for ri, Tt, key in ((0, T1r, "T1r"), (1, T1i, "T1i")):
                            for i in range(16):
                                P(lambda pe, ri=ri, j8=j8, jj=jj, m=m, i=i, Tt=Tt: pe.matmul(
                                    pZ[:, ri, j8, :], lhsT=Tt[32 * m:32 * m + 32, jj, 15 - i, :],
                                    rhs=uT[32 * m:32 * m + 32, jj, i:2048:16], start=(i == 0), stop=(i == 15), tile_position=(32 * m, 0)),
                                    [key, "uT"], ["pZ%d" % ri])
                    hs = slice(8 * half, 8 * half + 8)
                    V(lambda e, hs=hs: e.tensor_copy(out=Zr[:, hs, :], in_=pZ[:, 0, :, :]), ["pZ0"], ["Zr"])
                    A(lambda e, hs=hs: e.activation(out=Zi[:, hs, :], in_=pZ[:, 1, :, :], func=AF.Copy), ["pZ1"], ["Zi"])

                V(lambda e: e.tensor_copy(out=Mr[:], in_=Fr[:, :, 16]), ["Fr"], ["Mr"])
                V(lambda e: e.tensor_copy(out=Mi[:], in_=Fi[:, :, 16]), ["Fi"], ["Mi"])
                src_r, src_i, dst_r, dst_i = Zr, Zi, Yr, Yi
                skr, ski, dkr, dki = "Zr", "Zi", "Yr", "Yi"
                for lvl in range(7):
                    s = 1 << lvl
                    n = 128 - s
                    mrb = Mr[:, :].unsqueeze(2).broadcast_to([128, 16, n])
                    mib = Mi[:, :].unsqueeze(2).broadcast_to([128, 16, n])
                    TT("dve", dst_r[:, :, s:], mrb, src_r[:, :, 0:n], ALU.mult, ["Mr", skr], [dkr])
                    TT("dve", tq[:, :, s:], mib, src_i[:, :, 0:n], ALU.mult, ["Mi", ski], ["tq"])
                    TT("dve", dst_r[:, :, s:], dst_r[:, :, s:], tq[:, :, s:], ALU.subtract, [dkr, "tq"], [dkr])
                    TT("dve", dst_r[:, :, s:], dst_r[:, :, s:], src_r[:, :, s:], ALU.add, [dkr, skr], [dkr])
                    V(lambda e, s=s, dst_r=dst_r, src_r=src_r: e.tensor_copy(out=dst_r[:, :, 0:s], in_=src_r[:, :, 0:s]), [skr], [dkr])
                    TT("pool", dst_i[:, :, s:], mrb, src_i[:, :, 0:n], ALU.mult, ["Mr", ski], [dki])
                    TT("pool", tw[:, :, s:], mib, src_r[:, :, 0:n], ALU.mult, ["Mi", skr], ["tw"])
                    TT("pool", dst_i[:, :, s:], dst_i[:, :, s:], tw[:, :, s:], ALU.add, [dki, "tw"], [dki])
                    TT("pool", dst_i[:, :, s:], dst_i[:, :, s:], src_i[:, :, s:], ALU.add, [dki, ski], [dki])
                    G(lambda e, s=s, dst_i=dst_i, src_i=src_i: e.tensor_copy(out=dst_i[:, :, 0:s], in_=src_i[:, :, 0:s]), [ski], [dki])
                    if lvl < 6:
                        TT("dve", m1[:], Mr[:], Mr[:], ALU.mult, ["Mr"], ["m1"])
                        TT("dve", m2[:], Mi[:], Mi[:], ALU.mult, ["Mi"], ["m2"])
                        TT("dve", m1[:], m1[:], m2[:], ALU.subtract, ["m1", "m2"], ["m1"])
                        TT("dve", m2[:], Mr[:], Mi[:], ALU.mult, ["Mr", "Mi", dkr, dki, "tq", "tw"], ["m2"])
                        V(lambda e: e.tensor_scalar(out=Mi[:], in0=m2[:], scalar1=2.0, scalar2=None, op0=ALU.mult), ["m2", dkr, dki, "tq", "tw"], ["Mi"])
                        V(lambda e: e.tensor_copy(out=Mr[:], in_=m1[:]), ["m1", dkr, dki, "tq", "tw"], ["Mr"])
                    src_r, src_i, dst_r, dst_i = dst_r, dst_i, src_r, src_i
                    skr, ski, dkr, dki = dkr, dki, skr, ski
                Hr_, Hi_, hkr, hki = src_r, src_i, skr, ski
                cx.dma("sp", hr_p.rearrange("(j q) -> q j", q=128), Hr_[:, :, 127], reads=[hkr], writes=["o_hrp"], allow_slow_non_contiguous=True)
                cx.dma("sp", hi_p.rearrange("(j q) -> q j", q=128), Hi_[:, :, 127], reads=[hki], writes=["o_hip"], allow_slow_non_contiguous=True)
                V(lambda e: e.memset(Hxr[:, :, 0:1], 0.0), [], ["Hxr"])
                V(lambda e: e.memset(Hxi[:, :, 0:1], 0.0), [], ["Hxi"])
                V(lambda e: e.tensor_copy(out=Hxr[:, :, 1:128], in_=Hr_[:, :, 0:127]), [hkr, "Hxr"], ["Hxr"])
                V(lambda e: e.tensor_copy(out=Hxi[:, :, 1:128], in_=Hi_[:, :, 0:127]), [hki, "Hxi"], ["Hxi"])
                V(lambda e: e.tensor_copy(out=h0rb[:], in_=h0r[:]), ["h0r"], ["h0rb"])
                V(lambda e: e.tensor_copy(out=h0ib[:], in_=h0i[:]), ["h0i"], ["h0ib"])

                for jj in range(4):
                    uv = uT[:, jj, 0:2048].rearrange("p (c i) -> p i c", i=16)
                    for bk in range(4):
                        for tau in range(0, 4 * bk + 4):
                            i0 = max(4 * bk, tau)
                            i1 = 4 * bk + 4
                            P(lambda pe, jj=jj, tau=tau, i0=i0, i1=i1, uv=uv: pe.matmul(
                                pY[:, i0:i1, :], lhsT=Kbd[:, jj, tau, :], rhs=uv[:, i0 - tau:i1 - tau, :],
                                start=(tau == 0), stop=False, skip_group_check=True),
                                ["Kbd", "uT"], ["pY"])
                    for i in range(16):
                        for m in range(4):
                            j = 4 * jj + m
                            P(lambda pe, i=i, m=m, j=j: pe.matmul(pY[32 * m:32 * m + 32, i, :], lhsT=T2r[:, j, i, :], rhs=Hxr[:, j, :],
                                                                  start=False, stop=False, skip_group_check=True, tile_position=(0, 32 * m)), ["T2r", "Hxr"], ["pY"])
                            P(lambda pe, i=i, m=m, j=j: pe.matmul(pY[32 * m:32 * m + 32, i, :], lhsT=T2n[:, j, i, :], rhs=Hxi[:, j, :],
                                                                  start=False, stop=True, skip_group_check=True, tile_position=(0, 32 * m)), ["T2n", "Hxi"], ["pY"])
                    yv = yssm[:, jj, 0:2048].rearrange("p (c i) -> p i c", i=16)
                    V(lambda e, yv=yv: e.tensor_copy(out=yv[:, 0:8, :], in_=pY[:, 0:8, :]), ["pY"], ["yssm"])
                    A(lambda e, yv=yv: e.activation(out=yv[:, 8:16, :], in_=pY[:, 8:16, :], func=AF.Copy), ["pY"], ["yssm"])

                for jj in range(4):
                    uv = uT[:, jj, 2048:T].rearrange("p (b t) -> p t b", t=4)
                    for tau in range(4):
                        P(lambda pe, jj=jj, tau=tau, uv=uv: pe.matmul(
                            pY[:, 0, 0:64].rearrange("p (t b) -> p t b", t=4)[:, tau:4, :], lhsT=Kbd[:, jj, tau, :], rhs=uv[:, 0:4 - tau, :],
                            start=(tau == 0), stop=False, skip_group_check=True), ["Kbd", "uT"], ["pY"])
                    for t in range(4):
                        for m in range(4):
                            j = 4 * jj + m
                            P(lambda pe, t=t, m=m, j=j: pe.matmul(pY[32 * m:32 * m + 32, 0, 16 * t:16 * t + 16], lhsT=T2r[:, j, t, :], rhs=h0rb[:, j, :],
                                                                  start=False, stop=False, skip_group_check=True, tile_position=(0, 32 * m)), ["T2r", "h0rb"], ["pY"])
                            P(lambda pe, t=t, m=m, j=j: pe.matmul(pY[32 * m:32 * m + 32, 0, 16 * t:16 * t + 16], lhsT=T2n[:, j, t, :], rhs=h0ib[:, j, :],
                                                                  start=False, stop=True, skip_group_check=True, tile_position=(0, 32 * m)), ["T2n", "h0ib"], ["pY"])
                    V(lambda e, jj=jj: e.tensor_copy(out=yssm[:, jj, 2048:T].rearrange("p (b t) -> p t b", t=4),
                                                     in_=pY[:, 0, 0:64].rearrange("p (t b) -> p t b", t=4)), ["pY"], ["yssm"])

                for half in range(2):
                    for j8 in range(8):
                        j = 8 * half + j8
                        jj, m = j // 4, j % 4
                        for ri, Tt, key in ((0, T1r, "T1r"), (1, T1i, "T1i")):
                            for t in range(4):
                                P(lambda pe, ri=ri, j8=j8, jj=jj, m=m, t=t, Tt=Tt: pe.matmul(
                                    pZ[:, ri, j8, 0:SB], lhsT=Tt[32 * m:32 * m + 32, jj, 3 - t, :],
                                    rhs=uT[32 * m:32 * m + 32, jj, 2048 + t:T:4], start=(t == 0), stop=(t == 3), tile_position=(32 * m, 0)),
                                    [key, "uT"], ["pZ%d" % ri])
                    hs = slice(8 * half, 8 * half + 8)
                    f4r = Fr[:, hs, 4:5].broadcast_to([128, 8, SB])
                    f4i = Fi[:, hs, 4:5].broadcast_to([128, 8, SB])
                    TT("dve", tq[:, hs, 0:SB], f4r, h0r[:, hs, :], ALU.mult, ["Fr", "h0r"], ["tq"])
                    TT("dve", hsr[:, hs, :], pZ[:, 0, :, 0:SB], tq[:, hs, 0:SB], ALU.add, ["pZ0", "tq"], ["hsr"])
                    TT("dve", tq[:, hs, 0:SB], f4i, h0i[:, hs, :], ALU.mult, ["Fi", "h0i", "hsr"], ["tq"])
                    TT("dve", hsr[:, hs, :], hsr[:, hs, :], tq[:, hs, 0:SB], ALU.subtract, ["hsr", "tq"], ["hsr"])
                    TT("dve", tw[:, hs, 0:SB], f4r, h0i[:, hs, :], ALU.mult, ["Fr", "h0i"], ["tw"])
                    TT("dve", hsi[:, hs, :], pZ[:, 1, :, 0:SB], tw[:, hs, 0:SB], ALU.add, ["pZ1", "tw"], ["hsi"])
                    TT("dve", tw[:, hs, 0:SB], f4i, h0r[:, hs, :], ALU.mult, ["Fi", "h0r", "hsi"], ["tw"])
                    TT("dve", hsi[:, hs, :], hsi[:, hs, :], tw[:, hs, 0:SB], ALU.add, ["hsi", "tw"], ["hsi"])
                for ri, src, key, dstd in ((0, hsr, "hsr", hr_s), (1, hsi, "hsi", hi_s)):
                    for j in range(16):
                        P(lambda pe, src=src, j=j: pe.transpose(out=pY[0:SB, j, :], in_=src[:, j, :], identity=ident_f[:, :]),
                          [key, "ident_f"], ["pY"])
                    V(lambda e, ri=ri: e.tensor_copy(out=hsn[:, :], in_=pY[0:SB, :, :].rearrange("p a b -> p (a b)")), ["pY"], ["hsn"])
                    cx.dma("sp", dstd[:, :], hsn[:, :], reads=["hsn"], writes=["o_hs%d" % ri])
                cx.barrier()

        pxu.close()

        with ExitStack() as ph:
            wg_st = sb("wg_st", [128, 4, 512], F32, ph)
            wg_b = sb("wg_b", [128, 4, 512], BF16, ph)
            bg = sb("bg", [128, 4], F32, ph)
            gs = sb("gs", [128, 4], F32, ph)
            ga = sb("ga", [128, 4, 512], F32, ph)
            gb = sb("gb", [128, 4, 512], BF16, ph)
            t_a = sb("t_a", [128, 4, 512], F32, ph)
            t_b = sb("t_b", [128, 4, 512], F32, ph)
            sq = sb("sq", [128, 4, 512], BF16, ph)
            rstd = sb("rstd", [128, 512], F32, ph)
            pz = [ps("pz%d" % i, [128, 512], F32, ph) for i in range(4)]
            pss = ps("pss", [128, 512], F32, ph)
            cx.dma("sp", wg_st[:, :, :], w_glu.rearrange("(kc p) c -> p kc c", p=128), writes=["wg_st"])
            cx.dma("act", bg[:, :], b_glu.rearrange("(c p) -> p c", p=128), writes=["bg"], allow_slow_non_contiguous=True)
            cx.dma("act", gs[:, :], g_ssm.rearrange("(c p) -> p c", p=128), writes=["gs"], allow_slow_non_contiguous=True)
            V(lambda e: e.tensor_copy(out=wg_b[:], in_=wg_st[:]), ["wg_st"], ["wg_b"])
            for nt in range(5):
                t0 = nt * 512
                n = 512 if nt < 4 else ST
                yv = yssm[:, :, t0:t0 + n]
                A(lambda e, yv=yv, n=n: e.activation(out=t_a[:, :, 0:n], in_=yv, func=AF.Square), ["yssm"], ["t_a"])
                V(lambda e, n=n: e.tensor_scalar(out=t_a[:, :, 0:n], in0=t_a[:, :, 0:n], scalar1=0.044715, scalar2=1.0, op0=ALU.mult, op1=ALU.add), ["t_a"], ["t_a"])
                TT("dve", t_a[:, :, 0:n], t_a[:, :, 0:n], yv, ALU.mult, ["t_a", "yssm"], ["t_a"])
                A(lambda e, n=n: e.activation(out=t_b[:, :, 0:n], in_=t_a[:, :, 0:n], func=AF.Sigmoid, scale=1.5957691216057308), ["t_a"], ["t_b"])
                TT("dve", ga[:, :, 0:n], t_b[:, :, 0:n], yv, ALU.mult, ["t_b", "yssm"], ["ga"])
                G(lambda e, n=n: e.tensor_copy(out=gb[:, :, 0:n], in_=ga[:, :, 0:n]), ["ga"], ["gb"])
                for oc in range(4):
                    for kc in range(4):
                        P(lambda pe, oc=oc, kc=kc, n=n: pe.matmul(pz[oc][:, 0:n], lhsT=wg_b[:, kc, oc * 128:(oc + 1) * 128], rhs=gb[:, kc, 0:n],
                                                                  start=(kc == 0), stop=(kc == 3)), ["wg_b", "gb"], ["pz%d" % oc])
                    A(lambda e, oc=oc, n=n: e.activation(out=t_b[:, oc, 0:n], in_=pz[oc][:, 0:n], func=AF.Sigmoid, bias=bg[:, oc:oc + 1], scale=1.0),
                      ["bg"], ["t_b%d" % oc, "pz%d" % oc])
                    TT("dve", t_a[:, oc, 0:n], ga[:, oc, 0:n], t_b[:, oc, 0:n], ALU.mult, ["ga", "t_b%d" % oc], ["t_a%d" % oc])
                    A(lambda e, oc=oc, n=n: e.activation(out=sq[:, oc, 0:n], in_=t_a[:, oc, 0:n], func=AF.Square), ["t_a%d" % oc], ["sq%d" % oc])
                for oc in range(4):
                    P(lambda pe, oc=oc, n=n: pe.matmul(pss[:, 0:n], lhsT=ones_b[:, :], rhs=sq[:, oc, 0:n], start=(oc == 0), stop=(oc == 3)),
                      ["ones_b", "sq%d" % oc], ["pss"])
                A(lambda e, n=n: e.activation(out=rstd[:, 0:n], in_=pss[:, 0:n], func=AF.Sqrt, bias=eps_t[:, 0:1], scale=1.0 / 512.0), ["eps_t"], ["rstd", "pss"])
                V(lambda e, n=n: e.reciprocal(out=rstd[:, 0:n], in_=rstd[:, 0:n]), ["rstd"], ["rstd"])
                for oc in range(4):
                    V(lambda e, oc=oc, n=n, t0=t0: e.scalar_tensor_tensor(out=yssm[:, oc, t0:t0 + n], in0=t_a[:, oc, 0:n], scalar=gs[:, oc:oc + 1],
                                                                          in1=rstd[:, 0:n], op0=ALU.mult, op1=ALU.mult),
                      ["t_a%d" % oc, "gs", "rstd"], ["yssm"])
                V(lambda e: e.memset(rstd[:, 0:1], 0.0), ["t_a0", "t_a1", "t_a2", "t_a3", "t_b0", "t_b1", "t_b2", "t_b3", "sq0", "sq1", "sq2", "sq3", "yssm"],
                  ["t_a", "t_b", "rstd"])
            cx.barrier()

        attT = sb("attT", [128, 4, T], BF16)
        pqk = ExitStack()
        qT = sb("qT", [128, 4, T], BF16, pqk)
        kT = sb("kT", [128, 4, T], BF16, pqk)
        px = ExitStack()
        xT = sb("xT2", [128, 8, T], BF16, px)
        phase1(xT)
        phase2(xT, [0, 1, 2], qT, kT, None)
        with ExitStack() as ph:
            memT = sb("memT", [128, 8, NMEM], BF16, ph)
            phase1(memT, [(memp[0:128, :], 128), (memp[128:256, :], 128)])
            wst = [sb("mwst%d" % i, [128, 8, 512], F32, ph) for i in range(2)]
            wbf = [sb("mwbf%d" % i, [128, 8, 512], BF16, ph) for i in range(2)]
            ost = [sb("most%d" % i, [128, 512], F32, ph) for i in range(2)]
            pp = [ps("mpp%d" % i, [128, 512], F32, ph) for i in range(2)]
            n = 0
            for wi, (wd, od) in enumerate(((w_mk, mk_p), (w_mv, mv_p))):
                w_v = wd.rearrange("(kc p) c -> p kc c", p=128)
                for cb in range(2):
                    i = n % 2
                    n += 1
                    cx.dma("sp" if i == 0 else "act", wst[i][:, :, :], w_v[:, :, cb * 512:(cb + 1) * 512], writes=["mwst%d" % i])
                    cx.op("pool", lambda e, i=i: e.tensor_copy(out=wbf[i][:, 0:4, :], in_=wst[i][:, 0:4, :]), reads=["mwst%d" % i], writes=["mwbf%d_a" % i])
                    cx.op("dve", lambda e, i=i: e.tensor_copy(out=wbf[i][:, 4:8, :], in_=wst[i][:, 4:8, :]), reads=["mwst%d" % i], writes=["mwbf%d_b" % i])
                    for tt in range(2):
                        p = tt
                        for kc in range(8):
                            cx.op("pe", lambda pe, p=p, i=i, kc=kc, tt=tt: pe.matmul(
                                pp[p][:, :], lhsT=memT[:, kc, tt * 128:(tt + 1) * 128], rhs=wbf[i][:, kc, :], start=(kc == 0), stop=(kc == 7)),
                                reads=["mwbf%d_a" % i, "mwbf%d_b" % i, "xT"], writes=["mpp%d" % p])
                        if tt == 0:
                            cx.op("act", lambda e, p=p: e.activation(out=ost[p][:, :], in_=pp[p][:, :], func=AF.Copy), reads=[], writes=["most%d" % p, "mpp%d" % p])
                        else:
                            cx.op("dve", lambda e, p=p: e.tensor_copy(out=ost[p][:, :], in_=pp[p][:, :]), reads=[], writes=["most%d" % p, "mpp%d" % p])
                        cx.dma("sp", od[tt * 128:(tt + 1) * 128, cb * 512:(cb + 1) * 512], ost[p][:, :], reads=["most%d" % p], writes=["o_mem"])
            cx.barrier()
        px.close()
        with ExitStack() as ph:
            Vall = sb("Vall", [128, 48, 8, 128], BF16, ph)
            vst = [sb("vst%d" % i, [128, 512], F32, ph) for i in range(3)]
            Mpc = sb("Mpc", [128, 4, 128], BF16, ph)
            M16 = sb("M16", [128, 4, 32], BF16, ph)
            mi = sb("mi", [128, 128], I32, ph)
            mf = sb("mf", [128, 128], F32, ph)
            Pt = [sb("Pt%d" % i, [128, 512], BF16, ph) for i in range(4)]
            oatt = sb("oatt", [128, 4, 512], F32, ph)
            osq = sb("osq", [128, 4, 512], BF16, ph)
            rec = sb("rec", [128, 512], F32, ph)
            rstd = sb("a_rstd", [128, 512], F32, ph)
            gatt = sb("gatt", [128, 4], F32, ph)
            Sb = [ps("Sb%d" % i, [128, 512], F32, ph) for i in range(4)]
            acc = [ps("acc%d" % i, [128, 512], F32, ph) for i in range(2)]
            pss = ps("a_pss", [128, 512], F32, ph)

            cx.dma("act", gatt[:, :], g_att.rearrange("(c p) -> p c", p=128), writes=["gatt"], allow_slow_non_contiguous=True)
            G(lambda g: g.iota(mi[:], pattern=[[1, 128]], base=0, channel_multiplier=-1), [], ["mi"])
            V(lambda e: e.tensor_copy(out=mf[:], in_=mi[:]), ["mi"], ["mf"])
            for sl in range(4):
                V(lambda e, sl=sl: e.tensor_scalar(out=Mpc[:, sl, :], in0=mf[:], scalar1=0.0, scalar2=None,
                                                   op0=(ALU.is_le if sl % 2 == 0 else ALU.is_ge)), ["mf"], ["Mpc"])
            for n in range(4):
                V(lambda e, n=n: e.tensor_scalar(out=M16[:, n, :], in0=mf[:, 32 * n:32 * n + 32], scalar1=0.0, scalar2=None, op0=ALU.is_ge),
                  ["mf"], ["M16"])
            V(lambda e: e.memset(Vall[:, 0:24, :, :], 1.0), [], ["Vall"])
            G(lambda e: e.memset(Vall[:, 24:48, :, :], 1.0), [], ["Vall2"])
            vorder = [0, 1, 2, 3, 16, 17, 18, 19] + list(range(32, 48)) + [4, 5, 6, 7, 20, 21, 22, 23, 8, 9, 10, 11, 24, 25, 26, 27, 12, 13, 14, 15, 28, 29, 30, 31]
            for vcnt, tid in enumerate(vorder):
                if tid < 16:
                    src = wv_p[128 * tid:128 * tid + 128, :]
                elif tid < 32:
                    n_, r_ = (tid - 16) // 4, (tid - 16) % 4
                    src = wv_p[512 * n_ + r_:512 * n_ + 512:4, :]
                else:
                    src = wv_p[tid - 32:2048:16, :]
                i = vcnt % 3
                cx.dma("sp", vst[i][:, :], src, writes=["vst%d" % i])
                vk = "Vall" if tid < 24 else "Vall2"
                sv = vst[i][:, :].rearrange("p (a e d) -> p a e d", a=4, e=2)
                dv = Vall[:, tid, :, :].rearrange("p (a e) c -> p a e c", e=2)
                cx.op("pool" if tid % 2 == 0 else "dve", lambda e, sv=sv, dv=dv: e.tensor_copy(out=dv[:, :, 0, 0:64], in_=sv[:, :, 0, :]),
                      reads=["vst%d" % i, vk], writes=["V%d" % tid])
                cx.op("dve" if tid % 2 == 0 else "act", (lambda e, sv=sv, dv=dv: e.tensor_copy(out=dv[:, :, 1, 64:128], in_=sv[:, :, 1, :])) if tid % 2 == 0 else
                      (lambda e, sv=sv, dv=dv: e.activation(out=dv[:, :, 1, 64:128], in_=sv[:, :, 1, :], func=AF.Copy)),
                      reads=["vst%d" % i, vk, "V%d" % tid], writes=["V%d" % tid])
            VK = ["Vall", "Vall2"]

            banks = []
            aidx = [0]
            for n in range(4):
                for hp in range(4):
                    for e_ in range(2):
                        h = 2 * hp + e_
                        pb = 64 * e_
                        ai = aidx[0] % 2
                        aidx[0] += 1
                        hb = []
                        for half in range(2):
                            slots, pvs = [], []
                            for b2 in range(2):
                                qb = 4 * n + 2 * half + b2
                                qap = qT[pb:pb + 64, hp, 128 * qb:128 * qb + 128]
                                for kind in range(2):
                                    kt = qb - 1 + kind
                                    if kt < 0:
                                        continue
                                    sl = 2 * b2 + kind
                                    slots.append((128 * sl, 128, kT[pb:pb + 64, hp, 128 * kt:128 * kt + 128], qap))
                                    pvs.append((slice((qb - 4 * n) * 128, (qb - 4 * n) * 128 + 128), kt, slice(128 * sl, 128 * sl + 128)))
                            hb.append(dict(slots=slots, pvs=pvs, Kn=128, mask=0))
                        for half in range(2):
                            slots, pvs = [], []
                            for b2 in range(2):
                                r4 = 2 * half + b2
                                qap = qT[pb:pb + 64, hp, 512 * n + r4:512 * n + 512:4]
                                for kind in range(2):
                                    kn = n - 1 + kind
                                    if kn < 0:
                                        continue
                                    sl = 2 * b2 + kind
                                    slots.append((128 * sl, 128, kT[pb:pb + 64, hp, 512 * kn + r4:512 * kn + 512:4], qap))
                                    pvs.append((slice(r4, 512, 4), 16 + 4 * kn + r4, slice(128 * sl, 128 * sl + 128)))
                            hb.append(dict(slots=slots, pvs=pvs, Kn=128, mask=0))
                        Kn = 32 * (n + 1)
                        slots, pvs = [], []
                        for r in range(16):
                            slots.append((32 * r, 32, kT[pb:pb + 64, hp, r:16 * Kn:16], qT[pb:pb + 64, hp, 512 * n + r:512 * n + 512:16]))
                            pvs.append((slice(r, 512, 16), 32 + r, slice(32 * r, 32 * r + 32)))
                        hb.append(dict(slots=slots, pvs=pvs, Kn=Kn, mask=1))
                        for bi_, bk in enumerate(hb):
                            bk.update(n=n, hp=hp, e_=e_, h=h, ai=ai, first=(bi_ == 0), last=(bi_ == len(hb) - 1))
                            banks.append(bk)

            def emit_scores(k):
                bk = banks[k]
                si = k % 4
                Kn = bk["Kn"]
                for (c0, ncol, kap, qap) in bk["slots"]:
                    P(lambda pe, si=si, c0=c0, ncol=ncol, kap=kap, qap=qap, Kn=Kn: pe.matmul(Sb[si][0:Kn, c0:c0 + ncol], lhsT=kap, rhs=qap, start=True, stop=True),
                      ["qT", "kT"], ["Sb%d" % si])
                A(lambda e: e.activation(out=Pt[si][0:Kn, :], in_=Sb[si][0:Kn, :], func=AF.Exp, scale=0.125), [], ["Pt%d" % si, "Sb%d" % si])
                if bk["mask"] == 0:
                    pv_, m_ = Pt[si][0:Kn, :], Mpc[:, :, :].rearrange("p a b -> p (a b)")
                else:
                    pv_ = Pt[si][0:Kn, :].rearrange("p (a b) -> p a b", a=16)
                    m_ = M16[0:Kn, bk["n"], :].unsqueeze(1).broadcast_to([Kn, 16, 32])
                V(lambda e: e.tensor_tensor(out=pv_, in0=pv_, in1=m_, op=ALU.mult), ["Mpc", "M16"], ["Pt%d" % si])

            def emit_pv(k):
                bk = banks[k]
                si = k % 4
                Kn = bk["Kn"]
                A_ = acc[bk["ai"]]
                ak = "acc%d" % bk["ai"]
                h, hp, e_, n = bk["h"], bk["hp"], bk["e_"], bk["n"]
                for j_, (cols, vid, pcols) in enumerate(bk["pvs"]):
                    st = bk["first"] and j_ == 0
                    P(lambda pe, cols=cols, vid=vid, pcols=pcols, st=st: pe.matmul(A_[:, cols], lhsT=Vall[0:Kn, vid, h, :], rhs=Pt[si][0:Kn, pcols],
                                                                                 start=st, stop=False, skip_group_check=True), ["V%d" % vid, "Pt%d" % si], [ak])
                if bk["last"]:
                    vo, do = (0, 64) if e_ == 0 else (64, 0)
                    V(lambda e: e.reciprocal(out=rec[do:do + 64, :], in_=A_[do:do + 64, :]), [], ["rec", ak])
                    V(lambda e: e.tensor_tensor(out=oatt[vo:vo + 64, hp, :], in0=A_[vo:vo + 64, :], in1=rec[do:do + 64, :], op=ALU.mult),
                      ["rec"], ["oatt%d" % hp, ak])
                    if hp == 3 and e_ == 1:
                        for hp2 in range(4):
                            A(lambda e, hp2=hp2: e.activation(out=osq[:, hp2, :], in_=oatt[:, hp2, :], func=AF.Square), ["oatt%d" % hp2], ["osq%d" % hp2])
                        for hp2 in range(4):
                            P(lambda pe, hp2=hp2: pe.matmul(pss[:, :], lhsT=ones_b[:, :], rhs=osq[:, hp2, :], start=(hp2 == 0), stop=(hp2 == 3)),
                              ["ones_b", "osq%d" % hp2], ["a_pss"])
                        A(lambda e: e.activation(out=rstd[:, :], in_=pss[:, :], func=AF.Sqrt, bias=eps_t[:, 0:1], scale=1.0 / 512.0), ["eps_t"], ["a_rstd", "a_pss"])
                        V(lambda e: e.reciprocal(out=rstd[:, :], in_=rstd[:, :]), ["a_rstd"], ["a_rstd"])
                        for hp2 in range(4):
                            V(lambda e, hp2=hp2: e.scalar_tensor_tensor(out=attT[:, hp2, 512 * n:512 * n + 512], in0=oatt[:, hp2, :], scalar=gatt[:, hp2:hp2 + 1],
                                                                        in1=rstd[:, :], op0=ALU.mult, op1=ALU.mult), ["oatt%d" % hp2, "gatt", "a_rstd"], ["attT"])

            NBK = len(banks)
            LOOK = 2
            for k in range(NBK + LOOK):
                if k < NBK:
                    emit_scores(k)
                if k >= LOOK:
                    emit_pv(k - LOOK)
            cx.barrier()

        with ExitStack() as ph:
            kst = [sb("kst%d" % i, [128, 512], F32, ph) for i in range(3)]
            kbf = [sb("kbf%d" % i, [128, 512], BF16, ph) for i in range(3)]
            vst = [sb("svst%d" % i, [128, 512], F32, ph) for i in range(3)]
            kTs = [sb("kTs%d" % i, [128, 4, 8, 128], BF16, ph) for i in range(2)]
            Vs = [sb("Vs%d" % i, [128, 8, 8, 64], BF16, ph) for i in range(2)]
            vnst = sb("vnst", [4, SB, 512], F32, ph)
            Vn = sb("Vn", [4, SB, 8, 64], BF16, ph)
            mi2 = sb("mi2", [128, 4], I32, ph)
            ma2 = sb("ma2", [128, 4], I32, ph)
            mf2 = sb("mf2", [128, 4], F32, ph)
            mg2 = sb("mg2", [128, 4], F32, ph)
            mh2 = sb("mh2", [128, 4], F32, ph)
            Msf = sb("Msf", [128, 9, 4], F32, ph)
            Msb = sb("Msb", [128, 2, 9, 4], BF16, ph)
            Ps = [sb("Ps%d" % i, [128, 2, 9, 4], BF16, ph) for i in range(2)]
            oas = sb("oas", [128, 4, ST], F32, ph)
            osq = sb("s_osq", [128, 4, ST], BF16, ph)
            rec = sb("s_rec", [128, 4, 4], F32, ph)
            rstd = sb("s_rstd", [128, ST], F32, ph)
            gatt = sb("s_gatt", [128, 4], F32, ph)
            pTs = [ps("pTs%d" % i, [128, 8, 128], BF16, ph) for i in range(2)]
            Sps = [ps("Sps%d" % i, [128, 512], F32, ph) for i in range(2)]
            accs = [ps("accs%d" % i, [128, 512], F32, ph) for i in range(2)]
            pss = ps("s_pss", [128, 512], F32, ph)

            cx.dma("act", gatt[:, :], g_att.rearrange("(c p) -> p c", p=128), writes=["gatt"], allow_slow_non_contiguous=True)
            cx.dma("sp", vnst[:, :, :], wv_s.rearrange("(b t) c -> t b c", t=4), writes=["vnst"])
            V(lambda e: e.tensor_copy(out=Vn[:, :, :, :], in_=vnst[:, :, :].rearrange("p b (h d) -> p b h d", h=8)), ["vnst"], ["Vn"])
            G(lambda g: g.iota(mi2[:], pattern=[[-1, 4]], base=4, channel_multiplier=1), [], ["mi2"])
            V(lambda e: e.tensor_scalar(out=ma2[:], in0=mi2[:], scalar1=3, scalar2=None, op0=ALU.bitwise_and), ["mi2"], ["ma2"])
            V(lambda e: e.tensor_copy(out=mf2[:], in_=ma2[:]), ["ma2"], ["mf2"])
            V(lambda e: e.tensor_scalar(out=mf2[:], in0=mf2[:], scalar1=0.0, scalar2=None, op0=ALU.is_equal), ["mf2"], ["mf2"])
            V(lambda e: e.tensor_copy(out=mg2[:], in_=mi2[:]), ["mi2"], ["mg2"])
            for i in range(3):
                V(lambda e, i=i: e.tensor_copy(out=Msf[:, i, :], in_=mf2[:]), ["mf2"], ["Msf"])
            V(lambda e: e.tensor_scalar(out=mh2[:], in0=mg2[:], scalar1=4.0, scalar2=None, op0=ALU.is_ge), ["mg2"], ["mh2"])
            TT("dve", Msf[:, 3, :], mf2[:], mh2[:], ALU.add, ["mf2", "mh2"], ["Msf"])
            for tp in range(4):
                V(lambda e, tp=tp: e.memset(Msf[:, 4 + tp, :], 0.0), [], ["Msf"])
                V(lambda e, tp=tp: e.memset(Msf[:, 4 + tp, tp:tp + 1], 1.0), [], ["Msf"])
            V(lambda e: e.tensor_scalar(out=mh2[:], in0=mg2[:], scalar1=4.0, scalar2=None, op0=ALU.is_le), ["mg2", "Msf"], ["mh2"])
            V(lambda e: e.tensor_scalar(out=mf2[:], in0=mg2[:], scalar1=4.0, scalar2=2.0, op0=ALU.is_equal, op1=ALU.mult), ["mg2", "Msf"], ["mf2"])
            TT("dve", Msf[:, 8, :], mf2[:], mh2[:], ALU.add, ["mf2", "mh2"], ["Msf"])
            for e_ in range(2):
                V(lambda e, e_=e_: e.tensor_copy(out=Msb[:, e_, :, :], in_=Msf[:, :, :]), ["Msf"], ["Msb"])

            ld = [0]
            for b in range(SB):
                bi = b % 2
                for tile in range(8):
                    rows = slice(1536 + 128 * tile, 1536 + 128 * tile + 128) if tile < 4 else slice(tile - 4, 2048, 16)
                    i = ld[0] % 3
                    ld[0] += 1
                    cx.dma("sp", kst[i][:, :], cwk[b, rows, :], writes=["kst%d" % i])
                    cx.dma("sp", vst[i][:, :], cwv[b, rows, :], writes=["svst%d" % i])
                    pi = ld[0] % 2
                    cx.op("dve" if tile % 2 == 0 else "pool", lambda e, i=i: e.tensor_copy(out=kbf[i][:, :], in_=kst[i][:, :]), reads=["kst%d" % i], writes=["kbf%d" % i])
                    for hp in range(4):
                        P(lambda pe, i=i, hp=hp, pi=pi: pe.transpose(out=pTs[pi][:, hp, :], in_=kbf[i][:, 128 * hp:128 * hp + 128], identity=ident_b[:, :]),
                          ["kbf%d" % i, "ident_b"], ["pTs%d" % pi])
                    A(lambda e, pi=pi, bi=bi, tile=tile: e.activation(out=kTs[bi][:, :, tile, :], in_=pTs[pi][:, 0:4, :], func=AF.Copy), [], ["kTs%d" % bi, "pTs%d" % pi])
                    cx.op("pool" if tile % 2 == 0 else "dve", lambda e, i=i, bi=bi, tile=tile: e.tensor_copy(
                        out=Vs[bi][:, tile, :, :], in_=vst[i][:, :].rearrange("p (h d) -> p h d", h=8)), reads=["svst%d" % i], writes=["Vs%d_%d" % (bi, tile % 2)])
                tok = slice(2048 + 4 * b, 2048 + 4 * b + 4)
                ai = b % 2
                def sc_fn(hp, b=b, bi=bi, tok=tok):
                    si = (4 * b + hp) % 2
                    Sv = Sps[si][:, 0:72].rearrange("p (e t q) -> p e t q", e=2, t=9)
                    for e_ in range(2):
                        pb = 64 * e_
                        qap = qT[pb:pb + 64, hp, tok]
                        for tile in range(8):
                            P(lambda pe, Sv=Sv, e_=e_, tile=tile, pb=pb, hp=hp, bi=bi, qap=qap: pe.matmul(
                                Sv[:, e_, tile, :], lhsT=kTs[bi][pb:pb + 64, hp, tile, :], rhs=qap, start=True, stop=True), ["kTs%d" % bi, "qT"], ["Sps%d" % si])
                        P(lambda pe, Sv=Sv, e_=e_, pb=pb, hp=hp, qap=qap, tok=tok: pe.matmul(
                            Sv[0:4, e_, 8, :], lhsT=kT[pb:pb + 64, hp, tok], rhs=qap, start=True, stop=True), ["kT", "qT"], ["Sps%d" % si])
                    A(lambda e, si=si: e.activation(out=Ps[si][:, :, :, :].rearrange("p e t q -> p (e t q)"), in_=Sps[si][:, 0:72], func=AF.Exp, scale=0.125),
                      [], ["Ps%d" % si, "Sps%d" % si])
                    TT("dve", Ps[si][:, :, :, :], Ps[si][:, :, :, :], Msb[:, :, :, :], ALU.mult, ["Msb"], ["Ps%d" % si])

                def pv_fn(hp, b=b, bi=bi, ai=ai):
                    si = (4 * b + hp) % 2
                    Av = accs[ai][:, 0:32].rearrange("p (a e q) -> p a e q", a=4, e=2)
                    for e_ in range(2):
                        h = 2 * hp + e_
                        vo, do = (0, 64) if e_ == 0 else (64, 0)
                        for tile in range(9):
                            if tile < 8:
                                vap, oap, pap = Vs[bi][:, tile, h, :], ones_b[:, 0:64], Ps[si][:, e_, tile, :]
                            else:
                                vap, oap, pap = Vn[0:4, b, h, :], ones_b[0:4, 0:64], Ps[si][0:4, e_, 8, :]
                            P(lambda pe, vap=vap, pap=pap, tile=tile, vo=vo, hp=hp, e_=e_, Av=Av: pe.matmul(
                                Av[vo:vo + 64, hp, e_, :], lhsT=vap, rhs=pap, start=(tile == 0), stop=(tile == 8), skip_group_check=True, tile_position=(0, vo)),
                                ["Vs%d_0" % bi, "Vs%d_1" % bi, "Vn", "Ps%d" % si], ["accs%d" % ai])
                            P(lambda pe, oap=oap, pap=pap, tile=tile, do=do, hp=hp, e_=e_, Av=Av: pe.matmul(
                                Av[do:do + 64, hp, e_, :], lhsT=oap, rhs=pap, start=(tile == 0), stop=(tile == 8), skip_group_check=True, tile_position=(0, do)),
                                ["ones_b", "Ps%d" % si], ["accs%d" % ai])

                sc_fn(0)
                for hp in range(4):
                    if hp + 1 < 4:
                        sc_fn(hp + 1)
                    pv_fn(hp)
                Av = accs[ai][:, 0:32].rearrange("p (a e q) -> p a e q", a=4, e=2)
                for e_ in range(2):
                    vo, do = (0, 64) if e_ == 0 else (64, 0)
                    V(lambda e, Av=Av, do=do, e_=e_: e.reciprocal(out=rec[do:do + 64, :, :], in_=Av[do:do + 64, :, e_, :]), [], ["s_rec", "accs%d" % ai])
                    V(lambda e, Av=Av, do=do, vo=vo, e_=e_, tok=tok: e.tensor_tensor(out=oas[vo:vo + 64, :, 4 * (tok.start - 2048) // 4:4 * (tok.start - 2048) // 4 + 4],
                                                                                      in0=Av[vo:vo + 64, :, e_, :], in1=rec[do:do + 64, :, :], op=ALU.mult),
                      ["s_rec"], ["oas", "accs%d" % ai])
            A(lambda e: e.activation(out=osq[:, :, :], in_=oas[:, :, :], func=AF.Square), ["oas"], ["s_osq"])
            for hp in range(4):
                P(lambda pe, hp=hp: pe.matmul(pss[:, 0:ST], lhsT=ones_b[:, :], rhs=osq[:, hp, :], start=(hp == 0), stop=(hp == 3)), ["ones_b", "s_osq"], ["s_pss"])
            A(lambda e: e.activation(out=rstd[:, :], in_=pss[:, 0:ST], func=AF.Sqrt, bias=eps_t[:, 0:1], scale=1.0 / 512.0), ["eps_t"], ["s_rstd", "s_pss"])
            V(lambda e: e.reciprocal(out=rstd[:, :], in_=rstd[:, :]), ["s_rstd"], ["s_rstd"])
            for hp in range(4):
                V(lambda e, hp=hp: e.scalar_tensor_tensor(out=attT[:, hp, 2048:T], in0=oas[:, hp, :], scalar=gatt[:, hp:hp + 1],
                                                          in1=rstd[:, :], op0=ALU.mult, op1=ALU.mult), ["oas", "gatt", "s_rstd"], ["attT"])
            cx.barrier()
        pqk.close()

        TILES = [(128 * i, 128) for i in range(16)] + [(2048, ST)]
        x1s = nc.dram_tensor("x1s", [T, D], F32, kind="Internal").ap()
        x2s = nc.dram_tensor("x2s", [T, D], F32, kind="Internal").ap()

        def xin_rows(t0, rows):
            return x_p[t0:t0 + rows, :] if t0 < 2048 else x_s[:, :]

        def dense_ln(tag, nk, W, lhs_fn, in_keys, prep_fn, xres_fn, g_d, b_d, out_fn, xT_out, xT_key, tiles=None, nbuf=3, wb_pre=None):
            with ExitStack() as ph:
                if wb_pre is not None:
                    wb, wkeys = wb_pre
                else:
                    wb = sb(tag + "wb", [128, nk, 1024], BF16, ph)
                    wst = [sb(tag + "wst%d" % i, [128, 1, 1024], F32, ph) for i in range(2)]
                    Wv = W.rearrange("(kc p) c -> p kc c", p=128)
                    for gi, g0 in enumerate(range(0, nk, 1)):
                        i = gi % 2
                        cx.dma("sp", wst[i][:, :, :], Wv[:, g0:g0 + 1, :], writes=[tag + "wst%d" % i])
                        cx.op("pool" if i == 0 else "dve", lambda e, i=i, g0=g0: e.tensor_copy(out=wb[:, g0:g0 + 1, :], in_=wst[i][:, :, :]),
                              reads=[tag + "wst%d" % i], writes=[tag + "wb%d" % i])
                    wkeys = [tag + "wb0", tag + "wb1"]
                gam = sb(tag + "gam", [128, 1024], F32, ph)
                bet = sb(tag + "bet", [128, 1024], F32, ph)
                cx.dma("sp", gam[:, :], g_d.rearrange("(o c) -> o c", o=1).broadcast_to([128, 1024]), writes=[tag + "gam"])
                cx.dma("act", bet[:, :], b_d.rearrange("(o c) -> o c", o=1).broadcast_to([128, 1024]), writes=[tag + "bet"])
                xr = [sb(tag + "xr%d" % i, [128, 1024], F32, ph) for i in range(nbuf)]
                rr = [sb(tag + "rr%d" % i, [128, 1024], F32, ph) for i in range(nbuf)]
                nn = [sb(tag + "nn%d" % i, [128, 1024], F32, ph) for i in range(nbuf)]
                nb = [sb(tag + "nb%d" % i, [128, 1024], BF16, ph) for i in range(nbuf)] if xT_out is not None else None
                st6 = sb(tag + "st6", [128, 2, 6], F32, ph)
                mv = sb(tag + "mv", [128, 2], F32, ph)
                sd = sb(tag + "sd", [128, 1], F32, ph)
                nbi = sb(tag + "nbi", [128, 1], F32, ph)
                pd = [ps(tag + "pd%d" % i, [128, 1024], F32, ph) for i in range(nbuf)]
                ptr = [ps(tag + "ptr%d" % i, [128, 8, 128], BF16, ph) for i in range(2)]
                pending = []
                for ti, (t0, rows) in enumerate(TILES if tiles is None else tiles):
                    i = ti % nbuf
                    j2 = ti % 2
                    ik = list(in_keys)
                    if prep_fn is not None:
                        ik = ik + prep_fn(ti, t0, rows)
                    cx.dma("sp", xr[i][0:rows, :], xres_fn(t0, rows), writes=[tag + "xr%d" % i])
                    for half in range(2):
                        for kc in range(nk):
                            P(lambda pe, i=i, half=half, kc=kc, t0=t0, rows=rows: pe.matmul(
                                pd[i][0:rows, half * 512:half * 512 + 512], lhsT=lhs_fn(kc, ti, t0, rows), rhs=wb[:, kc, half * 512:half * 512 + 512],
                                start=(kc == 0), stop=(kc == nk - 1)), wkeys + ik, [tag + "pd%d_%d" % (i, half)])
                    for half in range(2):
                        hs = slice(half * 512, half * 512 + 512)
                        V(lambda e, i=i, rows=rows, hs=hs: e.scalar_tensor_tensor(out=rr[i][0:rows, hs], in0=xr[i][0:rows, hs], scalar=ALPHA, in1=pd[i][0:rows, hs],
                                                                                  op0=ALU.mult, op1=ALU.add), [tag + "xr%d" % i], [tag + "rr%d_%d" % (i, half), tag + "pd%d_%d" % (i, half)])
                        V(lambda e, i=i, rows=rows, hs=hs, half=half: e.bn_stats(out=st6[0:rows, half, :], in_=rr[i][0:rows, hs]), [tag + "rr%d_%d" % (i, half)], [tag + "st6_%d" % half])
                    V(lambda e, rows=rows: e.bn_aggr(out=mv[0:rows, :], in_=st6[0:rows, :, :].rearrange("p a b -> p (a b)")), [tag + "st6_0", tag + "st6_1"], [tag + "mv"])
                    A(lambda e, rows=rows: e.activation(out=sd[0:rows, :], in_=mv[0:rows, 1:2], func=AF.Sqrt, bias=eps_t[0:rows, 0:1], scale=1.0), [tag + "mv", "eps_t"], [tag + "sd"])
                    V(lambda e, rows=rows: e.reciprocal(out=sd[0:rows, :], in_=sd[0:rows, :]), [tag + "sd"], [tag + "sd"])
                    V(lambda e, rows=rows: e.scalar_tensor_tensor(out=nbi[0:rows, :], in0=mv[0:rows, 0:1], scalar=-1.0, in1=sd[0:rows, :], op0=ALU.mult, op1=ALU.mult),
                      [tag + "mv", tag + "sd"], [tag + "nbi"])
                    A(lambda e, i=i, rows=rows: e.activation(out=nn[i][0:rows, :], in_=rr[i][0:rows, :], func=AF.Identity, scale=sd[0:rows, 0:1], bias=nbi[0:rows, 0:1]),
                      [tag + "rr%d_0" % i, tag + "rr%d_1" % i, tag + "sd", tag + "nbi"], [tag + "nn%d" % i])
                    TT("dve", nn[i][0:rows, :], nn[i][0:rows, :], gam[0:rows, :], ALU.mult, [tag + "gam", tag + "nn%d" % i], [tag + "nn%d" % i])
                    TT("dve", nn[i][0:rows, :], nn[i][0:rows, :], bet[0:rows, :], ALU.add, [tag + "bet", tag + "nn%d" % i], [tag + "nn%d" % i])
                    def tail_fn(i=i, j2=j2, t0=t0, rows=rows):
                        cx.dma("sp", out_fn(t0, rows), nn[i][0:rows, :], reads=[tag + "nn%d" % i], writes=[tag + "out"])
                        if xT_out is not None:
                            G(lambda e: e.tensor_copy(out=nb[i][0:rows, :], in_=nn[i][0:rows, :]), [tag + "nn%d" % i], [tag + "nb%d" % i])
                            for kc in range(8):
                                P(lambda pe, kc=kc: pe.transpose(out=ptr[j2][:, kc, 0:rows], in_=nb[i][0:rows, kc * 128:(kc + 1) * 128],
                                                                 identity=ident_b[0:rows, 0:rows]), [tag + "nb%d" % i, "ident_b"], [tag + "ptr%d" % j2])
                            A(lambda e: e.activation(out=xT_out[:, :, t0:t0 + rows], in_=ptr[j2][:, :, 0:rows], func=AF.Copy), [], [xT_key, tag + "ptr%d" % j2])
                    if pending:
                        pending.pop(0)()
                    pending.append(tail_fn)
                while pending:
                    pending.pop(0)()
                cx.barrier()

        pxa = ExitStack()
        x1T = sb("x1T", [128, 8, T], BF16, pxa)
        with ExitStack() as ph6:
            mixb = [sb("mixb%d" % i, [128, 4, 128], BF16, ph6) for i in range(2)]

            def prep6(ti, t0, rows):
                i = ti % 2
                G(lambda e: e.tensor_copy(out=mixb[i][:, :, 0:rows], in_=yssm[:, :, t0:t0 + rows]), ["yssm"], ["mixb%d" % i])
                return ["mixb%d" % i]

            def lhs6(kc, ti, t0, rows):
                if kc < 4:
                    return attT[:, kc, t0:t0 + rows]
                return mixb[ti % 2][:, kc - 4, 0:rows]

            dense_ln("l1", 8, w_out, lhs6, ["attT"], prep6, xin_rows, ln_g[0], ln_b[0], lambda t0, rows: x1s[t0:t0 + rows, :], x1T, "x1T")
        if STG < 7:
            cx.enabled = False
        pqm = ExitStack()
        qmT = sb("qmT", [128, 8, T], BF16, pqm)
        QK = ["qm%d" % h for h in range(4)]
        with ExitStack() as ph:
            wqb = sb("wqb", [128, 8, 1024], BF16, ph)
            wqs = [sb("wqs%d" % i, [128, 2, 1024], F32, ph) for i in range(2)]
            Wv = w_mq.rearrange("(kc p) c -> p kc c", p=128)
            for gi in range(4):
                i = gi % 2
                cx.dma("sp" if i == 0 else "act", wqs[i][:, :, :], Wv[:, 2 * gi:2 * gi + 2, :], writes=["wqs%d" % i])
                cx.op("pool" if i == 0 else "dve", lambda e, i=i, gi=gi: e.tensor_copy(out=wqb[:, 2 * gi:2 * gi + 2, :], in_=wqs[i][:, :, :]),
                      reads=["wqs%d" % i], writes=["wqb%d" % i])
            pq = [ps("pq%d" % i, [128, 512], F32, ph) for i in range(4)]
            cnt = 0
            for oc in range(8):
                for nt in range(5):
                    t0 = nt * 512
                    n = 512 if nt < 4 else ST
                    p = cnt % 4
                    cnt += 1
                    for kc in range(8):
                        P(lambda pe, p=p, oc=oc, kc=kc, t0=t0, n=n: pe.matmul(pq[p][:, 0:n], lhsT=wqb[:, kc, oc * 128:(oc + 1) * 128], rhs=x1T[:, kc, t0:t0 + n],
                                                                            start=(kc == 0), stop=(kc == 7)), ["wqb0", "wqb1", "x1T"], ["pq%d" % p])
                    if cnt % 2:
                        A(lambda e, p=p, oc=oc, t0=t0, n=n: e.activation(out=qmT[:, oc, t0:t0 + n], in_=pq[p][:, 0:n], func=AF.Copy), [], ["qm%d" % (oc // 2), "pq%d" % p])
                    else:
                        V(lambda e, p=p, oc=oc, t0=t0, n=n: e.tensor_copy(out=qmT[:, oc, t0:t0 + n], in_=pq[p][:, 0:n]), [], ["qm%d" % (oc // 2), "pq%d" % p])
            cx.barrier()

        with ExitStack() as ph:
            mst = [sb("mst%d" % i, [128, 1024], F32, ph) for i in range(2)]
            mkb = [sb("mkb%d" % i, [128, 1024], BF16, ph) for i in range(2)]
            mkT = [sb("mkT%d" % i, [128, 8, 256], BF16, ph) for i in range(2)]
            mvb = [sb("mvb%d" % i, [128, 2, 1024], BF16, ph) for i in range(2)]
            Pm = [sb("Pm%d" % i, [128, 2, 512], BF16, ph) for i in range(2)]
            recm = sb("recm", [128, 512], F32, ph)
            ptm = [ps("ptm%d" % i, [128, 8, 128], BF16, ph) for i in range(2)]
            Sm = [ps("Sm%d" % i, [128, 512], F32, ph) for i in range(2)]
            Om = [ps("Om%d" % i, [128, 512], F32, ph) for i in range(2)]
            Dm = ps("Dm", [128, 512], F32, ph)
            ldc = [0]

            def load_mem(kd, vd, slot):
                for mt in range(2):
                    i = ldc[0] % 2
                    ldc[0] += 1
                    cx.dma("sp", mst[i][:, :], kd[mt * 128:(mt + 1) * 128, :], writes=["mst%d" % i])
                    V(lambda e, i=i: e.tensor_copy(out=mkb[i][:, :], in_=mst[i][:, :]), ["mst%d" % i], ["mkb%d" % i])
                    for oc in range(8):
                        P(lambda pe, i=i, oc=oc: pe.transpose(out=ptm[i][:, oc, :], in_=mkb[i][:, oc * 128:(oc + 1) * 128], identity=ident_b[:, :]),
                          ["mkb%d" % i, "ident_b"], ["ptm%d" % i])
                    A(lambda e, i=i, mt=mt, slot=slot: e.activation(out=mkT[slot][:, :, mt * 128:(mt + 1) * 128], in_=ptm[i][:, :, :], func=AF.Copy), [], ["mkT%d" % slot, "ptm%d" % i])
                for mt in range(2):
                    i = ldc[0] % 2
                    ldc[0] += 1
                    cx.dma("sp", mst[i][:, :], vd[mt * 128:(mt + 1) * 128, :], writes=["mst%d" % i])
                    V(lambda e, i=i, mt=mt, slot=slot: e.tensor_copy(out=mvb[slot][:, mt, :], in_=mst[i][:, :]), ["mst%d" % i], ["mvb%d" % slot])

            if STG < 7.2:
                cx.enabled = False
            load_mem(mk_p, mv_p, 0)
            it = 0
            for nt in range(4):
                tok = slice(512 * nt, 512 * nt + 512)
                for h in range(4):
                    pi = it % 2
                    it += 1
                    for mt in range(2):
                        for c in range(2):
                            P(lambda pe, mt=mt, c=c, h=h, tok=tok: pe.matmul(Sm[mt][:, :], lhsT=mkT[0][:, 2 * h + c, mt * 128:(mt + 1) * 128], rhs=qmT[:, 2 * h + c, tok],
                                                                         start=(c == 0), stop=(c == 1)), ["mkT0", "qm%d" % h], ["Sm%d" % mt])
                        A(lambda e, mt=mt, pi=pi: e.activation(out=Pm[pi][:, mt, :], in_=Sm[mt][:, :], func=AF.Exp, scale=1.0 / 16.0), [], ["Pm%d_%d" % (pi, mt), "Sm%d" % mt])
                    pk = ["Pm%d_0" % pi, "Pm%d_1" % pi]
                    for c in range(2):
                        for mt in range(2):
                            P(lambda pe, mt=mt, c=c, h=h, pi=pi: pe.matmul(Om[c][:, :], lhsT=mvb[0][:, mt, h * 256 + c * 128:h * 256 + c * 128 + 128], rhs=Pm[pi][:, mt, :],
                                                                        start=(mt == 0), stop=(mt == 1)), ["mvb0"] + pk, ["Om%d" % c])
                    for mt in range(2):
                        P(lambda pe, mt=mt, pi=pi: pe.matmul(Dm[:, :], lhsT=ones_b[:, :], rhs=Pm[pi][:, mt, :], start=(mt == 0), stop=(mt == 1)), ["ones_b"] + pk, ["Dm"])
                    V(lambda e: e.reciprocal(out=recm[:, :], in_=Dm[:, :]), [], ["recm", "Dm"])
                    for c in range(2):
                        V(lambda e, c=c, h=h, tok=tok: e.tensor_tensor(out=qmT[:, 2 * h + c, tok], in0=Om[c][:, :], in1=recm[:, :], op=ALU.mult),
                          ["recm"], ["qm%d" % h, "Om%d" % c])
            cx.barrier()

            if STG < 7.3:
                cx.enabled = False
            for b in range(SB):
                slot = b % 2
                load_mem(cmk[b], cmv[b], slot)
                tok = slice(2048 + 4 * b, 2048 + 4 * b + 4)
                Sv = Sm[slot][:, 0:32].rearrange("p (h m q) -> p h m q", h=4, m=2)
                Ov = Om[slot][:, 0:32].rearrange("p (h c q) -> p h c q", h=4, c=2)
                Dv = Om[slot][:, 32:48].rearrange("p (h q) -> p h q", h=4)
                Pv = Pm[slot][:, 0, 0:32].rearrange("p (h m q) -> p h m q", h=4, m=2)
                for h in range(4):
                    for mt in range(2):
                        for c in range(2):
                            P(lambda pe, mt=mt, c=c, h=h, tok=tok, Sv=Sv, slot=slot: pe.matmul(
                                Sv[:, h, mt, :], lhsT=mkT[slot][:, 2 * h + c, mt * 128:(mt + 1) * 128], rhs=qmT[:, 2 * h + c, tok], start=(c == 0), stop=(c == 1),
                                skip_group_check=True), ["mkT%d" % slot] + QK, ["Sm%d" % slot])
                A(lambda e, slot=slot: e.activation(out=Pm[slot][:, 0, 0:32], in_=Sm[slot][:, 0:32], func=AF.Exp, scale=1.0 / 16.0), [], ["Pm%d_0" % slot, "Sm%d" % slot])
                for h in range(4):
                    for c in range(2):
                        for mt in range(2):
                            P(lambda pe, mt=mt, c=c, h=h, Ov=Ov, Pv=Pv, slot=slot: pe.matmul(
                                Ov[:, h, c, :], lhsT=mvb[slot][:, mt, h * 256 + c * 128:h * 256 + c * 128 + 128], rhs=Pv[:, h, mt, :], start=(mt == 0), stop=(mt == 1),
                                skip_group_check=True), ["mvb%d" % slot, "Pm%d_0" % slot], ["Om%d" % slot])
                    for mt in range(2):
                        P(lambda pe, mt=mt, h=h, Dv=Dv, Pv=Pv: pe.matmul(Dv[:, h, :], lhsT=ones_b[:, :], rhs=Pv[:, h, mt, :], start=(mt == 0), stop=(mt == 1),
                                                                      skip_group_check=True), ["ones_b", "Pm%d_0" % slot], ["Om%d" % slot])
                V(lambda e, Dv=Dv: e.reciprocal(out=recm[:, 0:16].rearrange("p (h q) -> p h q", h=4), in_=Dv), [], ["recm", "Om%d" % slot])
                V(lambda e, Ov=Ov, tok=tok: e.tensor_tensor(out=qmT[:, :, tok].rearrange("p (h c) q -> p h c q", c=2), in0=Ov,
                                                            in1=recm[:, 0:16].rearrange("p (h q) -> p h q", h=4).unsqueeze(2).broadcast_to([128, 4, 2, 4]), op=ALU.mult),
                  ["recm"], QK + ["Om%d" % slot])
            cx.barrier()

        if STG < 7.4:
            cx.enabled = False
        dense_ln("l2", 8, w_mo, lambda kc, ti, t0, rows: qmT[:, kc, t0:t0 + rows], QK, None,
                 lambda t0, rows: x1s[t0:t0 + rows, :], ln_g[1], ln_b[1], lambda t0, rows: x2s[t0:t0 + rows, :], x1T, "x1T")
        pqm.close()


        if STG < 8:
            cx.enabled = False
        x2T = x1T
        GROUPS = [(0, 768, [(0, 512), (512, 256)]), (768, 768, [(768, 512), (1280, 256)]), (1536, 576, [(1536, 512), (2048, ST)])]
        phT = ExitStack()
        wdb = sb("wdb", [128, 22, 1024], BF16, phT)
        hT = sb("hT", [128, 22, 768], BF16, phT)
        with ExitStack() as ph:
            wdst = [sb("wdst%d" % i, [128, 1, 1024], F32, ph) for i in range(3)]
            Wdv = w_down.rearrange("(kc p) c -> p kc c", p=128)
            for g0 in range(22):
                i = g0 % 3
                cx.dma("sp", wdst[i][:, :, :], Wdv[:, g0:g0 + 1, :], writes=["wdst%d" % i])
                cx.op(("act", "dve", "pool")[i], (lambda e, i=i, g0=g0: e.activation(out=wdb[:, g0:g0 + 1, :], in_=wdst[i][:, :, :], func=AF.Copy)) if i == 0 else
                      (lambda e, i=i, g0=g0: e.tensor_copy(out=wdb[:, g0:g0 + 1, :], in_=wdst[i][:, :, :])), reads=["wdst%d" % i], writes=["wdb%d" % i])
            cx.barrier()
        WDK = ["wdb0", "wdb1", "wdb2"]
        wgv = w_gate.rearrange("(kc p) f -> p kc f", p=128)
        wuv = w_up.rearrange("(kc p) f -> p kc f", p=128)
        for gidx, (g0, glen, subs) in enumerate(GROUPS):
            with ExitStack() as ph:
                gst = [sb("gst%d_%d" % (gidx, i), [128, 2, 8, 128], F32, ph) for i in range(2)]
                gbf = [sb("gbf%d_%d" % (gidx, i), [128, 2, 8, 128], BF16, ph) for i in range(2)]
                sg = [sb("sg%d_%d" % (gidx, i), [128, 512], F32, ph) for i in range(4)]
                tg = [sb("tg%d_%d" % (gidx, i), [128, 512], F32, ph) for i in range(4)]
                pg = [ps("pg%d_%d" % (gidx, i), [128, 512], F32, ph) for i in range(4)]
                pu = [ps("pu%d_%d" % (gidx, i), [128, 512], F32, ph) for i in range(4)]
                it = 0
                def wload(fc_):
                    i_ = fc_ % 2
                    cx.dma("sp", gst[i_][:, 0, :, :], wgv[:, :, fc_ * 128:(fc_ + 1) * 128], writes=["gst%d_0" % i_])
                    cx.dma("sp", gst[i_][:, 1, :, :], wuv[:, :, fc_ * 128:(fc_ + 1) * 128], writes=["gst%d_1" % i_])
                wload(0)
                for fc in range(22):
                    i = fc % 2
                    G(lambda e, i=i: e.tensor_copy(out=gbf[i][:, 0, :, :], in_=gst[i][:, 0, :, :]), ["gst%d_0" % i], ["gbf%d_0" % i])
                    V(lambda e, i=i: e.tensor_copy(out=gbf[i][:, 1, :, :], in_=gst[i][:, 1, :, :]), ["gst%d_1" % i], ["gbf%d_1" % i])
                    if fc + 1 < 22:
                        wload(fc + 1)
                    for (t0, n) in subs:
                        p = it % 4
                        it += 1
                        for kc in range(8):
                            P(lambda pe, p=p, i=i, kc=kc, t0=t0, n=n: pe.matmul(pg[p][:, 0:n], lhsT=gbf[i][:, 0, kc, :], rhs=x2T[:, kc, t0:t0 + n],
                                                                             start=(kc == 0), stop=(kc == 7)), ["gbf%d_0" % i, "x1T"], ["pg%d" % p])
                        for kc in range(8):
                            P(lambda pe, p=p, i=i, kc=kc, t0=t0, n=n: pe.matmul(pu[p][:, 0:n], lhsT=gbf[i][:, 1, kc, :], rhs=x2T[:, kc, t0:t0 + n],
                                                                             start=(kc == 0), stop=(kc == 7)), ["gbf%d_1" % i, "x1T"], ["pu%d" % p])
                        A(lambda e, p=p, n=n: e.activation(out=sg[p][:, 0:n], in_=pg[p][:, 0:n], func=AF.Sigmoid), [], ["sg%d" % p, "pg%d" % p])
                        V(lambda e, p=p, n=n: e.tensor_tensor(out=tg[p][:, 0:n], in0=pg[p][:, 0:n], in1=sg[p][:, 0:n], op=ALU.mult), ["sg%d" % p], ["tg%d" % p, "pg%d" % p])
                        V(lambda e, p=p, n=n, t0=t0, fc=fc, g0=g0: e.tensor_tensor(out=hT[:, fc, t0 - g0:t0 - g0 + n], in0=pu[p][:, 0:n], in1=tg[p][:, 0:n], op=ALU.mult),
                          ["tg%d" % p], ["hT", "pu%d" % p])
                cx.barrier()
            gt = [(t0, rows) for (t0, rows) in TILES if g0 <= t0 < g0 + glen]
            dense_ln("l3_%d" % gidx, 22, w_down, lambda kc, ti, t0, rows, g0=g0: hT[:, kc, t0 - g0:t0 - g0 + rows], ["hT"], None,
                     lambda t0, rows: x2s[t0:t0 + rows, :], ln_g[2], ln_b[2],
                     lambda t0, rows: (y_p[t0:t0 + rows, :] if t0 < 2048 else y_s[:, :]), None, None, tiles=gt, nbuf=3, wb_pre=(wdb, []))
        phT.close()
        cx.enabled = True
        pxa.close()
        cx.finish()
    return nc


_NC_CACHE = {}
_DEV = {}


def kernel(**inp):
    f = lambda a: np.ascontiguousarray(np.asarray(a, dtype=np.float32))
    if "nc" not in _NC_CACHE:
        _NC_CACHE["nc"] = build()
    nc = _NC_CACHE["nc"]
    shared = {
        "w_in": f(inp["w_in"][0]), "g_att": f(inp["g_att"][0]), "g_ssm": f(inp["g_ssm"][0]),
        "a_re": f(inp["ssm_a_re"][0]).reshape(2048), "a_im": f(inp["ssm_a_im"][0]).reshape(2048),
        "log_dt": f(inp["ssm_log_dt"][0]),
        "b_re": f(inp["ssm_b_re"][0]).reshape(2048, 16), "b_im": f(inp["ssm_b_im"][0]).reshape(2048, 16),
        "c_re": f(inp["ssm_c_re"][0]).reshape(512, 64), "c_im": f(inp["ssm_c_im"][0]).reshape(512, 64),
        "d_skip": f(inp["ssm_d"][0]).reshape(512), "w_glu": f(inp["w_glu"][0]), "b_glu": f(inp["b_glu"][0]),
        "w_out": f(inp["w_out"][0]),
        "ln1_g": f(inp["ln1_g"][0]), "ln1_b": f(inp["ln1_b"][0]),
        "ln2_g": f(inp["ln2_g"][0]), "ln2_b": f(inp["ln2_b"][0]),
        "ln3_g": f(inp["ln3_g"][0]), "ln3_b": f(inp["ln3_b"][0]),
        "w_mq": f(inp["w_mem_q"][0]), "w_mk": f(inp["w_mem_k"][0]), "w_mv": f(inp["w_mem_v"][0]),
        "w_mo": f(inp["w_mem_o"][0]),
        "w_gate": f(inp["w_gate"][0]), "w_up": f(inp["w_up"][0]), "w_down": f(inp["w_down"][0]),
    }
    in_maps = []
    for c in range(NCORES):
        b0 = c * SB
        m = dict(shared)
        m["x_p"] = f(inp["x_prompt"][c])
        m["x_s"] = f(inp["x_sample"][b0:b0 + SB]).reshape(ST, D)
        m["cwk"] = f(inp["cache_win_k"][0, b0:b0 + SB]).reshape(SB, 2048, DATT)
        m["cwv"] = f(inp["cache_win_v"][0, b0:b0 + SB]).reshape(SB, 2048, DATT)
        m["s_re"] = f(inp["state_ssm_re"][0, b0:b0 + SB]).reshape(SB, 2048)
        m["s_im"] = f(inp["state_ssm_im"][0, b0:b0 + SB]).reshape(SB, 2048)
        m["cmk"] = f(inp["cache_mem_k"][0, b0:b0 + SB]).reshape(SB, NMEM, D)
        m["cmv"] = f(inp["cache_mem_v"][0, b0:b0 + SB]).reshape(SB, NMEM, D)
        m["memp"] = f(inp["mem_prompt"][c])
        in_maps.append(m)
    nrun = _DEV.get("cores", NCORES)
    res = run_bass_kernel_spmd(nc, in_maps[:nrun], core_ids=list(range(nrun)))
    R = list(res.results)
    _DEV["raw"] = R
    while len(R) < NCORES:
        R.append({k: np.zeros_like(np.asarray(v)) for k, v in R[0].items()})
    cat = lambda k: np.stack([np.asarray(R[c][k], dtype=np.float32) for c in range(NCORES)], 0)
    y_p = cat("y_p")
    y_s = cat("y_s").reshape(128, 4, D)
    wk_p = cat("wk_p").reshape(1, 8, S, 8, 64)
    wv_p = cat("wv_p").reshape(1, 8, S, 8, 64)
    wk_s = cat("wk_s").reshape(1, 128, 4, 8, 64)
    wv_s = cat("wv_s").reshape(1, 128, 4, 8, 64)
    hr_p = cat("hr_p").reshape(1, 8, 32, 64)
    hi_p = cat("hi_p").reshape(1, 8, 32, 64)
    hr_s = cat("hr_s").reshape(1, 128, 32, 64)
    hi_s = cat("hi_s").reshape(1, 128, 32, 64)
    mk_p = cat("mk_p").reshape(1, 8, NMEM, 4, 256)
    mv_p = cat("mv_p").reshape(1, 8, NMEM, 4, 256)
    return (y_p, y_s, wk_p, wv_p, wk_s, wv_s, hr_p, hi_p, hr_s, hi_s, mk_p, mv_p)
```

```python
import math
from contextlib import ExitStack
import numpy as np
import concourse.bass as bass
import concourse.mybir as mybir
from concourse.bass_utils import run_bass_kernel_spmd

F32 = mybir.dt.float32
BF16 = mybir.dt.bfloat16
I32 = mybir.dt.int32
AF = mybir.ActivationFunctionType
ALU = mybir.AluOpType

NCORES = 8
D = 1024
S = 2048
SB = 16
ST = 64
T = S + ST
DIN = 2048
DATT = 512
DFF = 2816
NMEM = 256
ALPHA = 2.0 ** 0.25
EPS = 1e-5
STAGE = 99


class _Stop(Exception):
    pass


class Ctx:
    def __init__(self, nc, es):
        self.nc = nc
        self.eng = {"pe": nc.tensor, "act": nc.scalar, "dve": nc.vector, "pool": nc.gpsimd, "sp": nc.sync}
        self.sem = {}
        self.cnt = {}
        for e in self.eng:
            self.sem[e] = es.enter_context(nc.semaphore("s_" + e))
            self.cnt[e] = 0
        self.R = 12
        self.dsem = {}
        self.dcnt = {}
        self.didx = {}
        for q in ("sp", "act", "pool"):
            self.dsem[q] = [es.enter_context(nc.semaphore("d_%s%d" % (q, i))) for i in range(self.R)]
            self.dcnt[q] = [0] * self.R
            self.didx[q] = 0
        self.seen = {e: {} for e in self.eng}
        self.bufs = {}
        self.dma_tokens = []
        self.rr = 0
        self.enabled = True

    def _semof(self, key):
        if isinstance(key, tuple):
            return self.dsem[key[0]][key[1]]
        return self.sem[key]

    def wait(self, e, tok):
        key, val = tok
        if self.seen[e].get(key, 0) >= val:
            return
        self.eng[e].wait_ge(self._semof(key), val)
        self.seen[e][key] = val

    def _deps(self, reads, writes):
        toks = {}
        for k in reads:
            b = self.bufs.get(k)
            if b and b["w"]:
                key, val = b["w"]
                toks[key] = max(toks.get(key, 0), val)
        for k in writes:
            b = self.bufs.get(k)
            if b:
                if b["w"]:
                    key, val = b["w"]
                    toks[key] = max(toks.get(key, 0), val)
                for key, val in b["r"].items():
                    toks[key] = max(toks.get(key, 0), val)
        return toks

    def _record(self, tok, reads, writes):
        for k in reads:
            b = self.bufs.setdefault(k, {"w": None, "r": {}})
            b["r"][tok[0]] = max(b["r"].get(tok[0], 0), tok[1])
        for k in writes:
            self.bufs[k] = {"w": tok, "r": {}}

    def op(self, e, fn, reads=(), writes=()):
        if not self.enabled:
            return None
        deps = self._deps(reads, writes)
        if e == "pe":
            own = 0
            for k in reads:
                b = self.bufs.get(k)
                if b and b["w"] and b["w"][0] == "pe":
                    own = max(own, b["w"][1])
            for k in writes:
                b = self.bufs.get(k)
                if b and b["r"].get("pe"):
                    own = max(own, 0)
            if own == 0:
                deps.pop("pe", None)
            else:
                deps["pe"] = own
        for key, val in deps.items():
            self.wait(e, (key, val))
        inst = fn(self.eng[e])
        self.cnt[e] += 1
        inst.then_inc(self.sem[e], 1)
        tok = (e, self.cnt[e])
        self._record(tok, reads, writes)
        return tok

    def dma(self, q, out, in_, reads=(), writes=(), **kw):
        if not self.enabled:
            return None
        for key, val in self._deps(reads, writes).items():
            self.wait(q, (key, val))
        i = self.didx[q]
        self.didx[q] = (i + 1) % self.R
        inst = self.eng[q].dma_start(out=out, in_=in_, **kw)
        self.dcnt[q][i] += 16
        inst.then_inc(self.dsem[q][i], 16)
        tok = ((q, i), self.dcnt[q][i])
        self.dma_tokens.append(tok)
        self._record(tok, reads, writes)
        return tok

    def barrier(self):
        for e in self.eng:
            if e != "sp" and self.cnt[e] > 0:
                self.wait("sp", (e, self.cnt[e]))
        for tok in self.dma_tokens:
            self.wait("sp", tok)
        self.dma_tokens = []
        inst = self.eng["sp"].nop()
        self.cnt["sp"] += 1
        inst.then_inc(self.sem["sp"], 1)
        for e in self.eng:
            if e != "sp":
                self.wait(e, ("sp", self.cnt["sp"]))
        self.bufs = {}

    def finish(self):
        self.barrier()


class _Catch:
    def __init__(self, cx):
        self.cx = cx

    def __enter__(self):
        return self

    def __exit__(self, et, ev, tb):
        if et is _Stop:
            self.cx.barrier()
            return True
        return False


def build():
    STG = _DEV.get('stage', 99)
    nc = bass.Bass("TRN2", target_bir_lowering=False)

    def din(name, shape):
        return nc.dram_tensor(name, list(shape), F32, kind="ExternalInput").ap()

    def dout(name, shape):
        return nc.dram_tensor(name, list(shape), F32, kind="ExternalOutput").ap()

    x_p = din("x_p", [S, D])
    x_s = din("x_s", [ST, D])
    cwk = din("cwk", [SB, 2048, DATT])
    cwv = din("cwv", [SB, 2048, DATT])
    s_re = din("s_re", [SB, 2048])
    s_im = din("s_im", [SB, 2048])
    cmk = din("cmk", [SB, NMEM, D])
    cmv = din("cmv", [SB, NMEM, D])
    memp = din("memp", [NMEM, D])
    w_in = din("w_in", [D, DIN])
    g_att = din("g_att", [DATT])
    g_ssm = din("g_ssm", [DATT])
    a_re = din("a_re", [2048])
    a_im = din("a_im", [2048])
    log_dt = din("log_dt", [32])
    b_re = din("b_re", [2048, 16])
    b_im = din("b_im", [2048, 16])
    c_re = din("c_re", [512, 64])
    c_im = din("c_im", [512, 64])
    d_skip = din("d_skip", [512])
    w_glu = din("w_glu", [512, 512])
    b_glu = din("b_glu", [512])
    w_out = din("w_out", [D, D])
    ln_g = [din("ln%d_g" % i, [D]) for i in (1, 2, 3)]
    ln_b = [din("ln%d_b" % i, [D]) for i in (1, 2, 3)]
    w_mq = din("w_mq", [D, D])
    w_mk = din("w_mk", [D, D])
    w_mv = din("w_mv", [D, D])
    w_mo = din("w_mo", [D, D])
    w_gate = din("w_gate", [D, DFF])
    w_up = din("w_up", [D, DFF])
    w_down = din("w_down", [DFF, D])

    y_p = dout("y_p", [S, D])
    y_s = dout("y_s", [ST, D])
    wk_p = dout("wk_p", [S, DATT])
    wv_p = dout("wv_p", [S, DATT])
    wk_s = dout("wk_s", [ST, DATT])
    wv_s = dout("wv_s", [ST, DATT])
    hr_p = dout("hr_p", [2048])
    hi_p = dout("hi_p", [2048])
    hr_s = dout("hr_s", [SB, 2048])
    hi_s = dout("hi_s", [SB, 2048])
    mk_p = dout("mk_p", [NMEM, D])
    mv_p = dout("mv_p", [NMEM, D])

    with ExitStack() as es:
        cx = Ctx(nc, es)
        _UID = [0]

        def sb(name, shape, dt=F32, stack=es):
            return stack.enter_context(nc.sbuf_tensor(name, list(shape), dt))

        def ps(name, shape, dt=F32, stack=es):
            return stack.enter_context(nc.psum_tensor(name, list(shape), dt))

        ident_b = sb("ident_b", [128, 128], BF16)
        ident_f = sb("ident_f", [128, 128], F32)
        ones_b = sb("ones_b", [128, 128], BF16)
        eps_t = sb("eps_t", [128, 1], F32)
        yssm = sb("yssm", [128, 4, T], F32)

        with ExitStack() as ph:
            it = sb("it_i", [128, 128], I32, ph)
            itf = sb("it_f", [128, 128], F32, ph)
            cx.op("pool", lambda g: g.iota(it[:], pattern=[[1, 128]], base=0, channel_multiplier=-1), writes=["it"])
            cx.op("dve", lambda v: v.tensor_copy(out=itf[:], in_=it[:]), reads=["it"], writes=["itf"])
            cx.op("dve", lambda v: v.tensor_scalar(out=ident_f[:], in0=itf[:], scalar1=0.0, scalar2=None,
                                                   op0=ALU.is_equal), reads=["itf"], writes=["ident_f"])
            cx.op("dve", lambda v: v.tensor_copy(out=ident_b[:], in_=ident_f[:]), reads=["ident_f"], writes=["ident_b"])
            cx.op("dve", lambda v: v.memset(ones_b[:], 1.0), writes=["ones_b"])
            cx.op("dve", lambda v: v.memset(eps_t[:], EPS), writes=["eps_t"])
            cx.barrier()

        def phase1(xT, srcs=None):
          _UID[0] += 1
          u_ = 'a%d_' % _UID[0]
          with ExitStack() as ph:
              xs = [sb(u_ + "xs%d" % i, [128, D], F32, ph) for i in range(2)]
              xb = [sb(u_ + "xb%d" % i, [128, D], BF16, ph) for i in range(2)]
              pt = [ps(u_ + "pt%d" % i, [128, 8, 128], BF16, ph) for i in range(2)]
              if srcs is None:
                  srcs = [(x_p[tt * 128:(tt + 1) * 128, :], 128) for tt in range(16)] + [(x_s[:, :], ST)]
              for tt, (src, rows) in enumerate(srcs):
                  i = tt % 2
                  cx.dma("sp", xs[i][0:rows, :], src, writes=["xs%d" % i])
                  cx.op("act" if tt % 2 else "dve",
                        (lambda e, i=i, rows=rows: e.activation(out=xb[i][0:rows, :], in_=xs[i][0:rows, :], func=AF.Copy))
                        if tt % 2 else
                        (lambda e, i=i, rows=rows: e.tensor_copy(out=xb[i][0:rows, :], in_=xs[i][0:rows, :])),
                        reads=["xs%d" % i], writes=["xb%d" % i])
                  for kc in range(8):
                      cx.op("pe", lambda pe, i=i, kc=kc, rows=rows: pe.transpose(
                          out=pt[i][:, kc, 0:rows], in_=xb[i][0:rows, kc * 128:(kc + 1) * 128],
                          identity=ident_b[0:rows, 0:rows]),
                          reads=["xb%d" % i, "ident_b"], writes=["pt%d_%d" % (i, kc)])
                  cx.op("dve" if tt % 2 else "act",
                        (lambda e, i=i, rows=rows, tt=tt: e.tensor_copy(out=xT[:, :, tt * 128:tt * 128 + rows], in_=pt[i][:, :, 0:rows]))
                        if tt % 2 else
                        (lambda e, i=i, rows=rows, tt=tt: e.activation(out=xT[:, :, tt * 128:tt * 128 + rows], in_=pt[i][:, :, 0:rows], func=AF.Copy)),
                        reads=["pt%d_%d" % (i, kc) for kc in range(8)], writes=["xT"])
              cx.barrier()

        def phase2(xT, cbs, qT, kT, uT):
          _UID[0] += 1
          u_ = 'b%d_' % _UID[0]
          with ExitStack() as ph:
              wst = [sb(u_ + "wst%d" % i, [128, 8, 512], F32, ph) for i in range(2)]
              wbf = [sb(u_ + "wbf%d" % i, [128, 8, 512], BF16, ph) for i in range(2)]
              ost = [sb(u_ + "ost%d" % i, [128, 512], F32, ph) for i in range(2)]
              pp = [ps(u_ + "pp%d" % i, [128, 512], F32, ph) for i in range(4)]
              w_v = w_in.rearrange("(kc p) c -> p kc c", p=128)
              ppi = 0
              osi = 0
              for cb in cbs:
                  i = cb % 2
                  cx.dma("sp" if cb % 2 == 0 else "act", wst[i][:, :, :], w_v[:, :, cb * 512:(cb + 1) * 512], writes=["wst%d" % i])
                  cx.op("pool", lambda e, i=i: e.tensor_copy(out=wbf[i][:, 0:4, :], in_=wst[i][:, 0:4, :]),
                        reads=["wst%d" % i], writes=["wbf%d_a" % i])
                  cx.op("dve", lambda e, i=i: e.tensor_copy(out=wbf[i][:, 4:8, :], in_=wst[i][:, 4:8, :]),
                        reads=["wst%d" % i], writes=["wbf%d_b" % i])
                  wkeys = ["wbf%d_a" % i, "wbf%d_b" % i]
                  if cb in (0, 1, 3):
                      dst = {0: qT, 1: kT, 3: uT}[cb]
                      dkey = {0: "qT", 1: "kT", 3: "uT"}[cb]
                      for sbk in range(4):
                          for nt in range(5):
                              t0 = nt * 512
                              n = 512 if nt < 4 else ST
                              p = ppi % 4
                              ppi += 1
                              for kc in range(8):
                                  cx.op("pe", lambda pe, p=p, i=i, kc=kc, sbk=sbk, t0=t0, n=n: pe.matmul(
                                      pp[p][:, 0:n], lhsT=wbf[i][:, kc, sbk * 128:(sbk + 1) * 128], rhs=xT[:, kc, t0:t0 + n],
                                      start=(kc == 0), stop=(kc == 7)),
                                      reads=wkeys + ["xT"], writes=["pp%d" % p])
                              if ppi % 2:
                                  cx.op("act", lambda e, p=p, sbk=sbk, t0=t0, n=n, dst=dst: e.activation(
                                      out=dst[:, sbk, t0:t0 + n], in_=pp[p][:, 0:n], func=AF.Copy),
                                      reads=["pp%d" % p], writes=[dkey + "%d_%d" % (sbk, nt)])
                              else:
                                  cx.op("dve", lambda e, p=p, sbk=sbk, t0=t0, n=n, dst=dst: e.tensor_copy(
                                      out=dst[:, sbk, t0:t0 + n], in_=pp[p][:, 0:n]),
                                      reads=["pp%d" % p], writes=[dkey + "%d_%d" % (sbk, nt)])
                  if cb in (1, 2):
                      dp, dsm = (wk_p, wk_s) if cb == 1 else (wv_p, wv_s)
                      for tt in range(17):
                          rows = 128 if tt < 16 else ST
                          p = ppi % 4
                          ppi += 1
                          for kc in range(8):
                              cx.op("pe", lambda pe, p=p, i=i, kc=kc, tt=tt, rows=rows: pe.matmul(
                                  pp[p][0:rows, :], lhsT=xT[:, kc, tt * 128:tt * 128 + rows], rhs=wbf[i][:, kc, :],
                                  start=(kc == 0), stop=(kc == 7)),
                                  reads=wkeys + ["xT"], writes=["pp%d" % p])
                          o = osi % 2
                          osi += 1
                          if osi % 2:
                              cx.op("act", lambda e, p=p, o=o, rows=rows: e.activation(
                                  out=ost[o][0:rows, :], in_=pp[p][0:rows, :], func=AF.Copy),
                                  reads=["pp%d" % p], writes=["ost%d" % o])
                          else:
                              cx.op("dve", lambda e, p=p, o=o, rows=rows: e.tensor_copy(
                                  out=ost[o][0:rows, :], in_=pp[p][0:rows, :]),
                                  reads=["pp%d" % p], writes=["ost%d" % o])
                          dstd = dp[tt * 128:(tt + 1) * 128, :] if tt < 16 else dsm[:, :]
                          cx.dma("sp", dstd, ost[o][0:rows, :], reads=["ost%d" % o], writes=["wkv_out"])
              cx.barrier()

        pxu = ExitStack()
        uT = sb("uT", [128, 4, T], BF16, pxu)
        with ExitStack() as px:
            xT = sb("xT", [128, 8, T], BF16, px)
            phase1(xT)
            phase2(xT, [3], None, None, uT)
        TWO_PI = 2.0 * math.pi
        C1 = 6.28125
        C2 = round((TWO_PI - C1) * (1 << 24)) / float(1 << 24)
        C3 = TWO_PI - C1 - C2

        def V(fn, r=(), w=()):
            return cx.op("dve", fn, reads=r, writes=w)

        def A(fn, r=(), w=()):
            return cx.op("act", fn, reads=r, writes=w)

        def G(fn, r=(), w=()):
            return cx.op("pool", fn, reads=r, writes=w)

        def P(fn, r=(), w=()):
            return cx.op("pe", fn, reads=r, writes=w)

        def TT(e, out, in0, in1, op, r, w):
            return cx.op(e, lambda en: en.tensor_tensor(out=out, in0=in0, in1=in1, op=op), reads=r, writes=w)

        def cis(ph, x, n, cos_o, sin_o, tag, xk, ok):
            u = sb(tag + "u", [128, n], F32, ph)
            ki = sb(tag + "ki", [128, n], I32, ph)
            kf = sb(tag + "kf", [128, n], F32, ph)
            r = sb(tag + "r", [128, n], F32, ph)
            m = sb(tag + "m", [128, n], F32, ph)
            k = tag
            V(lambda e: e.tensor_scalar(out=u[:], in0=x, scalar1=1.0 / TWO_PI, scalar2=None, op0=ALU.mult), [xk], [k + "u"])
            V(lambda e: e.tensor_copy(out=ki[:], in_=u[:]), [k + "u"], [k + "ki"])
            V(lambda e: e.tensor_copy(out=kf[:], in_=ki[:]), [k + "ki"], [k + "kf"])
            V(lambda e: e.scalar_tensor_tensor(out=r[:], in0=kf[:], scalar=-C1, in1=x, op0=ALU.mult, op1=ALU.add), [k + "kf", xk], [k + "r"])
            V(lambda e: e.scalar_tensor_tensor(out=r[:], in0=kf[:], scalar=-C2, in1=r[:], op0=ALU.mult, op1=ALU.add), [k + "kf", k + "r"], [k + "r"])
            V(lambda e: e.scalar_tensor_tensor(out=r[:], in0=kf[:], scalar=-C3, in1=r[:], op0=ALU.mult, op1=ALU.add), [k + "kf", k + "r"], [k + "r"])
            V(lambda e: e.tensor_scalar(out=m[:], in0=r[:], scalar1=math.pi, scalar2=-TWO_PI, op0=ALU.is_gt, op1=ALU.mult), [k + "r"], [k + "m"])
            TT("dve", r[:], r[:], m[:], ALU.add, [k + "r", k + "m"], [k + "r"])
            V(lambda e: e.tensor_scalar(out=m[:], in0=r[:], scalar1=-math.pi, scalar2=TWO_PI, op0=ALU.is_lt, op1=ALU.mult), [k + "r"], [k + "m"])
            TT("dve", r[:], r[:], m[:], ALU.add, [k + "r", k + "m"], [k + "r"])
            A(lambda e: e.activation(out=sin_o, in_=r[:], func=AF.Sin), [k + "r"], [ok + "s"])
            V(lambda e: e.tensor_scalar(out=u[:], in0=r[:], scalar1=math.pi / 2, scalar2=None, op0=ALU.add), [k + "r"], [k + "u"])
            V(lambda e: e.tensor_scalar(out=m[:], in0=u[:], scalar1=math.pi, scalar2=-TWO_PI, op0=ALU.is_gt, op1=ALU.mult), [k + "u"], [k + "m"])
            TT("dve", u[:], u[:], m[:], ALU.add, [k + "u", k + "m"], [k + "u"])
            A(lambda e: e.activation(out=cos_o, in_=u[:], func=AF.Sin), [k + "u"], [ok + "c"])

        with ExitStack() as ph:
            T1r = sb("T1r", [128, 4, 16, 128], BF16, ph)
            T1i = sb("T1i", [128, 4, 16, 128], BF16, ph)
            T2r = sb("T2r", [128, 16, 16, 32], BF16, ph)
            T2n = sb("T2n", [128, 16, 16, 32], BF16, ph)
            Kbd = sb("Kbd", [128, 4, 16, 128], BF16, ph)
            Fr = sb("Fr", [128, 16, 17], F32, ph)
            Fi = sb("Fi", [128, 16, 17], F32, ph)
            h0r = sb("h0r", [128, 16, SB], F32, ph)
            h0i = sb("h0i", [128, 16, SB], F32, ph)

            with ExitStack() as p2:
                AR = sb("AR", [128, 16], F32, p2)
                AI = sb("AI", [128, 16], F32, p2)
                DT = sb("DT", [128, 16], F32, p2)
                dtar = sb("dtar", [128, 16], F32, p2)
                dtai = sb("dtai", [128, 16], F32, p2)
                ARG = sb("ARG", [128, 16, 17], F32, p2)
                ANG = sb("ANG", [128, 16, 17], F32, p2)
                MAG = sb("MAG", [128, 16, 17], F32, p2)
                COS = sb("COS", [128, 16, 17], F32, p2)
                SIN = sb("SIN", [128, 16, 17], F32, p2)
                cor = sb("cor", [128, 16], F32, p2)
                coi = sb("coi", [128, 16], F32, p2)
                t1 = sb("t1", [128, 16], F32, p2)
                t2 = sb("t2", [128, 16], F32, p2)
                t3 = sb("t3", [128, 16], F32, p2)
                FBr = sb("FBr", [128, 16, 16], F32, p2)
                FBi = sb("FBi", [128, 16, 16], F32, p2)
                f1 = sb("f1", [128, 16, 16], F32, p2)
                BDr = sb("BDr", [128, 16, 32], F32, p2)
                BDi = sb("BDi", [128, 16, 32], F32, p2)
                CDr = sb("CDr", [128, 16, 32], F32, p2)
                CDi = sb("CDi", [128, 16, 32], F32, p2)
                CDrb = sb("CDrb", [128, 16, 32], BF16, p2)
                CDnb = sb("CDnb", [128, 16, 32], BF16, p2)
                cn = sb("cn", [128, 2, 4, 2, 64], F32, p2)
                dsk = sb("dsk", [128, 4], F32, p2)
                h0n = sb("h0n", [SB, 2048], F32, p2)
                tA = sb("tA", [128, 4, 16, 32], F32, p2)
                tB = sb("tB", [128, 4, 16, 32], F32, p2)
                LBr = sb("LBr", [128, 4, 16, 32], BF16, p2)
                LBi = sb("LBi", [128, 4, 16, 32], BF16, p2)
                pT = ps("pT", [128, 2, 16, 128], BF16, p2)
                pK = ps("pK", [128, 16, 32], F32, p2)
                pC = ps("pC", [128, 4, 128], F32, p2)
                pH = ps("pH", [128, 2, 16, SB], F32, p2)

                cx.dma("sp", AR[:, :], a_re.rearrange("(j q) -> q j", q=128), writes=["AR"], allow_slow_non_contiguous=True)
                cx.dma("sp", AI[:, :], a_im.rearrange("(j q) -> q j", q=128), writes=["AI"], allow_slow_non_contiguous=True)
                ldv = log_dt.rearrange("(j a) -> a j", a=2)
                for a in range(2):
                    cx.dma("act", DT[64 * a:64 * a + 64, :], ldv[a:a + 1, :].broadcast_to([64, 16]), writes=["DT%d" % a],
                           allow_slow_non_contiguous=True)
                V(lambda e: e.memset(BDr[:], 0.0), [], ["BDr"])
                V(lambda e: e.memset(BDi[:], 0.0), [], ["BDi"])
                for (src, dst, key) in ((b_re, BDr, "BDr"), (b_im, BDi, "BDi")):
                    sv = src.rearrange("(j a p) h -> a p j h", a=2, p=64)
                    for a in range(2):
                        cx.dma("sp" if a == 0 else "act", dst[64 * a:64 * a + 64, :, 16 * a:16 * a + 16], sv[a], reads=[], writes=[key],
                               allow_slow_non_contiguous=True)
                for ri, src in enumerate((c_re, c_im)):
                    sv = src.rearrange("(jj r) p -> r jj p", r=128)
                    for dup in range(2):
                        cx.dma("sp" if dup == 0 else "act", cn[:, ri, :, dup, :], sv, writes=["cn"])
                cx.dma("sp", dsk[:, :], d_skip.rearrange("(jj r) -> r jj", r=128), writes=["dsk"], allow_slow_non_contiguous=True)

                if STG < 2:
                    cx.enabled = False
                A(lambda e: e.activation(out=DT[:], in_=DT[:], func=AF.Exp), ["DT0", "DT1"], ["DT"])
                TT("dve", dtar[:], DT[:], AR[:], ALU.mult, ["DT", "AR"], ["dtar"])
                TT("dve", dtai[:], DT[:], AI[:], ALU.mult, ["DT", "AI"], ["dtai"])
                for k in range(17):
                    V(lambda e, k=k: e.tensor_scalar(out=ARG[:, :, k], in0=dtar[:], scalar1=float(k), scalar2=None, op0=ALU.mult), ["dtar"], ["ARG"])
                    V(lambda e, k=k: e.tensor_scalar(out=ANG[:, :, k], in0=dtai[:], scalar1=float(k), scalar2=None, op0=ALU.mult), ["dtai"], ["ANG"])
                A(lambda e: e.activation(out=MAG[:], in_=ARG[:], func=AF.Exp), ["ARG"], ["MAG"])
                cis(p2, ANG[:].rearrange("p a b -> p (a b)"), 16 * 17, COS[:].rearrange("p a b -> p (a b)"),
                    SIN[:].rearrange("p a b -> p (a b)"), "c1", "ANG", "CS")
                TT("dve", Fr[:], MAG[:], COS[:], ALU.mult, ["MAG", "CSc"], ["Fr"])
                TT("dve", Fi[:], MAG[:], SIN[:], ALU.mult, ["MAG", "CSs"], ["Fi"])
                V(lambda e: e.tensor_scalar(out=t1[:], in0=Fr[:, :, 1], scalar1=-1.0, scalar2=None, op0=ALU.add), ["Fr"], ["t1"])
                TT("dve", t2[:], AR[:], AR[:], ALU.mult, ["AR"], ["t2"])
                TT("dve", t3[:], AI[:], AI[:], ALU.mult, ["AI"], ["t3"])
                TT("dve", t2[:], t2[:], t3[:], ALU.add, ["t2", "t3"], ["t2"])
                V(lambda e: e.reciprocal(out=t2[:], in_=t2[:]), ["t2"], ["t2"])
                TT("dve", cor[:], t1[:], AR[:], ALU.mult, ["t1", "AR"], ["cor"])
                TT("dve", t3[:], Fi[:, :, 1], AI[:], ALU.mult, ["Fi", "AI"], ["t3"])
                TT("dve", cor[:], cor[:], t3[:], ALU.add, ["cor", "t3"], ["cor"])
                TT("dve", cor[:], cor[:], t2[:], ALU.mult, ["cor", "t2"], ["cor"])
                TT("dve", coi[:], Fi[:, :, 1], AR[:], ALU.mult, ["Fi", "AR"], ["coi"])
                TT("dve", t3[:], t1[:], AI[:], ALU.mult, ["t1", "AI"], ["t3"])
                TT("dve", coi[:], coi[:], t3[:], ALU.subtract, ["coi", "t3"], ["coi"])
                TT("dve", coi[:], coi[:], t2[:], ALU.mult, ["coi", "t2"], ["coi"])
                corb = cor[:, :].unsqueeze(2).broadcast_to([128, 16, 16])
                coib = coi[:, :].unsqueeze(2).broadcast_to([128, 16, 16])
                TT("dve", FBr[:], Fr[:, :, 0:16], corb, ALU.mult, ["Fr", "cor"], ["FBr"])
                TT("dve", f1[:], Fi[:, :, 0:16], coib, ALU.mult, ["Fi", "coi"], ["f1"])
                TT("dve", FBr[:], FBr[:], f1[:], ALU.subtract, ["FBr", "f1"], ["FBr"])
                TT("dve", FBi[:], Fr[:, :, 0:16], coib, ALU.mult, ["Fr", "coi"], ["FBi"])
                TT("dve", f1[:], Fi[:, :, 0:16], corb, ALU.mult, ["Fi", "cor"], ["f1"])
                TT("dve", FBi[:], FBi[:], f1[:], ALU.add, ["FBi", "f1"], ["FBi"])

                if STG < 3:
                    cx.enabled = False
                V(lambda e: e.memset(CDr[:], 0.0), [], ["CDr"])
                V(lambda e: e.memset(CDi[:], 0.0), [], ["CDi"])
                for jj in range(4):
                    for ri, dst, key in ((0, CDr, "CDr"), (1, CDi, "CDi")):
                        P(lambda pe, jj=jj, ri=ri: pe.transpose(out=pC[:, ri, :], in_=cn[:, ri, jj, :, :].rearrange("p d q -> p (d q)"),
                                                                 identity=ident_f[:, :]), ["cn", "ident_f"], ["pC"])
                        pv = pC[:, ri, :].rearrange("p (m a h) -> p m a h", m=4, a=2)
                        for a in range(2):
                            V(lambda e, jj=jj, a=a, dst=dst, pv=pv: e.tensor_copy(
                                out=dst[64 * a:64 * a + 64, 4 * jj:4 * jj + 4, 16 * a:16 * a + 16], in_=pv[64 * a:64 * a + 64, :, a, :]),
                                [], [key, "pC"])
                if STG < 3.2:
                    cx.enabled = False
                V(lambda e: e.tensor_copy(out=CDrb[:], in_=CDr[:]), ["CDr"], ["CDrb"])
                V(lambda e: e.tensor_scalar(out=CDnb[:], in0=CDi[:], scalar1=-1.0, scalar2=None, op0=ALU.mult), ["CDi"], ["CDnb"])

                if STG < 3.5:
                    cx.enabled = False
                for ri, dst, key in ((0, h0r, "h0r"), (1, h0i, "h0i")):
                    cx.dma("sp", h0n[:, :], (s_re if ri == 0 else s_im)[:, :], writes=["h0n"])
                    for j in range(16):
                        P(lambda pe, ri=ri, j=j: pe.transpose(out=pH[:, ri, j, :], in_=h0n[:, 128 * j:128 * j + 128],
                                                               identity=ident_f[0:SB, 0:SB]), ["h0n", "ident_f"], ["pH"])
                    V(lambda e, ri=ri, dst=dst: e.tensor_copy(out=dst[:], in_=pH[:, ri, :, :]), [], [key, "pH"])

                if STG < 4:
                    cx.enabled = False
                V(lambda e: e.memset(Kbd[:], 0.0), [], ["Kbd"])
                for jj in range(4):
                    js = slice(4 * jj, 4 * jj + 4)
                    fbr = FBr[:, js, :].unsqueeze(3).broadcast_to([128, 4, 16, 32])
                    fbi = FBi[:, js, :].unsqueeze(3).broadcast_to([128, 4, 16, 32])
                    bdr = BDr[:, js, :].unsqueeze(2).broadcast_to([128, 4, 16, 32])
                    bdi = BDi[:, js, :].unsqueeze(2).broadcast_to([128, 4, 16, 32])
                    TT("dve", tA[:], fbr, bdr, ALU.mult, ["FBr", "BDr"], ["tA"])
                    TT("pool", tB[:], fbi, bdi, ALU.mult, ["FBi", "BDi"], ["tB"])
                    TT("dve", LBr[:], tA[:], tB[:], ALU.subtract, ["tA", "tB"], ["LBr"])
                    TT("dve", tA[:], fbr, bdi, ALU.mult, ["FBr", "BDi", "LBr"], ["tA"])
                    TT("pool", tB[:], fbi, bdr, ALU.mult, ["FBi", "BDr", "LBr"], ["tB"])
                    TT("dve", LBi[:], tA[:], tB[:], ALU.add, ["tA", "tB"], ["LBi"])
                    for ri, src, dst, skey, dkey in ((0, LBr, T1r, "LBr", "T1r"), (1, LBi, T1i, "LBi", "T1i")):
                        for k in range(16):
                            for m in range(4):
                                P(lambda pe, ri=ri, k=k, m=m, src=src: pe.transpose(
                                    out=pT[32 * m:32 * m + 32, ri, k, :], in_=src[:, m, k, :], identity=ident_b[:, :], tile_position=(0, 32 * m)),
                                    [skey, "ident_b"], ["pT%d" % ri])
                        A(lambda e, ri=ri, dst=dst, jj=jj: e.activation(out=dst[:, jj, :, :], in_=pT[:, ri, :, :], func=AF.Copy),
                          ["pT%d" % ri], [dkey])
                    for k in range(16):
                        for m in range(4):
                            P(lambda pe, k=k, m=m, jj=jj: pe.matmul(pK[32 * m:32 * m + 32, k, :], lhsT=LBr[:, m, k, :],
                                                                    rhs=CDrb[:, 4 * jj + m, :], start=True, stop=False, tile_position=(0, 32 * m)),
                              ["LBr", "CDrb"], ["pK"])
                            P(lambda pe, k=k, m=m, jj=jj: pe.matmul(pK[32 * m:32 * m + 32, k, :], lhsT=LBi[:, m, k, :],
                                                                    rhs=CDnb[:, 4 * jj + m, :], start=False, stop=True, tile_position=(0, 32 * m)),
                              ["LBi", "CDnb"], ["pK"])
                    for m in range(4):
                        V(lambda e, m=m, jj=jj: e.tensor_copy(out=Kbd[32 * m:32 * m + 32, jj, :, 32 * m:32 * m + 32],
                                                              in_=pK[32 * m:32 * m + 32, :, :]), ["pK"], ["Kbd"])
                    V(lambda e, jj=jj: e.scalar_tensor_tensor(out=Kbd[:, jj, 0, :], in0=ident_f[:, :], scalar=dsk[:, jj:jj + 1],
                                                              in1=Kbd[:, jj, 0, :], op0=ALU.mult, op1=ALU.add),
                      ["Kbd", "dsk", "ident_f"], ["Kbd"])
                    fr = Fr[:, js, 1:17].unsqueeze(3).broadcast_to([128, 4, 16, 32])
                    fi = Fi[:, js, 1:17].unsqueeze(3).broadcast_to([128, 4, 16, 32])
                    cdr = CDr[:, js, :].unsqueeze(2).broadcast_to([128, 4, 16, 32])
                    cdi = CDi[:, js, :].unsqueeze(2).broadcast_to([128, 4, 16, 32])
                    TT("dve", tA[:], fr, cdr, ALU.mult, ["Fr", "CDr", "LBi"], ["tA"])
                    TT("pool", tB[:], fi, cdi, ALU.mult, ["Fi", "CDi", "LBi"], ["tB"])
                    TT("dve", T2r[:, js, :, :], tA[:], tB[:], ALU.subtract, ["tA", "tB"], ["T2r"])
                    TT("dve", tA[:], fr, cdi, ALU.mult, ["Fr", "CDi", "T2r"], ["tA"])
                    TT("pool", tB[:], fi, cdr, ALU.mult, ["Fi", "CDr", "T2r"], ["tB"])
                    V(lambda e, js=js: e.scalar_tensor_tensor(out=T2n[:, js, :, :], in0=tA[:], scalar=-1.0, in1=tB[:],
                                                              op0=ALU.mult, op1=ALU.subtract), ["tA", "tB"], ["T2n"])
                cx.barrier()

            if STG < 5:
                cx.enabled = False
            with ExitStack() as p3:
                Zr = sb("Zr", [128, 16, 128], F32, p3)
                Zi = sb("Zi", [128, 16, 128], F32, p3)
                Yr = sb("Yr", [128, 16, 128], F32, p3)
                Yi = sb("Yi", [128, 16, 128], F32, p3)
                tq = sb("tq", [128, 16, 128], F32, p3)
                tw = sb("tw", [128, 16, 128], F32, p3)
                Mr = sb("Mr", [128, 16], F32, p3)
                Mi = sb("Mi", [128, 16], F32, p3)
                m1 = sb("m1", [128, 16], F32, p3)
                m2 = sb("m2", [128, 16], F32, p3)
                Hxr = sb("Hxr", [128, 16, 128], BF16, p3)
                Hxi = sb("Hxi", [128, 16, 128], BF16, p3)
                h0rb = sb("h0rb", [128, 16, SB], BF16, p3)
                h0ib = sb("h0ib", [128, 16, SB], BF16, p3)
                hsr = sb("hsr", [128, 16, SB], F32, p3)
                hsi = sb("hsi", [128, 16, SB], F32, p3)
                hsn = sb("hsn", [SB, 2048], F32, p3)
                pZ = ps("pZ", [128, 2, 8, 128], F32, p3)
                pY = ps("pY", [128, 16, 128], F32, p3)

                for half in range(2):
                    for j8 in range(8):
                        j = 8 * half + j8
                        jj, m = j // 4, j % 4
                        for ri, Tt, key in ((0, T1r, "T1r"), (1, T1i, "T1i")):
                            for i in range(16):
                                P(lambda pe, ri=ri, j8=j8, jj=jj, m=m, i=i, Tt=Tt: pe.matmul(
                                    pZ[:, ri, j8, :], lhsT=Tt[32 * m:32 * m + 32, jj, 15 - i, :],
                                    rhs=uT[32 * m:32 * m + 32, jj, i:2048:16], start=(i == 0), stop=(i == 15), tile_position=(32 * m, 0)),
                                    [key, "uT"], ["pZ%d" % ri])
                    hs = slice(8 * half, 8 * half + 8)
                    V(lambda e, hs=hs: e.tensor_copy(out=Zr[:, hs, :], in_=pZ[:, 0, :, :]), ["pZ0"], ["Zr"])
                    A(lambda e, hs=hs: e.activation(out=Zi[:, hs, :], in_=pZ[:, 1, :, :], func=AF.Copy), ["pZ1"], ["Zi"])

                V(lambda e: e.tensor_copy(out=Mr[:], in_=Fr[:, :, 16]), ["Fr"], ["Mr"])
                V(lambda e: e.tensor_copy(out=Mi[:], in_=Fi[:, :, 16]), ["Fi"], ["Mi"])
                src_r, src_i, dst_r, dst_i = Zr, Zi, Yr, Yi
                skr, ski, dkr, dki = "Zr", "Zi", "Yr", "Yi"
                for lvl in range(7):
                    s = 1 << lvl
                    n = 128 - s
                    mrb = Mr[:, :].unsqueeze(2).broadcast_to([128, 16, n])
                    mib = Mi[:, :].unsqueeze(2).broadcast_to([128, 16, n])
                    TT("dve", dst_r[:, :, s:], mrb, src_r[:, :, 0:n], ALU.mult, ["Mr", skr], [dkr])
                    TT("dve", tq[:, :, s:], mib, src_i[:, :, 0:n], ALU.mult, ["Mi", ski], ["tq"])
                    TT("dve", dst_r[:, :, s:], dst_r[:, :, s:], tq[:, :, s:], ALU.subtract, [dkr, "tq"], [dkr])
                    TT("dve", dst_r[:, :, s:], dst_r[:, :, s:], src_r[:, :, s:], ALU.add, [dkr, skr], [dkr])
                    V(lambda e, s=s, dst_r=dst_r, src_r=src_r: e.tensor_copy(out=dst_r[:, :, 0:s], in_=src_r[:, :, 0:s]), [skr], [dkr])
                    TT("pool", dst_i[:, :, s:], mrb, src_i[:, :, 0:n], ALU.mult, ["Mr", ski], [dki])
                    TT("pool", tw[:, :, s:], mib, src_r[:, :, 0:n], ALU.mult, ["Mi", skr], ["tw"])
                    TT("pool", dst_i[:, :, s:], dst_i[:, :, s:], tw[:, :, s:], ALU.add, [dki, "tw"], [dki])
                    TT("pool", dst_i[:, :, s:], dst_i[:, :, s:], src_i[:, :, s:], ALU.add, [dki, ski], [dki])
                    G(lambda e, s=s, dst_i=dst_i, src_i=src_i: e.tensor_copy(out=dst_i[:, :, 0:s], in_=src_i[:, :, 0:s]), [ski], [dki])
                    if lvl < 6:
                        TT("dve", m1[:], Mr[:], Mr[:], ALU.mult, ["Mr"], ["m1"])
                        TT("dve", m2[:], Mi[:], Mi[:], ALU.mult, ["Mi"], ["m2"])
                        TT("dve", m1[:], m1[:], m2[:], ALU.subtract, ["m1", "m2"], ["m1"])
                        TT("dve", m2[:], Mr[:], Mi[:], ALU.mult, ["Mr", "Mi", dkr, dki, "tq", "tw"], ["m2"])
                        V(lambda e: e.tensor_scalar(out=Mi[:], in0=m2[:], scalar1=2.0, scalar2=None, op0=ALU.mult), ["m2", dkr, dki, "tq", "tw"], ["Mi"])
                        V(lambda e: e.tensor_copy(out=Mr[:], in_=m1[:]), ["m1", dkr, dki, "tq", "tw"], ["Mr"])
                    src_r, src_i, dst_r, dst_i = dst_r, dst_i, src_r, src_i
                    skr, ski, dkr, dki = dkr, dki, skr, ski
                Hr_, Hi_, hkr, hki = src_r, src_i, skr, ski
                cx.dma("sp", hr_p.rearrange("(j q) -> q j", q=128), Hr_[:, :, 127], reads=[hkr], writes=["o_hrp"], allow_slow_non_contiguous=True)
                cx.dma("sp", hi_p.rearrange("(j q) -> q j", q=128), Hi_[:, :, 127], reads=[hki], writes=["o_hip"], allow_slow_non_contiguous=True)
                V(lambda e: e.memset(Hxr[:, :, 0:1], 0.0), [], ["Hxr"])
                V(lambda e: e.memset(Hxi[:, :, 0:1], 0.0), [], ["Hxi"])
                V(lambda e: e.tensor_copy(out=Hxr[:, :, 1:128], in_=Hr_[:, :, 0:127]), [hkr, "Hxr"], ["Hxr"])
                V(lambda e: e.tensor_copy(out=Hxi[:, :, 1:128], in_=Hi_[:, :, 0:127]), [hki, "Hxi"], ["Hxi"])
                V(lambda e: e.tensor_copy(out=h0rb[:], in_=h0r[:]), ["h0r"], ["h0rb"])
                V(lambda e: e.tensor_copy(out=h0ib[:], in_=h0i[:]), ["h0i"], ["h0ib"])

                for jj in range(4):
                    uv = uT[:, jj, 0:2048].rearrange("p (c i) -> p i c", i=16)
                    for bk in range(4):
                        for tau in range(0, 4 * bk + 4):
                            i0 = max(4 * bk, tau)
                            i1 = 4 * bk + 4
                            P(lambda pe, jj=jj, tau=tau, i0=i0, i1=i1, uv=uv: pe.matmul(
                                pY[:, i0:i1, :], lhsT=Kbd[:, jj, tau, :], rhs=uv[:, i0 - tau:i1 - tau, :],
                                start=(tau == 0), stop=False, skip_group_check=True),
                                ["Kbd", "uT"], ["pY"])
                    for i in range(16):
                        for m in range(4):
                            j = 4 * jj + m
                            P(lambda pe, i=i, m=m, j=j: pe.matmul(pY[32 * m:32 * m + 32, i, :], lhsT=T2r[:, j, i, :], rhs=Hxr[:, j, :],
                                                                  start=False, stop=False, skip_group_check=True, tile_position=(0, 32 * m)), ["T2r", "Hxr"], ["pY"])
                            P(lambda pe, i=i, m=m, j=j: pe.matmul(pY[32 * m:32 * m + 32, i, :], lhsT=T2n[:, j, i, :], rhs=Hxi[:, j, :],
                                                                  start=False, stop=True, skip_group_check=True, tile_position=(0, 32 * m)), ["T2n", "Hxi"], ["pY"])
                    yv = yssm[:, jj, 0:2048].rearrange("p (c i) -> p i c", i=16)
                    V(lambda e, yv=yv: e.tensor_copy(out=yv[:, 0:8, :], in_=pY[:, 0:8, :]), ["pY"], ["yssm"])
                    A(lambda e, yv=yv: e.activation(out=yv[:, 8:16, :], in_=pY[:, 8:16, :], func=AF.Copy), ["pY"], ["yssm"])

                for jj in range(4):
                    uv = uT[:, jj, 2048:T].rearrange("p (b t) -> p t b", t=4)
                    for tau in range(4):
                        P(lambda pe, jj=jj, tau=tau, uv=uv: pe.matmul(
                            pY[:, 0, 0:64].rearrange("p (t b) -> p t b", t=4)[:, tau:4, :], lhsT=Kbd[:, jj, tau, :], rhs=uv[:, 0:4 - tau, :],
                            start=(tau == 0), stop=False, skip_group_check=True), ["Kbd", "uT"], ["pY"])
                    for t in range(4):
                        for m in range(4):
                            j = 4 * jj + m
                            P(lambda pe, t=t, m=m, j=j: pe.matmul(pY[32 * m:32 * m + 32, 0, 16 * t:16 * t + 16], lhsT=T2r[:, j, t, :], rhs=h0rb[:, j, :],
                                                                  start=False, stop=False, skip_group_check=True, tile_position=(0, 32 * m)), ["T2r", "h0rb"], ["pY"])
                            P(lambda pe, t=t, m=m, j=j: pe.matmul(pY[32 * m:32 * m + 32, 0, 16 * t:16 * t + 16], lhsT=T2n[:, j, t, :], rhs=h0ib[:, j, :],
                                                                  start=False, stop=True, skip_group_check=True, tile_position=(0, 32 * m)), ["T2n", "h0ib"], ["pY"])
                    V(lambda e, jj=jj: e.tensor_copy(out=yssm[:, jj, 2048:T].rearrange("p (b t) -> p t b", t=4),
                                                     in_=pY[:, 0, 0:64].rearrange("p (t b) -> p t b", t=4)), ["pY"], ["yssm"])

                for half in range(2):
                    for j8 in range(8):
                        j = 8 * half + j8
                        jj, m = j // 4, j % 4
                        for ri, Tt, key in ((0, T1r, "T1r"), (1, T1i, "T1i")):
                            for t in range(4):
                                P(lambda pe, ri=ri, j8=j8, jj=jj, m=m, t=t, Tt=Tt: pe.matmul(
                                    pZ[:, ri, j8, 0:SB], lhsT=Tt[32 * m:32 * m + 32, jj, 3 - t, :],
                                    rhs=uT[32 * m:32 * m + 32, jj, 2048 + t:T:4], start=(t == 0), stop=(t == 3), tile_position=(32 * m, 0)),
                                    [key, "uT"], ["pZ%d" % ri])
                    hs = slice(8 * half, 8 * half + 8)
                    f4r = Fr[:, hs, 4:5].broadcast_to([128, 8, SB])
                    f4i = Fi[:, hs, 4:5].broadcast_to([128, 8, SB])
                    TT("dve", tq[:, hs, 0:SB], f4r, h0r[:, hs, :], ALU.mult, ["Fr", "h0r"], ["tq"])
                    TT("dve", hsr[:, hs, :], pZ[:, 0, :, 0:SB], tq[:, hs, 0:SB], ALU.add, ["pZ0", "tq"], ["hsr"])
                    TT("dve", tq[:, hs, 0:SB], f4i, h0i[:, hs, :], ALU.mult, ["Fi", "h0i", "hsr"], ["tq"])
                    TT("dve", hsr[:, hs, :], hsr[:, hs, :], tq[:, hs, 0:SB], ALU.subtract, ["hsr", "tq"], ["hsr"])
                    TT("dve", tw[:, hs, 0:SB], f4r, h0i[:, hs, :], ALU.mult, ["Fr", "h0i"], ["tw"])
                    TT("dve", hsi[:, hs, :], pZ[:, 1, :, 0:SB], tw[:, hs, 0:SB], ALU.add, ["pZ1", "tw"], ["hsi"])
                    TT("dve", tw[:, hs, 0:SB], f4i, h0r[:, hs, :], ALU.mult, ["Fi", "h0r", "hsi"], ["tw"])
                    TT("dve", hsi[:, hs, :], hsi[:, hs, :], tw[:, hs, 0:SB], ALU.add, ["hsi", "tw"], ["hsi"])
                for ri, src, key, dstd in ((0, hsr, "hsr", hr_s), (1, hsi, "hsi", hi_s)):
                    for j in range(16):
                        P(lambda pe, src=src, j=j: pe.transpose(out=pY[0:SB, j, :], in_=src[:, j, :], identity=ident_f[:, :]),
                          [key, "ident_f"], ["pY"])
                    V(lambda e, ri=ri: e.tensor_copy(out=hsn[:, :], in_=pY[0:SB, :, :].rearrange("p a b -> p (a b)")), ["pY"], ["hsn"])
                    cx.dma("sp", dstd[:, :], hsn[:, :], reads=["hsn"], writes=["o_hs%d" % ri])
                cx.barrier()

        pxu.close()

        with ExitStack() as ph:
            wg_st = sb("wg_st", [128, 4, 512], F32, ph)
            wg_b = sb("wg_b", [128, 4, 512], BF16, ph)
            bg = sb("bg", [128, 4], F32, ph)
            gs = sb("gs", [128, 4], F32, ph)
            ga = sb("ga", [128, 4, 512], F32, ph)
            gb = sb("gb", [128, 4, 512], BF16, ph)
            t_a = sb("t_a", [128, 4, 512], F32, ph)
            t_b = sb("t_b", [128, 4, 512], F32, ph)
            sq = sb("sq", [128, 4, 512], BF16, ph)
            rstd = sb("rstd", [128, 512], F32, ph)
            pz = [ps("pz%d" % i, [128, 512], F32, ph) for i in range(4)]
            pss = ps("pss", [128, 512], F32, ph)
            cx.dma("sp", wg_st[:, :, :], w_glu.rearrange("(kc p) c -> p kc c", p=128), writes=["wg_st"])
            cx.dma("act", bg[:, :], b_glu.rearrange("(c p) -> p c", p=128), writes=["bg"], allow_slow_non_contiguous=True)
            cx.dma("act", gs[:, :], g_ssm.rearrange("(c p) -> p c", p=128), writes=["gs"], allow_slow_non_contiguous=True)
            V(lambda e: e.tensor_copy(out=wg_b[:], in_=wg_st[:]), ["wg_st"], ["wg_b"])
            for nt in range(5):
                t0 = nt * 512
                n = 512 if nt < 4 else ST
                yv = yssm[:, :, t0:t0 + n]
                A(lambda e, yv=yv, n=n: e.activation(out=t_a[:, :, 0:n], in_=yv, func=AF.Square), ["yssm"], ["t_a"])
                V(lambda e, n=n: e.tensor_scalar(out=t_a[:, :, 0:n], in0=t_a[:, :, 0:n], scalar1=0.044715, scalar2=1.0, op0=ALU.mult, op1=ALU.add), ["t_a"], ["t_a"])
                TT("dve", t_a[:, :, 0:n], t_a[:, :, 0:n], yv, ALU.mult, ["t_a", "yssm"], ["t_a"])
                A(lambda e, n=n: e.activation(out=t_b[:, :, 0:n], in_=t_a[:, :, 0:n], func=AF.Sigmoid, scale=1.5957691216057308), ["t_a"], ["t_b"])
                TT("dve", ga[:, :, 0:n], t_b[:, :, 0:n], yv, ALU.mult, ["t_b", "yssm"], ["ga"])
                G(lambda e, n=n: e.tensor_copy(out=gb[:, :, 0:n], in_=ga[:, :, 0:n]), ["ga"], ["gb"])
                for oc in range(4):
                    for kc in range(4):
                        P(lambda pe, oc=oc, kc=kc, n=n: pe.matmul(pz[oc][:, 0:n], lhsT=wg_b[:, kc, oc * 128:(oc + 1) * 128], rhs=gb[:, kc, 0:n],
                                                                  start=(kc == 0), stop=(kc == 3)), ["wg_b", "gb"], ["pz%d" % oc])
                    A(lambda e, oc=oc, n=n: e.activation(out=t_b[:, oc, 0:n], in_=pz[oc][:, 0:n], func=AF.Sigmoid, bias=bg[:, oc:oc + 1], scale=1.0),
                      ["bg"], ["t_b%d" % oc, "pz%d" % oc])
                    TT("dve", t_a[:, oc, 0:n], ga[:, oc, 0:n], t_b[:, oc, 0:n], ALU.mult, ["ga", "t_b%d" % oc], ["t_a%d" % oc])
                    A(lambda e, oc=oc, n=n: e.activation(out=sq[:, oc, 0:n], in_=t_a[:, oc, 0:n], func=AF.Square), ["t_a%d" % oc], ["sq%d" % oc])
                for oc in range(4):
                    P(lambda pe, oc=oc, n=n: pe.matmul(pss[:, 0:n], lhsT=ones_b[:, :], rhs=sq[:, oc, 0:n], start=(oc == 0), stop=(oc == 3)),
                      ["ones_b", "sq%d" % oc], ["pss"])
                A(lambda e, n=n: e.activation(out=rstd[:, 0:n], in_=pss[:, 0:n], func=AF.Sqrt, bias=eps_t[:, 0:1], scale=1.0 / 512.0), ["eps_t"], ["rstd", "pss"])
                V(lambda e, n=n: e.reciprocal(out=rstd[:, 0:n], in_=rstd[:, 0:n]), ["rstd"], ["rstd"])
                for oc in range(4):
                    V(lambda e, oc=oc, n=n, t0=t0: e.scalar_tensor_tensor(out=yssm[:, oc, t0:t0 + n], in0=t_a[:, oc, 0:n], scalar=gs[:, oc:oc + 1],
                                                                          in1=rstd[:, 0:n], op0=ALU.mult, op1=ALU.mult),
                      ["t_a%d" % oc, "gs", "rstd"], ["yssm"])
                V(lambda e: e.memset(rstd[:, 0:1], 0.0), ["t_a0", "t_a1", "t_a2", "t_a3", "t_b0", "t_b1", "t_b2", "t_b3", "sq0", "sq1", "sq2", "sq3", "yssm"],
                  ["t_a", "t_b", "rstd"])
            cx.barrier()

        attT = sb("attT", [128, 4, T], BF16)
        pqk = ExitStack()
        qT = sb("qT", [128, 4, T], BF16, pqk)
        kT = sb("kT", [128, 4, T], BF16, pqk)
        px = ExitStack()
        xT = sb("xT2", [128, 8, T], BF16, px)
        phase1(xT)
        phase2(xT, [0, 1, 2], qT, kT, None)
        with ExitStack() as ph:
            memT = sb("memT", [128, 8, NMEM], BF16, ph)
            phase1(memT, [(memp[0:128, :], 128), (memp[128:256, :], 128)])
            wst = [sb("mwst%d" % i, [128, 8, 512], F32, ph) for i in range(2)]
            wbf = [sb("mwbf%d" % i, [128, 8, 512], BF16, ph) for i in range(2)]
            ost = [sb("most%d" % i, [128, 512], F32, ph) for i in range(2)]
            pp = [ps("mpp%d" % i, [128, 512], F32, ph) for i in range(2)]
            n = 0
            for wi, (wd, od) in enumerate(((w_mk, mk_p), (w_mv, mv_p))):
                w_v = wd.rearrange("(kc p) c -> p kc c", p=128)
                for cb in range(2):
                    i = n % 2
                    n += 1
                    cx.dma("sp" if i == 0 else "act", wst[i][:, :, :], w_v[:, :, cb * 512:(cb + 1) * 512], writes=["mwst%d" % i])
                    cx.op("pool", lambda e, i=i: e.tensor_copy(out=wbf[i][:, 0:4, :], in_=wst[i][:, 0:4, :]), reads=["mwst%d" % i], writes=["mwbf%d_a" % i])
                    cx.op("dve", lambda e, i=i: e.tensor_copy(out=wbf[i][:, 4:8, :], in_=wst[i][:, 4:8, :]), reads=["mwst%d" % i], writes=["mwbf%d_b" % i])
                    for tt in range(2):
                        p = tt
                        for kc in range(8):
                            cx.op("pe", lambda pe, p=p, i=i, kc=kc, tt=tt: pe.matmul(
                                pp[p][:, :], lhsT=memT[:, kc, tt * 128:(tt + 1) * 128], rhs=wbf[i][:, kc, :], start=(kc == 0), stop=(kc == 7)),
                                reads=["mwbf%d_a" % i, "mwbf%d_b" % i, "xT"], writes=["mpp%d" % p])
                        if tt == 0:
                            cx.op("act", lambda e, p=p: e.activation(out=ost[p][:, :], in_=pp[p][:, :], func=AF.Copy), reads=[], writes=["most%d" % p, "mpp%d" % p])
                        else:
                            cx.op("dve", lambda e, p=p: e.tensor_copy(out=ost[p][:, :], in_=pp[p][:, :]), reads=[], writes=["most%d" % p, "mpp%d" % p])
                        cx.dma("sp", od[tt * 128:(tt + 1) * 128, cb * 512:(cb + 1) * 512], ost[p][:, :], reads=["most%d" % p], writes=["o_mem"])
            cx.barrier()
        px.close()
        with ExitStack() as ph:
            Vall = sb("Vall", [128, 48, 8, 128], BF16, ph)
            vst = [sb("vst%d" % i, [128, 512], F32, ph) for i in range(3)]
            Mpc = sb("Mpc", [128, 4, 128], BF16, ph)
            M16 = sb("M16", [128, 4, 32], BF16, ph)
            mi = sb("mi", [128, 128], I32, ph)
            mf = sb("mf", [128, 128], F32, ph)
            Pt = [sb("Pt%d" % i, [128, 512], BF16, ph) for i in range(4)]
            oatt = sb("oatt", [128, 4, 512], F32, ph)
            osq = sb("osq", [128, 4, 512], BF16, ph)
            rec = sb("rec", [128, 512], F32, ph)
            rstd = sb("a_rstd", [128, 512], F32, ph)
            gatt = sb("gatt", [128, 4], F32, ph)
            Sb = [ps("Sb%d" % i, [128, 512], F32, ph) for i in range(4)]
            acc = [ps("acc%d" % i, [128, 512], F32, ph) for i in range(2)]
            pss = ps("a_pss", [128, 512], F32, ph)

            cx.dma("act", gatt[:, :], g_att.rearrange("(c p) -> p c", p=128), writes=["gatt"], allow_slow_non_contiguous=True)
            G(lambda g: g.iota(mi[:], pattern=[[1, 128]], base=0, channel_multiplier=-1), [], ["mi"])
            V(lambda e: e.tensor_copy(out=mf[:], in_=mi[:]), ["mi"], ["mf"])
            for sl in range(4):
                V(lambda e, sl=sl: e.tensor_scalar(out=Mpc[:, sl, :], in0=mf[:], scalar1=0.0, scalar2=None,
                                                   op0=(ALU.is_le if sl % 2 == 0 else ALU.is_ge)), ["mf"], ["Mpc"])
            for n in range(4):
                V(lambda e, n=n: e.tensor_scalar(out=M16[:, n, :], in0=mf[:, 32 * n:32 * n + 32], scalar1=0.0, scalar2=None, op0=ALU.is_ge),
                  ["mf"], ["M16"])
            V(lambda e: e.memset(Vall[:, 0:24, :, :], 1.0), [], ["Vall"])
            G(lambda e: e.memset(Vall[:, 24:48, :, :], 1.0), [], ["Vall2"])
            vorder = [0, 1, 2, 3, 16, 17, 18, 19] + list(range(32, 48)) + [4, 5, 6, 7, 20, 21, 22, 23, 8, 9, 10, 11, 24, 25, 26, 27, 12, 13, 14, 15, 28, 29, 30, 31]
            for vcnt, tid in enumerate(vorder):
                if tid < 16:
                    src = wv_p[128 * tid:128 * tid + 128, :]
                elif tid < 32:
                    n_, r_ = (tid - 16) // 4, (tid - 16) % 4
                    src = wv_p[512 * n_ + r_:512 * n_ + 512:4, :]
                else:
                    src = wv_p[tid - 32:2048:16, :]
                i = vcnt % 3
                cx.dma("sp", vst[i][:, :], src, writes=["vst%d" % i])
                vk = "Vall" if tid < 24 else "Vall2"
                sv = vst[i][:, :].rearrange("p (a e d) -> p a e d", a=4, e=2)
                dv = Vall[:, tid, :, :].rearrange("p (a e) c -> p a e c", e=2)
                cx.op("pool" if tid % 2 == 0 else "dve", lambda e, sv=sv, dv=dv: e.tensor_copy(out=dv[:, :, 0, 0:64], in_=sv[:, :, 0, :]),
                      reads=["vst%d" % i, vk], writes=["V%d" % tid])
                cx.op("dve" if tid % 2 == 0 else "act", (lambda e, sv=sv, dv=dv: e.tensor_copy(out=dv[:, :, 1, 64:128], in_=sv[:, :, 1, :])) if tid % 2 == 0 else
                      (lambda e, sv=sv, dv=dv: e.activation(out=dv[:, :, 1, 64:128], in_=sv[:, :, 1, :], func=AF.Copy)),
                      reads=["vst%d" % i, vk, "V%d" % tid], writes=["V%d" % tid])
            VK = ["Vall", "Vall2"]

            banks = []
            aidx = [0]
            for n in range(4):
                for hp in range(4):
                    for e_ in range(2):
                        h = 2 * hp + e_
                        pb = 64 * e_
                        ai = aidx[0] % 2
                        aidx[0] += 1
                        hb = []
                        for half in range(2):
                            slots, pvs = [], []
                            for b2 in range(2):
                                qb = 4 * n + 2 * half + b2
                                qap = qT[pb:pb + 64, hp, 128 * qb:128 * qb + 128]
                                for kind in range(2):
                                    kt = qb - 1 + kind
                                    if kt < 0:
                                        continue
                                    sl = 2 * b2 + kind
                                    slots.append((128 * sl, 128, kT[pb:pb + 64, hp, 128 * kt:128 * kt + 128], qap))
                                    pvs.append((slice((qb - 4 * n) * 128, (qb - 4 * n) * 128 + 128), kt, slice(128 * sl, 128 * sl + 128)))
                            hb.append(dict(slots=slots, pvs=pvs, Kn=128, mask=0))
                        for half in range(2):
                            slots, pvs = [], []
                            for b2 in range(2):
                                r4 = 2 * half + b2
                                qap = qT[pb:pb + 64, hp, 512 * n + r4:512 * n + 512:4]
                                for kind in range(2):
                                    kn = n - 1 + kind
                                    if kn < 0:
                                        continue
                                    sl = 2 * b2 + kind
                                    slots.append((128 * sl, 128, kT[pb:pb + 64, hp, 512 * kn + r4:512 * kn + 512:4], qap))
                                    pvs.append((slice(r4, 512, 4), 16 + 4 * kn + r4, slice(128 * sl, 128 * sl + 128)))
                            hb.append(dict(slots=slots, pvs=pvs, Kn=128, mask=0))
                        Kn = 32 * (n + 1)
                        slots, pvs = [], []
                        for r in range(16):
                            slots.append((32 * r, 32, kT[pb:pb + 64, hp, r:16 * Kn:16], qT[pb:pb + 64, hp, 512 * n + r:512 * n + 512:16]))
                            pvs.append((slice(r, 512, 16), 32 + r, slice(32 * r, 32 * r + 32)))
                        hb.append(dict(slots=slots, pvs=pvs, Kn=Kn, mask=1))
                        for bi_, bk in enumerate(hb):
                            bk.update(n=n, hp=hp, e_=e_, h=h, ai=ai, first=(bi_ == 0), last=(bi_ == len(hb) - 1))
                            banks.append(bk)

            def emit_scores(k):
                bk = banks[k]
                si = k % 4
                Kn = bk["Kn"]
                for (c0, ncol, kap, qap) in bk["slots"]:
                    P(lambda pe, si=si, c0=c0, ncol=ncol, kap=kap, qap=qap, Kn=Kn: pe.matmul(Sb[si][0:Kn, c0:c0 + ncol], lhsT=kap, rhs=qap, start=True, stop=True),
                      ["qT", "kT"], ["Sb%d" % si])
                A(lambda e: e.activation(out=Pt[si][0:Kn, :], in_=Sb[si][0:Kn, :], func=AF.Exp, scale=0.125), [], ["Pt%d" % si, "Sb%d" % si])
                if bk["mask"] == 0:
                    pv_, m_ = Pt[si][0:Kn, :], Mpc[:, :, :].rearrange("p a b -> p (a b)")
                else:
                    pv_ = Pt[si][0:Kn, :].rearrange("p (a b) -> p a b", a=16)
                    m_ = M16[0:Kn, bk["n"], :].unsqueeze(1).broadcast_to([Kn, 16, 32])
                V(lambda e: e.tensor_tensor(out=pv_, in0=pv_, in1=m_, op=ALU.mult), ["Mpc", "M16"], ["Pt%d" % si])

            def emit_pv(k):
                bk = banks[k]
                si = k % 4
                Kn = bk["Kn"]
                A_ = acc[bk["ai"]]
                ak = "acc%d" % bk["ai"]
                h, hp, e_, n = bk["h"], bk["hp"], bk["e_"], bk["n"]
                for j_, (cols, vid, pcols) in enumerate(bk["pvs"]):
                    st = bk["first"] and j_ == 0
                    P(lambda pe, cols=cols, vid=vid, pcols=pcols, st=st: pe.matmul(A_[:, cols], lhsT=Vall[0:Kn, vid, h, :], rhs=Pt[si][0:Kn, pcols],
                                                                                 start=st, stop=False, skip_group_check=True), ["V%d" % vid, "Pt%d" % si], [ak])
                if bk["last"]:
                    vo, do = (0, 64) if e_ == 0 else (64, 0)
                    V(lambda e: e.reciprocal(out=rec[do:do + 64, :], in_=A_[do:do + 64, :]), [], ["rec", ak])
                    V(lambda e: e.tensor_tensor(out=oatt[vo:vo + 64, hp, :], in0=A_[vo:vo + 64, :], in1=rec[do:do + 64, :], op=ALU.mult),
                      ["rec"], ["oatt%d" % hp, ak])
                    if hp == 3 and e_ == 1:
                        for hp2 in range(4):
                            A(lambda e, hp2=hp2: e.activation(out=osq[:, hp2, :], in_=oatt[:, hp2, :], func=AF.Square), ["oatt%d" % hp2], ["osq%d" % hp2])
                        for hp2 in range(4):
                            P(lambda pe, hp2=hp2: pe.matmul(pss[:, :], lhsT=ones_b[:, :], rhs=osq[:, hp2, :], start=(hp2 == 0), stop=(hp2 == 3)),
                              ["ones_b", "osq%d" % hp2], ["a_pss"])
                        A(lambda e: e.activation(out=rstd[:, :], in_=pss[:, :], func=AF.Sqrt, bias=eps_t[:, 0:1], scale=1.0 / 512.0), ["eps_t"], ["a_rstd", "a_pss"])
                        V(lambda e: e.reciprocal(out=rstd[:, :], in_=rstd[:, :]), ["a_rstd"], ["a_rstd"])
                        for hp2 in range(4):
                            V(lambda e, hp2=hp2: e.scalar_tensor_tensor(out=attT[:, hp2, 512 * n:512 * n + 512], in0=oatt[:, hp2, :], scalar=gatt[:, hp2:hp2 + 1],
                                                                        in1=rstd[:, :], op0=ALU.mult, op1=ALU.mult), ["oatt%d" % hp2, "gatt", "a_rstd"], ["attT"])

            NBK = len(banks)
            LOOK = 2
            for k in range(NBK + LOOK):
                if k < NBK:
                    emit_scores(k)
                if k >= LOOK:
                    emit_pv(k - LOOK)
            cx.barrier()

        with ExitStack() as ph:
            kst = [sb("kst%d" % i, [128, 512], F32, ph) for i in range(6)]
            vst = [sb("svst%d" % i, [128, 512], F32, ph) for i in range(6)]
            kTs = [sb("kTs%d" % i, [128, 4, 8, 128], BF16, ph) for i in range(2)]
            Vs = [sb("Vs%d" % i, [128, 8, 8, 64], BF16, ph) for i in range(2)]
            vnst = sb("vnst", [4, SB, 512], F32, ph)
            Vn = sb("Vn", [4, SB, 8, 64], BF16, ph)
            mi2 = sb("mi2", [128, 4], I32, ph)
            ma2 = sb("ma2", [128, 4], I32, ph)
            mf2 = sb("mf2", [128, 4], F32, ph)
            mg2 = sb("mg2", [128, 4], F32, ph)
            mh2 = sb("mh2", [128, 4], F32, ph)
            Msf = sb("Msf", [128, 9, 4], F32, ph)
            Msb = sb("Msb", [128, 2, 9, 4], BF16, ph)
            Ps = [sb("Ps%d" % i, [128, 2, 9, 4], BF16, ph) for i in range(2)]
            oas = sb("oas", [128, 4, ST], F32, ph)
            osq = sb("s_osq", [128, 4, ST], BF16, ph)
            rec = sb("s_rec", [128, 4, 4], F32, ph)
            rstd = sb("s_rstd", [128, ST], F32, ph)
            gatt = sb("s_gatt", [128, 4], F32, ph)
            pTs = [ps("pTs%d" % i, [128, 4, 128], F32, ph) for i in range(2)]
            Sps = [ps("Sps%d" % i, [128, 512], F32, ph) for i in range(2)]
            accs = [ps("accs%d" % i, [128, 512], F32, ph) for i in range(2)]
            pss = ps("s_pss", [128, 512], F32, ph)

            cx.dma("act", gatt[:, :], g_att.rearrange("(c p) -> p c", p=128), writes=["gatt"], allow_slow_non_contiguous=True)
            cx.dma("sp", vnst[:, :, :], wv_s.rearrange("(b t) c -> t b c", t=4), writes=["vnst"])
            V(lambda e: e.tensor_copy(out=Vn[:, :, :, :], in_=vnst[:, :, :].rearrange("p b (h d) -> p b h d", h=8)), ["vnst"], ["Vn"])
            G(lambda g: g.iota(mi2[:], pattern=[[-1, 4]], base=4, channel_multiplier=1), [], ["mi2"])
            V(lambda e: e.tensor_scalar(out=ma2[:], in0=mi2[:], scalar1=3, scalar2=None, op0=ALU.bitwise_and), ["mi2"], ["ma2"])
            V(lambda e: e.tensor_copy(out=mf2[:], in_=ma2[:]), ["ma2"], ["mf2"])
            V(lambda e: e.tensor_scalar(out=mf2[:], in0=mf2[:], scalar1=0.0, scalar2=None, op0=ALU.is_equal), ["mf2"], ["mf2"])
            V(lambda e: e.tensor_copy(out=mg2[:], in_=mi2[:]), ["mi2"], ["mg2"])
            for i in range(3):
                V(lambda e, i=i: e.tensor_copy(out=Msf[:, i, :], in_=mf2[:]), ["mf2"], ["Msf"])
            V(lambda e: e.tensor_scalar(out=mh2[:], in0=mg2[:], scalar1=4.0, scalar2=None, op0=ALU.is_ge), ["mg2"], ["mh2"])
            TT("dve", Msf[:, 3, :], mf2[:], mh2[:], ALU.add, ["mf2", "mh2"], ["Msf"])
            for tp in range(4):
                V(lambda e, tp=tp: e.memset(Msf[:, 4 + tp, :], 0.0), [], ["Msf"])
                V(lambda e, tp=tp: e.memset(Msf[:, 4 + tp, tp:tp + 1], 1.0), [], ["Msf"])
            V(lambda e: e.tensor_scalar(out=mh2[:], in0=mg2[:], scalar1=4.0, scalar2=None, op0=ALU.is_le), ["mg2", "Msf"], ["mh2"])
            V(lambda e: e.tensor_scalar(out=mf2[:], in0=mg2[:], scalar1=4.0, scalar2=2.0, op0=ALU.is_equal, op1=ALU.mult), ["mg2", "Msf"], ["mf2"])
            TT("dve", Msf[:, 8, :], mf2[:], mh2[:], ALU.add, ["mf2", "mh2"], ["Msf"])
            for e_ in range(2):
                V(lambda e, e_=e_: e.tensor_copy(out=Msb[:, e_, :, :], in_=Msf[:, :, :]), ["Msf"], ["Msb"])

            ld = [0]
            for b in range(SB):
                bi = b % 2
                for tile in range(8):
                    rows = slice(1536 + 128 * tile, 1536 + 128 * tile + 128) if tile < 4 else slice(tile - 4, 2048, 16)
                    i = ld[0] % 6
                    ld[0] += 1
                    cx.dma("sp", kst[i][:, :], cwk[b, rows, :], writes=["kst%d" % i])
                    cx.dma("sp", vst[i][:, :], cwv[b, rows, :], writes=["svst%d" % i])
                    pi = ld[0] % 2
                    for hp in range(4):
                        P(lambda pe, i=i, hp=hp, pi=pi: pe.transpose(out=pTs[pi][:, hp, :], in_=kst[i][:, 128 * hp:128 * hp + 128], identity=ident_f[:, :]),
                          ["kst%d" % i, "ident_f"], ["pTs%d" % pi])
                    A(lambda e, pi=pi, bi=bi, tile=tile: e.activation(out=kTs[bi][:, :, tile, :], in_=pTs[pi][:, :, :], func=AF.Copy), [], ["kTs%d" % bi, "pTs%d" % pi])
                    cx.op("pool" if tile % 2 == 0 else "dve", lambda e, i=i, bi=bi, tile=tile: e.tensor_copy(
                        out=Vs[bi][:, tile, :, :], in_=vst[i][:, :].rearrange("p (h d) -> p h d", h=8)), reads=["svst%d" % i], writes=["Vs%d_%d" % (bi, tile % 2)])
                tok = slice(2048 + 4 * b, 2048 + 4 * b + 4)
                ai = b % 2
                def sc_fn(hp, b=b, bi=bi, tok=tok):
                    si = (4 * b + hp) % 2
                    Sv = Sps[si][:, 0:72].rearrange("p (e t q) -> p e t q", e=2, t=9)
                    for e_ in range(2):
                        pb = 64 * e_
                        qap = qT[pb:pb + 64, hp, tok]
                        for tile in range(8):
                            P(lambda pe, Sv=Sv, e_=e_, tile=tile, pb=pb, hp=hp, bi=bi, qap=qap: pe.matmul(
                                Sv[:, e_, tile, :], lhsT=kTs[bi][pb:pb + 64, hp, tile, :], rhs=qap, start=True, stop=True), ["kTs%d" % bi, "qT"], ["Sps%d" % si])
                        P(lambda pe, Sv=Sv, e_=e_, pb=pb, hp=hp, qap=qap, tok=tok: pe.matmul(
                            Sv[0:4, e_, 8, :], lhsT=kT[pb:pb + 64, hp, tok], rhs=qap, start=True, stop=True), ["kT", "qT"], ["Sps%d" % si])
                    A(lambda e, si=si: e.activation(out=Ps[si][:, :, :, :].rearrange("p e t q -> p (e t q)"), in_=Sps[si][:, 0:72], func=AF.Exp, scale=0.125),
                      [], ["Ps%d" % si, "Sps%d" % si])
                    TT("dve", Ps[si][:, :, :, :], Ps[si][:, :, :, :], Msb[:, :, :, :], ALU.mult, ["Msb"], ["Ps%d" % si])

                def pv_fn(hp, b=b, bi=bi, ai=ai):
                    si = (4 * b + hp) % 2
                    Av = accs[ai][:, 0:32].rearrange("p (a e q) -> p a e q", a=4, e=2)
                    for e_ in range(2):
                        h = 2 * hp + e_
                        vo, do = (0, 64) if e_ == 0 else (64, 0)
                        for tile in range(9):
                            if tile < 8:
                                vap, oap, pap = Vs[bi][:, tile, h, :], ones_b[:, 0:64], Ps[si][:, e_, tile, :]
                            else:
                                vap, oap, pap = Vn[0:4, b, h, :], ones_b[0:4, 0:64], Ps[si][0:4, e_, 8, :]
                            P(lambda pe, vap=vap, pap=pap, tile=tile, vo=vo, hp=hp, e_=e_, Av=Av: pe.matmul(
                                Av[vo:vo + 64, hp, e_, :], lhsT=vap, rhs=pap, start=(tile == 0), stop=(tile == 8), skip_group_check=True, tile_position=(0, vo)),
                                ["Vs%d_0" % bi, "Vs%d_1" % bi, "Vn", "Ps%d" % si], ["accs%d" % ai])
                            P(lambda pe, oap=oap, pap=pap, tile=tile, do=do, hp=hp, e_=e_, Av=Av: pe.matmul(
                                Av[do:do + 64, hp, e_, :], lhsT=oap, rhs=pap, start=(tile == 0), stop=(tile == 8), skip_group_check=True, tile_position=(0, do)),
                                ["ones_b", "Ps%d" % si], ["accs%d" % ai])

                sc_fn(0)
                for hp in range(4):
                    if hp + 1 < 4:
                        sc_fn(hp + 1)
                    pv_fn(hp)
                Av = accs[ai][:, 0:32].rearrange("p (a e q) -> p a e q", a=4, e=2)
                for e_ in range(2):
                    vo, do = (0, 64) if e_ == 0 else (64, 0)
                    V(lambda e, Av=Av, do=do, e_=e_: e.reciprocal(out=rec[do:do + 64, :, :], in_=Av[do:do + 64, :, e_, :]), [], ["s_rec", "accs%d" % ai])
                    V(lambda e, Av=Av, do=do, vo=vo, e_=e_, tok=tok: e.tensor_tensor(out=oas[vo:vo + 64, :, 4 * (tok.start - 2048) // 4:4 * (tok.start - 2048) // 4 + 4],
                                                                                      in0=Av[vo:vo + 64, :, e_, :], in1=rec[do:do + 64, :, :], op=ALU.mult),
                      ["s_rec"], ["oas", "accs%d" % ai])
            A(lambda e: e.activation(out=osq[:, :, :], in_=oas[:, :, :], func=AF.Square), ["oas"], ["s_osq"])
            for hp in range(4):
                P(lambda pe, hp=hp: pe.matmul(pss[:, 0:ST], lhsT=ones_b[:, :], rhs=osq[:, hp, :], start=(hp == 0), stop=(hp == 3)), ["ones_b", "s_osq"], ["s_pss"])
            A(lambda e: e.activation(out=rstd[:, :], in_=pss[:, 0:ST], func=AF.Sqrt, bias=eps_t[:, 0:1], scale=1.0 / 512.0), ["eps_t"], ["s_rstd", "s_pss"])
            V(lambda e: e.reciprocal(out=rstd[:, :], in_=rstd[:, :]), ["s_rstd"], ["s_rstd"])
            for hp in range(4):
                V(lambda e, hp=hp: e.scalar_tensor_tensor(out=attT[:, hp, 2048:T], in0=oas[:, hp, :], scalar=gatt[:, hp:hp + 1],
                                                          in1=rstd[:, :], op0=ALU.mult, op1=ALU.mult), ["oas", "gatt", "s_rstd"], ["attT"])
            cx.barrier()
        pqk.close()

        TILES = [(128 * i, 128) for i in range(16)] + [(2048, ST)]
        x1s = nc.dram_tensor("x1s", [T, D], F32, kind="Internal").ap()
        x2s = nc.dram_tensor("x2s", [T, D], F32, kind="Internal").ap()

        def xin_rows(t0, rows):
            return x_p[t0:t0 + rows, :] if t0 < 2048 else x_s[:, :]

        def dense_ln(tag, nk, W, lhs_fn, in_keys, prep_fn, xres_fn, g_d, b_d, out_fn, xT_out, xT_key, tiles=None, nbuf=3, wb_pre=None):
            with ExitStack() as ph:
                if wb_pre is not None:
                    wb, wkeys = wb_pre
                else:
                    wb = sb(tag + "wb", [128, nk, 1024], BF16, ph)
                    wst = [sb(tag + "wst%d" % i, [128, 1, 1024], F32, ph) for i in range(2)]
                    Wv = W.rearrange("(kc p) c -> p kc c", p=128)
                    for gi, g0 in enumerate(range(0, nk, 1)):
                        i = gi % 2
                        cx.dma("sp", wst[i][:, :, :], Wv[:, g0:g0 + 1, :], writes=[tag + "wst%d" % i])
                        cx.op("pool" if i == 0 else "dve", lambda e, i=i, g0=g0: e.tensor_copy(out=wb[:, g0:g0 + 1, :], in_=wst[i][:, :, :]),
                              reads=[tag + "wst%d" % i], writes=[tag + "wb%d" % i])
                    wkeys = [tag + "wb0", tag + "wb1"]
                gam = sb(tag + "gam", [128, 1024], F32, ph)
                bet = sb(tag + "bet", [128, 1024], F32, ph)
                cx.dma("sp", gam[:, :], g_d.rearrange("(o c) -> o c", o=1).broadcast_to([128, 1024]), writes=[tag + "gam"])
                cx.dma("act", bet[:, :], b_d.rearrange("(o c) -> o c", o=1).broadcast_to([128, 1024]), writes=[tag + "bet"])
                xr = [sb(tag + "xr%d" % i, [128, 1024], F32, ph) for i in range(nbuf)]
                rr = [sb(tag + "rr%d" % i, [128, 1024], F32, ph) for i in range(nbuf)]
                nn = [sb(tag + "nn%d" % i, [128, 1024], F32, ph) for i in range(nbuf)]
                nb = [sb(tag + "nb%d" % i, [128, 1024], BF16, ph) for i in range(nbuf)] if xT_out is not None else None
                st6 = sb(tag + "st6", [128, 2, 6], F32, ph)
                mv = sb(tag + "mv", [128, 2], F32, ph)
                sd = sb(tag + "sd", [128, 1], F32, ph)
                nbi = sb(tag + "nbi", [128, 1], F32, ph)
                pd = [ps(tag + "pd%d" % i, [128, 1024], F32, ph) for i in range(nbuf)]
                ptr = [ps(tag + "ptr%d" % i, [128, 8, 128], BF16, ph) for i in range(2)]
                pending = []
                for ti, (t0, rows) in enumerate(TILES if tiles is None else tiles):
                    i = ti % nbuf
                    j2 = ti % 2
                    ik = list(in_keys)
                    if prep_fn is not None:
                        ik = ik + prep_fn(ti, t0, rows)
                    cx.dma("sp", xr[i][0:rows, :], xres_fn(t0, rows), writes=[tag + "xr%d" % i])
                    for half in range(2):
                        for kc in range(nk):
                            P(lambda pe, i=i, half=half, kc=kc, t0=t0, rows=rows: pe.matmul(
                                pd[i][0:rows, half * 512:half * 512 + 512], lhsT=lhs_fn(kc, ti, t0, rows), rhs=wb[:, kc, half * 512:half * 512 + 512],
                                start=(kc == 0), stop=(kc == nk - 1)), wkeys + ik, [tag + "pd%d_%d" % (i, half)])
                    for half in range(2):
                        hs = slice(half * 512, half * 512 + 512)
                        V(lambda e, i=i, rows=rows, hs=hs: e.scalar_tensor_tensor(out=rr[i][0:rows, hs], in0=xr[i][0:rows, hs], scalar=ALPHA, in1=pd[i][0:rows, hs],
                                                                                  op0=ALU.mult, op1=ALU.add), [tag + "xr%d" % i], [tag + "rr%d_%d" % (i, half), tag + "pd%d_%d" % (i, half)])
                        V(lambda e, i=i, rows=rows, hs=hs, half=half: e.bn_stats(out=st6[0:rows, half, :], in_=rr[i][0:rows, hs]), [tag + "rr%d_%d" % (i, half)], [tag + "st6_%d" % half])
                    V(lambda e, rows=rows: e.bn_aggr(out=mv[0:rows, :], in_=st6[0:rows, :, :].rearrange("p a b -> p (a b)")), [tag + "st6_0", tag + "st6_1"], [tag + "mv"])
                    A(lambda e, rows=rows: e.activation(out=sd[0:rows, :], in_=mv[0:rows, 1:2], func=AF.Sqrt, bias=eps_t[0:rows, 0:1], scale=1.0), [tag + "mv", "eps_t"], [tag + "sd"])
                    V(lambda e, rows=rows: e.reciprocal(out=sd[0:rows, :], in_=sd[0:rows, :]), [tag + "sd"], [tag + "sd"])
                    V(lambda e, rows=rows: e.scalar_tensor_tensor(out=nbi[0:rows, :], in0=mv[0:rows, 0:1], scalar=-1.0, in1=sd[0:rows, :], op0=ALU.mult, op1=ALU.mult),
                      [tag + "mv", tag + "sd"], [tag + "nbi"])
                    A(lambda e, i=i, rows=rows: e.activation(out=nn[i][0:rows, :], in_=rr[i][0:rows, :], func=AF.Identity, scale=sd[0:rows, 0:1], bias=nbi[0:rows, 0:1]),
                      [tag + "rr%d_0" % i, tag + "rr%d_1" % i, tag + "sd", tag + "nbi"], [tag + "nn%d" % i])
                    TT("dve", nn[i][0:rows, :], nn[i][0:rows, :], gam[0:rows, :], ALU.mult, [tag + "gam", tag + "nn%d" % i], [tag + "nn%d" % i])
                    TT("dve", nn[i][0:rows, :], nn[i][0:rows, :], bet[0:rows, :], ALU.add, [tag + "bet", tag + "nn%d" % i], [tag + "nn%d" % i])
                    def tail_fn(i=i, j2=j2, t0=t0, rows=rows):
                        cx.dma("sp", out_fn(t0, rows), nn[i][0:rows, :], reads=[tag + "nn%d" % i], writes=[tag + "out"])
                        if xT_out is not None:
                            G(lambda e: e.tensor_copy(out=nb[i][0:rows, :], in_=nn[i][0:rows, :]), [tag + "nn%d" % i], [tag + "nb%d" % i])
                            for kc in range(8):
                                P(lambda pe, kc=kc: pe.transpose(out=ptr[j2][:, kc, 0:rows], in_=nb[i][0:rows, kc * 128:(kc + 1) * 128],
                                                                 identity=ident_b[0:rows, 0:rows]), [tag + "nb%d" % i, "ident_b"], [tag + "ptr%d" % j2])
                            A(lambda e: e.activation(out=xT_out[:, :, t0:t0 + rows], in_=ptr[j2][:, :, 0:rows], func=AF.Copy), [], [xT_key, tag + "ptr%d" % j2])
                    if pending:
                        pending.pop(0)()
                    pending.append(tail_fn)
                while pending:
                    pending.pop(0)()
                cx.barrier()

        pxa = ExitStack()
        x1T = sb("x1T", [128, 8, T], BF16, pxa)
        with ExitStack() as ph6:
            mixb = [sb("mixb%d" % i, [128, 4, 128], BF16, ph6) for i in range(2)]

            def prep6(ti, t0, rows):
                i = ti % 2
                G(lambda e: e.tensor_copy(out=mixb[i][:, :, 0:rows], in_=yssm[:, :, t0:t0 + rows]), ["yssm"], ["mixb%d" % i])
                return ["mixb%d" % i]

            def lhs6(kc, ti, t0, rows):
                if kc < 4:
                    return attT[:, kc, t0:t0 + rows]
                return mixb[ti % 2][:, kc - 4, 0:rows]

            dense_ln("l1", 8, w_out, lhs6, ["attT"], prep6, xin_rows, ln_g[0], ln_b[0], lambda t0, rows: x1s[t0:t0 + rows, :], x1T, "x1T")
        if STG < 7:
            cx.enabled = False
        pqm = ExitStack()
        qmT = sb("qmT", [128, 8, T], BF16, pqm)
        QK = ["qm%d" % h for h in range(4)]
        with ExitStack() as ph:
            wqb = sb("wqb", [128, 8, 1024], BF16, ph)
            wqs = [sb("wqs%d" % i, [128, 2, 1024], F32, ph) for i in range(2)]
            Wv = w_mq.rearrange("(kc p) c -> p kc c", p=128)
            for gi in range(4):
                i = gi % 2
                cx.dma("sp" if i == 0 else "act", wqs[i][:, :, :], Wv[:, 2 * gi:2 * gi + 2, :], writes=["wqs%d" % i])
                cx.op("pool" if i == 0 else "dve", lambda e, i=i, gi=gi: e.tensor_copy(out=wqb[:, 2 * gi:2 * gi + 2, :], in_=wqs[i][:, :, :]),
                      reads=["wqs%d" % i], writes=["wqb%d" % i])
            pq = [ps("pq%d" % i, [128, 512], F32, ph) for i in range(4)]
            cnt = 0
            for oc in range(8):
                for nt in range(5):
                    t0 = nt * 512
                    n = 512 if nt < 4 else ST
                    p = cnt % 4
                    cnt += 1
                    for kc in range(8):
                        P(lambda pe, p=p, oc=oc, kc=kc, t0=t0, n=n: pe.matmul(pq[p][:, 0:n], lhsT=wqb[:, kc, oc * 128:(oc + 1) * 128], rhs=x1T[:, kc, t0:t0 + n],
                                                                            start=(kc == 0), stop=(kc == 7)), ["wqb0", "wqb1", "x1T"], ["pq%d" % p])
                    if cnt % 2:
                        A(lambda e, p=p, oc=oc, t0=t0, n=n: e.activation(out=qmT[:, oc, t0:t0 + n], in_=pq[p][:, 0:n], func=AF.Copy), [], ["qm%d" % (oc // 2), "pq%d" % p])
                    else:
                        V(lambda e, p=p, oc=oc, t0=t0, n=n: e.tensor_copy(out=qmT[:, oc, t0:t0 + n], in_=pq[p][:, 0:n]), [], ["qm%d" % (oc // 2), "pq%d" % p])
            cx.barrier()

        with ExitStack() as ph:
            mst = [sb("mst%d" % i, [128, 1024], F32, ph) for i in range(2)]
            mkb = [sb("mkb%d" % i, [128, 1024], BF16, ph) for i in range(2)]
            mkT = [sb("mkT%d" % i, [128, 8, 256], BF16, ph) for i in range(2)]
            mvb = [sb("mvb%d" % i, [128, 2, 1024], BF16, ph) for i in range(2)]
            Pm = [sb("Pm%d" % i, [128, 2, 512], BF16, ph) for i in range(2)]
            recm = sb("recm", [128, 512], F32, ph)
            ptm = [ps("ptm%d" % i, [128, 8, 128], BF16, ph) for i in range(2)]
            Sm = [ps("Sm%d" % i, [128, 512], F32, ph) for i in range(2)]
            Om = [ps("Om%d" % i, [128, 512], F32, ph) for i in range(2)]
            Dm = ps("Dm", [128, 512], F32, ph)
            ldc = [0]

            def load_mem(kd, vd, slot):
                for mt in range(2):
                    i = ldc[0] % 2
                    ldc[0] += 1
                    cx.dma("sp", mst[i][:, :], kd[mt * 128:(mt + 1) * 128, :], writes=["mst%d" % i])
                    V(lambda e, i=i: e.tensor_copy(out=mkb[i][:, :], in_=mst[i][:, :]), ["mst%d" % i], ["mkb%d" % i])
                    for oc in range(8):
                        P(lambda pe, i=i, oc=oc: pe.transpose(out=ptm[i][:, oc, :], in_=mkb[i][:, oc * 128:(oc + 1) * 128], identity=ident_b[:, :]),
                          ["mkb%d" % i, "ident_b"], ["ptm%d" % i])
                    A(lambda e, i=i, mt=mt, slot=slot: e.activation(out=mkT[slot][:, :, mt * 128:(mt + 1) * 128], in_=ptm[i][:, :, :], func=AF.Copy), [], ["mkT%d" % slot, "ptm%d" % i])
                for mt in range(2):
                    i = ldc[0] % 2
                    ldc[0] += 1
                    cx.dma("sp", mst[i][:, :], vd[mt * 128:(mt + 1) * 128, :], writes=["mst%d" % i])
                    V(lambda e, i=i, mt=mt, slot=slot: e.tensor_copy(out=mvb[slot][:, mt, :], in_=mst[i][:, :]), ["mst%d" % i], ["mvb%d" % slot])

            if STG < 7.2:
                cx.enabled = False
            load_mem(mk_p, mv_p, 0)
            it = 0
            for nt in range(4):
                tok = slice(512 * nt, 512 * nt + 512)
                for h in range(4):
                    pi = it % 2
                    it += 1
                    for mt in range(2):
                        for c in range(2):
                            P(lambda pe, mt=mt, c=c, h=h, tok=tok: pe.matmul(Sm[mt][:, :], lhsT=mkT[0][:, 2 * h + c, mt * 128:(mt + 1) * 128], rhs=qmT[:, 2 * h + c, tok],
                                                                         start=(c == 0), stop=(c == 1)), ["mkT0", "qm%d" % h], ["Sm%d" % mt])
                        A(lambda e, mt=mt, pi=pi: e.activation(out=Pm[pi][:, mt, :], in_=Sm[mt][:, :], func=AF.Exp, scale=1.0 / 16.0), [], ["Pm%d_%d" % (pi, mt), "Sm%d" % mt])
                    pk = ["Pm%d_0" % pi, "Pm%d_1" % pi]
                    for c in range(2):
                        for mt in range(2):
                            P(lambda pe, mt=mt, c=c, h=h, pi=pi: pe.matmul(Om[c][:, :], lhsT=mvb[0][:, mt, h * 256 + c * 128:h * 256 + c * 128 + 128], rhs=Pm[pi][:, mt, :],
                                                                        start=(mt == 0), stop=(mt == 1)), ["mvb0"] + pk, ["Om%d" % c])
                    for mt in range(2):
                        P(lambda pe, mt=mt, pi=pi: pe.matmul(Dm[:, :], lhsT=ones_b[:, :], rhs=Pm[pi][:, mt, :], start=(mt == 0), stop=(mt == 1)), ["ones_b"] + pk, ["Dm"])
                    V(lambda e: e.reciprocal(out=recm[:, :], in_=Dm[:, :]), [], ["recm", "Dm"])
                    for c in range(2):
                        V(lambda e, c=c, h=h, tok=tok: e.tensor_tensor(out=qmT[:, 2 * h + c, tok], in0=Om[c][:, :], in1=recm[:, :], op=ALU.mult),
                          ["recm"], ["qm%d" % h, "Om%d" % c])
            cx.barrier()

            if STG < 7.3:
                cx.enabled = False
            for b in range(SB):
                slot = b % 2
                load_mem(cmk[b], cmv[b], slot)
                tok = slice(2048 + 4 * b, 2048 + 4 * b + 4)
                Sv = Sm[slot][:, 0:32].rearrange("p (h m q) -> p h m q", h=4, m=2)
                Ov = Om[slot][:, 0:32].rearrange("p (h c q) -> p h c q", h=4, c=2)
                Dv = Om[slot][:, 32:48].rearrange("p (h q) -> p h q", h=4)
                Pv = Pm[slot][:, 0, 0:32].rearrange("p (h m q) -> p h m q", h=4, m=2)
                for h in range(4):
                    for mt in range(2):
                        for c in range(2):
                            P(lambda pe, mt=mt, c=c, h=h, tok=tok, Sv=Sv, slot=slot: pe.matmul(
                                Sv[:, h, mt, :], lhsT=mkT[slot][:, 2 * h + c, mt * 128:(mt + 1) * 128], rhs=qmT[:, 2 * h + c, tok], start=(c == 0), stop=(c == 1),
                                skip_group_check=True), ["mkT%d" % slot] + QK, ["Sm%d" % slot])
                A(lambda e, slot=slot: e.activation(out=Pm[slot][:, 0, 0:32], in_=Sm[slot][:, 0:32], func=AF.Exp, scale=1.0 / 16.0), [], ["Pm%d_0" % slot, "Sm%d" % slot])
                for h in range(4):
                    for c in range(2):
                        for mt in range(2):
                            P(lambda pe, mt=mt, c=c, h=h, Ov=Ov, Pv=Pv, slot=slot: pe.matmul(
                                Ov[:, h, c, :], lhsT=mvb[slot][:, mt, h * 256 + c * 128:h * 256 + c * 128 + 128], rhs=Pv[:, h, mt, :], start=(mt == 0), stop=(mt == 1),
                                skip_group_check=True), ["mvb%d" % slot, "Pm%d_0" % slot], ["Om%d" % slot])
                    for mt in range(2):
                        P(lambda pe, mt=mt, h=h, Dv=Dv, Pv=Pv: pe.matmul(Dv[:, h, :], lhsT=ones_b[:, :], rhs=Pv[:, h, mt, :], start=(mt == 0), stop=(mt == 1),
                                                                      skip_group_check=True), ["ones_b", "Pm%d_0" % slot], ["Om%d" % slot])
                V(lambda e, Dv=Dv: e.reciprocal(out=recm[:, 0:16].rearrange("p (h q) -> p h q", h=4), in_=Dv), [], ["recm", "Om%d" % slot])
                V(lambda e, Ov=Ov, tok=tok: e.tensor_tensor(out=qmT[:, :, tok].rearrange("p (h c) q -> p h c q", c=2), in0=Ov,
                                                            in1=recm[:, 0:16].rearrange("p (h q) -> p h q", h=4).unsqueeze(2).broadcast_to([128, 4, 2, 4]), op=ALU.mult),
                  ["recm"], QK + ["Om%d" % slot])
            cx.barrier()

        if STG < 7.4:
            cx.enabled = False
        dense_ln("l2", 8, w_mo, lambda kc, ti, t0, rows: qmT[:, kc, t0:t0 + rows], QK, None,
                 lambda t0, rows: x1s[t0:t0 + rows, :], ln_g[1], ln_b[1], lambda t0, rows: x2s[t0:t0 + rows, :], x1T, "x1T")
        pqm.close()


        if STG < 8:
            cx.enabled = False
        x2T = x1T
        GROUPS = [(0, 768, [(0, 512), (512, 256)]), (768, 768, [(768, 512), (1280, 256)]), (1536, 576, [(1536, 512), (2048, ST)])]
        phT = ExitStack()
        wdb = sb("wdb", [128, 22, 1024], BF16, phT)
        hT = sb("hT", [128, 22, 768], BF16, phT)
        with ExitStack() as ph:
            wdst = [sb("wdst%d" % i, [128, 1, 1024], F32, ph) for i in range(3)]
            Wdv = w_down.rearrange("(kc p) c -> p kc c", p=128)
            for g0 in range(22):
                i = g0 % 3
                cx.dma("sp", wdst[i][:, :, :], Wdv[:, g0:g0 + 1, :], writes=["wdst%d" % i])
                cx.op(("act", "dve", "pool")[i], (lambda e, i=i, g0=g0: e.activation(out=wdb[:, g0:g0 + 1, :], in_=wdst[i][:, :, :], func=AF.Copy)) if i == 0 else
                      (lambda e, i=i, g0=g0: e.tensor_copy(out=wdb[:, g0:g0 + 1, :], in_=wdst[i][:, :, :])), reads=["wdst%d" % i], writes=["wdb%d" % i])
            cx.barrier()
        WDK = ["wdb0", "wdb1", "wdb2"]
        wgv = w_gate.rearrange("(kc p) f -> p kc f", p=128)
        wuv = w_up.rearrange("(kc p) f -> p kc f", p=128)
        for gidx, (g0, glen, subs) in enumerate(GROUPS):
            with ExitStack() as ph:
                gst = [sb("gst%d_%d" % (gidx, i), [128, 2, 8, 128], F32, ph) for i in range(2)]
                gbf = [sb("gbf%d_%d" % (gidx, i), [128, 2, 8, 128], BF16, ph) for i in range(2)]
                sg = [sb("sg%d_%d" % (gidx, i), [128, 512], F32, ph) for i in range(4)]
                tg = [sb("tg%d_%d" % (gidx, i), [128, 512], F32, ph) for i in range(4)]
                pg = [ps("pg%d_%d" % (gidx, i), [128, 512], F32, ph) for i in range(4)]
                pu = [ps("pu%d_%d" % (gidx, i), [128, 512], F32, ph) for i in range(4)]
                it = 0
                def wload(fc_):
                    i_ = fc_ % 2
                    cx.dma("sp", gst[i_][:, 0, :, :], wgv[:, :, fc_ * 128:(fc_ + 1) * 128], writes=["gst%d_0" % i_])
                    cx.dma("sp", gst[i_][:, 1, :, :], wuv[:, :, fc_ * 128:(fc_ + 1) * 128], writes=["gst%d_1" % i_])
                wload(0)
                for fc in range(22):
                    i = fc % 2
                    G(lambda e, i=i: e.tensor_copy(out=gbf[i][:, 0, :, :], in_=gst[i][:, 0, :, :]), ["gst%d_0" % i], ["gbf%d_0" % i])
                    V(lambda e, i=i: e.tensor_copy(out=gbf[i][:, 1, :, :], in_=gst[i][:, 1, :, :]), ["gst%d_1" % i], ["gbf%d_1" % i])
                    if fc + 1 < 22:
                        wload(fc + 1)
                    for (t0, n) in subs:
                        p = it % 4
                        it += 1
                        for kc in range(8):
                            P(lambda pe, p=p, i=i, kc=kc, t0=t0, n=n: pe.matmul(pg[p][:, 0:n], lhsT=gbf[i][:, 0, kc, :], rhs=x2T[:, kc, t0:t0 + n],
                                                                             start=(kc == 0), stop=(kc == 7)), ["gbf%d_0" % i, "x1T"], ["pg%d" % p])
                        for kc in range(8):
                            P(lambda pe, p=p, i=i, kc=kc, t0=t0, n=n: pe.matmul(pu[p][:, 0:n], lhsT=gbf[i][:, 1, kc, :], rhs=x2T[:, kc, t0:t0 + n],
                                                                             start=(kc == 0), stop=(kc == 7)), ["gbf%d_1" % i, "x1T"], ["pu%d" % p])
                        A(lambda e, p=p, n=n: e.activation(out=sg[p][:, 0:n], in_=pg[p][:, 0:n], func=AF.Sigmoid), [], ["sg%d" % p, "pg%d" % p])
                        V(lambda e, p=p, n=n: e.tensor_tensor(out=tg[p][:, 0:n], in0=pg[p][:, 0:n], in1=sg[p][:, 0:n], op=ALU.mult), ["sg%d" % p], ["tg%d" % p, "pg%d" % p])
                        V(lambda e, p=p, n=n, t0=t0, fc=fc, g0=g0: e.tensor_tensor(out=hT[:, fc, t0 - g0:t0 - g0 + n], in0=pu[p][:, 0:n], in1=tg[p][:, 0:n], op=ALU.mult),
                          ["tg%d" % p], ["hT", "pu%d" % p])
                cx.barrier()
            gt = [(t0, rows) for (t0, rows) in TILES if g0 <= t0 < g0 + glen]
            dense_ln("l3_%d" % gidx, 22, w_down, lambda kc, ti, t0, rows, g0=g0: hT[:, kc, t0 - g0:t0 - g0 + rows], ["hT"], None,
                     lambda t0, rows: x2s[t0:t0 + rows, :], ln_g[2], ln_b[2],
                     lambda t0, rows: (y_p[t0:t0 + rows, :] if t0 < 2048 else y_s[:, :]), None, None, tiles=gt, nbuf=3, wb_pre=(wdb, []))
        phT.close()
        cx.enabled = True
        pxa.close()
        cx.finish()
    return nc


_NC_CACHE = {}
_DEV = {}


def kernel(**inp):
    f = lambda a: np.ascontiguousarray(np.asarray(a, dtype=np.float32))
    if "nc" not in _NC_CACHE:
        _NC_CACHE["nc"] = build()
    nc = _NC_CACHE["nc"]
    shared = {
        "w_in": f(inp["w_in"][0]), "g_att": f(inp["g_att"][0]), "g_ssm": f(inp["g_ssm"][0]),
        "a_re": f(inp["ssm_a_re"][0]).reshape(2048), "a_im": f(inp["ssm_a_im"][0]).reshape(2048),
        "log_dt": f(inp["ssm_log_dt"][0]),
        "b_re": f(inp["ssm_b_re"][0]).reshape(2048, 16), "b_im": f(inp["ssm_b_im"][0]).reshape(2048, 16),
        "c_re": f(inp["ssm_c_re"][0]).reshape(512, 64), "c_im": f(inp["ssm_c_im"][0]).reshape(512, 64),
        "d_skip": f(inp["ssm_d"][0]).reshape(512), "w_glu": f(inp["w_glu"][0]), "b_glu": f(inp["b_glu"][0]),
        "w_out": f(inp["w_out"][0]),
        "ln1_g": f(inp["ln1_g"][0]), "ln1_b": f(inp["ln1_b"][0]),
        "ln2_g": f(inp["ln2_g"][0]), "ln2_b": f(inp["ln2_b"][0]),
        "ln3_g": f(inp["ln3_g"][0]), "ln3_b": f(inp["ln3_b"][0]),
        "w_mq": f(inp["w_mem_q"][0]), "w_mk": f(inp["w_mem_k"][0]), "w_mv": f(inp["w_mem_v"][0]),
        "w_mo": f(inp["w_mem_o"][0]),
        "w_gate": f(inp["w_gate"][0]), "w_up": f(inp["w_up"][0]), "w_down": f(inp["w_down"][0]),
    }
    in_maps = []
    for c in range(NCORES):
        b0 = c * SB
        m = dict(shared)
        m["x_p"] = f(inp["x_prompt"][c])
        m["x_s"] = f(inp["x_sample"][b0:b0 + SB]).reshape(ST, D)
        m["cwk"] = f(inp["cache_win_k"][0, b0:b0 + SB]).reshape(SB, 2048, DATT)
        m["cwv"] = f(inp["cache_win_v"][0, b0:b0 + SB]).reshape(SB, 2048, DATT)
        m["s_re"] = f(inp["state_ssm_re"][0, b0:b0 + SB]).reshape(SB, 2048)
        m["s_im"] = f(inp["state_ssm_im"][0, b0:b0 + SB]).reshape(SB, 2048)
        m["cmk"] = f(inp["cache_mem_k"][0, b0:b0 + SB]).reshape(SB, NMEM, D)
        m["cmv"] = f(inp["cache_mem_v"][0, b0:b0 + SB]).reshape(SB, NMEM, D)
        m["memp"] = f(inp["mem_prompt"][c])
        in_maps.append(m)
    nrun = _DEV.get("cores", NCORES)
    res = run_bass_kernel_spmd(nc, in_maps[:nrun], core_ids=list(range(nrun)))
    R = list(res.results)
    _DEV["raw"] = R
    while len(R) < NCORES:
        R.append({k: np.zeros_like(np.asarray(v)) for k, v in R[0].items()})
    cat = lambda k: np.stack([np.asarray(R[c][k], dtype=np.float32) for c in range(NCORES)], 0)
    y_p = cat("y_p")
    y_s = cat("y_s").reshape(128, 4, D)
    wk_p = cat("wk_p").reshape(1, 8, S, 8, 64)
    wv_p = cat("wv_p").reshape(1, 8, S, 8, 64)
    wk_s = cat("wk_s").reshape(1, 128, 4, 8, 64)
    wv_s = cat("wv_s").reshape(1, 128, 4, 8, 64)
    hr_p = cat("hr_p").reshape(1, 8, 32, 64)
    hi_p = cat("hi_p").reshape(1, 8, 32, 64)
    hr_s = cat("hr_s").reshape(1, 128, 32, 64)
    hi_s = cat("hi_s").reshape(1, 128, 32, 64)
    mk_p = cat("mk_p").reshape(1, 8, NMEM, 4, 256)
    mv_p = cat("mv_p").reshape(1, 8, NMEM, 4, 256)
    return (y_p, y_s, wk_p, wv_p, wk_s, wv_s, hr_p, hi_p, hr_s, hi_s, mk_p, mv_p)
```

```python
import math
from contextlib import ExitStack
import numpy as np
import concourse.bass as bass
import concourse.mybir as mybir
from concourse.bass_utils import run_bass_kernel_spmd

F32 = mybir.dt.float32
BF16 = mybir.dt.bfloat16
I32 = mybir.dt.int32
AF = mybir.ActivationFunctionType
ALU = mybir.AluOpType

NCORES = 8
D = 1024
S = 2048
SB = 16
ST = 64
T = S + ST
DIN = 2048
DATT = 512
DFF = 2816
NMEM = 256
ALPHA = 2.0 ** 0.25
EPS = 1e-5
STAGE = 99


class _Stop(Exception):
    pass


class Ctx:
    def __init__(self, nc, es):
        self.nc = nc
        self.eng = {"pe": nc.tensor, "act": nc.scalar, "dve": nc.vector, "pool": nc.gpsimd, "sp": nc.sync}
        self.sem = {}
        self.cnt = {}
        for e in self.eng:
            self.sem[e] = es.enter_context(nc.semaphore("s_" + e))
            self.cnt[e] = 0
        self.R = 12
        self.dsem = {}
        self.dcnt = {}
        self.didx = {}
        for q in ("sp", "act", "pool"):
            self.dsem[q] = [es.enter_context(nc.semaphore("d_%s%d" % (q, i))) for i in range(self.R)]
            self.dcnt[q] = [0] * self.R
            self.didx[q] = 0
        self.seen = {e: {} for e in self.eng}
        self.bufs = {}
        self.dma_tokens = []
        self.rr = 0
        self.enabled = True

    def _semof(self, key):
        if isinstance(key, tuple):
            return self.dsem[key[0]][key[1]]
        return self.sem[key]

    def wait(self, e, tok):
        key, val = tok
        if self.seen[e].get(key, 0) >= val:
            return
        self.eng[e].wait_ge(self._semof(key), val)
        self.seen[e][key] = val

    def _deps(self, reads, writes):
        toks = {}
        for k in reads:
            b = self.bufs.get(k)
            if b and b["w"]:
                key, val = b["w"]
                toks[key] = max(toks.get(key, 0), val)
        for k in writes:
            b = self.bufs.get(k)
            if b:
                if b["w"]:
                    key, val = b["w"]
                    toks[key] = max(toks.get(key, 0), val)
                for key, val in b["r"].items():
                    toks[key] = max(toks.get(key, 0), val)
        return toks

    def _record(self, tok, reads, writes):
        for k in reads:
            b = self.bufs.setdefault(k, {"w": None, "r": {}})
            b["r"][tok[0]] = max(b["r"].get(tok[0], 0), tok[1])
        for k in writes:
            self.bufs[k] = {"w": tok, "r": {}}

    def op(self, e, fn, reads=(), writes=()):
        if not self.enabled:
            return None
        deps = self._deps(reads, writes)
        if e == "pe":
            own = 0
            for k in reads:
                b = self.bufs.get(k)
                if b and b["w"] and b["w"][0] == "pe":
                    own = max(own, b["w"][1])
            for k in writes:
                b = self.bufs.get(k)
                if b and b["r"].get("pe"):
                    own = max(own, 0)
            if own == 0:
                deps.pop("pe", None)
            else:
                deps["pe"] = own
        for key, val in deps.items():
            self.wait(e, (key, val))
        inst = fn(self.eng[e])
        self.cnt[e] += 1
        inst.then_inc(self.sem[e], 1)
        tok = (e, self.cnt[e])
        self._record(tok, reads, writes)
        return tok

    def dma(self, q, out, in_, reads=(), writes=(), **kw):
        if not self.enabled:
            return None
        for key, val in self._deps(reads, writes).items():
            self.wait(q, (key, val))
        i = self.didx[q]
        self.didx[q] = (i + 1) % self.R
        inst = self.eng[q].dma_start(out=out, in_=in_, **kw)
        self.dcnt[q][i] += 16
        inst.then_inc(self.dsem[q][i], 16)
        tok = ((q, i), self.dcnt[q][i])
        self.dma_tokens.append(tok)
        self._record(tok, reads, writes)
        return tok

    def barrier(self):
        for e in self.eng:
            if e != "sp" and self.cnt[e] > 0:
                self.wait("sp", (e, self.cnt[e]))
        for tok in self.dma_tokens:
            self.wait("sp", tok)
        self.dma_tokens = []
        inst = self.eng["sp"].nop()
        self.cnt["sp"] += 1
        inst.then_inc(self.sem["sp"], 1)
        for e in self.eng:
            if e != "sp":
                self.wait(e, ("sp", self.cnt["sp"]))
        self.bufs = {}

    def finish(self):
        self.barrier()


class _Catch:
    def __init__(self, cx):
        self.cx = cx

    def __enter__(self):
        return self

    def __exit__(self, et, ev, tb):
        if et is _Stop:
            self.cx.barrier()
            return True
        return False


def build():
    STG = _DEV.get('stage', 99)
    nc = bass.Bass("TRN2", target_bir_lowering=False)

    def din(name, shape):
        return nc.dram_tensor(name, list(shape), F32, kind="ExternalInput").ap()

    def dout(name, shape):
        return nc.dram_tensor(name, list(shape), F32, kind="ExternalOutput").ap()

    x_p = din("x_p", [S, D])
    x_s = din("x_s", [ST, D])
    cwk = din("cwk", [SB, 2048, DATT])
    cwv = din("cwv", [SB, 2048, DATT])
    s_re = din("s_re", [SB, 2048])
    s_im = din("s_im", [SB, 2048])
    cmk = din("cmk", [SB, NMEM, D])
    cmv = din("cmv", [SB, NMEM, D])
    memp = din("memp", [NMEM, D])
    w_in = din("w_in", [D, DIN])
    g_att = din("g_att", [DATT])
    g_ssm = din("g_ssm", [DATT])
    a_re = din("a_re", [2048])
    a_im = din("a_im", [2048])
    log_dt = din("log_dt", [32])
    b_re = din("b_re", [2048, 16])
    b_im = din("b_im", [2048, 16])
    c_re = din("c_re", [512, 64])
    c_im = din("c_im", [512, 64])
    d_skip = din("d_skip", [512])
    w_glu = din("w_glu", [512, 512])
    b_glu = din("b_glu", [512])
    w_out = din("w_out", [D, D])
    ln_g = [din("ln%d_g" % i, [D]) for i in (1, 2, 3)]
    ln_b = [din("ln%d_b" % i, [D]) for i in (1, 2, 3)]
    w_mq = din("w_mq", [D, D])
    w_mk = din("w_mk", [D, D])
    w_mv = din("w_mv", [D, D])
    w_mo = din("w_mo", [D, D])
    w_gate = din("w_gate", [D, DFF])
    w_up = din("w_up", [D, DFF])
    w_down = din("w_down", [DFF, D])

    y_p = dout("y_p", [S, D])
    y_s = dout("y_s", [ST, D])
    wk_p = dout("wk_p", [S, DATT])
    wv_p = dout("wv_p", [S, DATT])
    wk_s = dout("wk_s", [ST, DATT])
    wv_s = dout("wv_s", [ST, DATT])
    hr_p = dout("hr_p", [2048])
    hi_p = dout("hi_p", [2048])
    hr_s = dout("hr_s", [SB, 2048])
    hi_s = dout("hi_s", [SB, 2048])
    mk_p = dout("mk_p", [NMEM, D])
    mv_p = dout("mv_p", [NMEM, D])

    with ExitStack() as es:
        cx = Ctx(nc, es)
        _UID = [0]

        def sb(name, shape, dt=F32, stack=es):
            return stack.enter_context(nc.sbuf_tensor(name, list(shape), dt))

        def ps(name, shape, dt=F32, stack=es):
            return stack.enter_context(nc.psum_tensor(name, list(shape), dt))

        ident_b = sb("ident_b", [128, 128], BF16)
        ident_f = sb("ident_f", [128, 128], F32)
        ones_b = sb("ones_b", [128, 128], BF16)
        eps_t = sb("eps_t", [128, 1], F32)
        yssm = sb("yssm", [128, 4, T], F32)

        with ExitStack() as ph:
            it = sb("it_i", [128, 128], I32, ph)
            itf = sb("it_f", [128, 128], F32, ph)
            cx.op("pool", lambda g: g.iota(it[:], pattern=[[1, 128]], base=0, channel_multiplier=-1), writes=["it"])
            cx.op("dve", lambda v: v.tensor_copy(out=itf[:], in_=it[:]), reads=["it"], writes=["itf"])
            cx.op("dve", lambda v: v.tensor_scalar(out=ident_f[:], in0=itf[:], scalar1=0.0, scalar2=None,
                                                   op0=ALU.is_equal), reads=["itf"], writes=["ident_f"])
            cx.op("dve", lambda v: v.tensor_copy(out=ident_b[:], in_=ident_f[:]), reads=["ident_f"], writes=["ident_b"])
            cx.op("dve", lambda v: v.memset(ones_b[:], 1.0), writes=["ones_b"])
            cx.op("dve", lambda v: v.memset(eps_t[:], EPS), writes=["eps_t"])
            cx.barrier()

        def phase1(xT, srcs=None):
          _UID[0] += 1
          u_ = 'a%d_' % _UID[0]
          with ExitStack() as ph:
              xs = [sb(u_ + "xs%d" % i, [128, D], F32, ph) for i in range(2)]
              xb = [sb(u_ + "xb%d" % i, [128, D], BF16, ph) for i in range(2)]
              pt = [ps(u_ + "pt%d" % i, [128, 8, 128], BF16, ph) for i in range(2)]
              if srcs is None:
                  srcs = [(x_p[tt * 128:(tt + 1) * 128, :], 128) for tt in range(16)] + [(x_s[:, :], ST)]
              for tt, (src, rows) in enumerate(srcs):
                  i = tt % 2
                  cx.dma("sp", xs[i][0:rows, :], src, writes=["xs%d" % i])
                  cx.op("act" if tt % 2 else "dve",
                        (lambda e, i=i, rows=rows: e.activation(out=xb[i][0:rows, :], in_=xs[i][0:rows, :], func=AF.Copy))
                        if tt % 2 else
                        (lambda e, i=i, rows=rows: e.tensor_copy(out=xb[i][0:rows, :], in_=xs[i][0:rows, :])),
                        reads=["xs%d" % i], writes=["xb%d" % i])
                  for kc in range(8):
                      cx.op("pe", lambda pe, i=i, kc=kc, rows=rows: pe.transpose(
                          out=pt[i][:, kc, 0:rows], in_=xb[i][0:rows, kc * 128:(kc + 1) * 128],
                          identity=ident_b[0:rows, 0:rows]),
                          reads=["xb%d" % i, "ident_b"], writes=["pt%d_%d" % (i, kc)])
                  cx.op("dve" if tt % 2 else "act",
                        (lambda e, i=i, rows=rows, tt=tt: e.tensor_copy(out=xT[:, :, tt * 128:tt * 128 + rows], in_=pt[i][:, :, 0:rows]))
                        if tt % 2 else
                        (lambda e, i=i, rows=rows, tt=tt: e.activation(out=xT[:, :, tt * 128:tt * 128 + rows], in_=pt[i][:, :, 0:rows], func=AF.Copy)),
                        reads=["pt%d_%d" % (i, kc) for kc in range(8)], writes=["xT"])
              cx.barrier()

        def phase2(xT, cbs, qT, kT, uT):
          _UID[0] += 1
          u_ = 'b%d_' % _UID[0]
          with ExitStack() as ph:
              wst = [sb(u_ + "wst%d" % i, [128, 8, 512], F32, ph) for i in range(2)]
              wbf = [sb(u_ + "wbf%d" % i, [128, 8, 512], BF16, ph) for i in range(2)]
              ost = [sb(u_ + "ost%d" % i, [128, 512], F32, ph) for i in range(2)]
              pp = [ps(u_ + "pp%d" % i, [128, 512], F32, ph) for i in range(4)]
              w_v = w_in.rearrange("(kc p) c -> p kc c", p=128)
              ppi = 0
              osi = 0
              for cb in cbs:
                  i = cb % 2
                  cx.dma("sp" if cb % 2 == 0 else "act", wst[i][:, :, :], w_v[:, :, cb * 512:(cb + 1) * 512], writes=["wst%d" % i])
                  cx.op("pool", lambda e, i=i: e.tensor_copy(out=wbf[i][:, 0:4, :], in_=wst[i][:, 0:4, :]),
                        reads=["wst%d" % i], writes=["wbf%d_a" % i])
                  cx.op("dve", lambda e, i=i: e.tensor_copy(out=wbf[i][:, 4:8, :], in_=wst[i][:, 4:8, :]),
                        reads=["wst%d" % i], writes=["wbf%d_b" % i])
                  wkeys = ["wbf%d_a" % i, "wbf%d_b" % i]
                  if cb in (0, 1, 3):
                      dst = {0: qT, 1: kT, 3: uT}[cb]
                      dkey = {0: "qT", 1: "kT", 3: "uT"}[cb]
                      for sbk in range(4):
                          for nt in range(5):
                              t0 = nt * 512
                              n = 512 if nt < 4 else ST
                              p = ppi % 4
                              ppi += 1
                              for kc in range(8):
                                  cx.op("pe", lambda pe, p=p, i=i, kc=kc, sbk=sbk, t0=t0, n=n: pe.matmul(
                                      pp[p][:, 0:n], lhsT=wbf[i][:, kc, sbk * 128:(sbk + 1) * 128], rhs=xT[:, kc, t0:t0 + n],
                                      start=(kc == 0), stop=(kc == 7)),
                                      reads=wkeys + ["xT"], writes=["pp%d" % p])
                              if ppi % 2:
                                  cx.op("act", lambda e, p=p, sbk=sbk, t0=t0, n=n, dst=dst: e.activation(
                                      out=dst[:, sbk, t0:t0 + n], in_=pp[p][:, 0:n], func=AF.Copy),
                                      reads=["pp%d" % p], writes=[dkey + "%d_%d" % (sbk, nt)])
                              else:
                                  cx.op("dve", lambda e, p=p, sbk=sbk, t0=t0, n=n, dst=dst: e.tensor_copy(
                                      out=dst[:, sbk, t0:t0 + n], in_=pp[p][:, 0:n]),
                                      reads=["pp%d" % p], writes=[dkey + "%d_%d" % (sbk, nt)])
                  if cb in (1, 2):
                      dp, dsm = (wk_p, wk_s) if cb == 1 else (wv_p, wv_s)
                      for tt in range(17):
                          rows = 128 if tt < 16 else ST
                          p = ppi % 4
                          ppi += 1
                          for kc in range(8):
                              cx.op("pe", lambda pe, p=p, i=i, kc=kc, tt=tt, rows=rows: pe.matmul(
                                  pp[p][0:rows, :], lhsT=xT[:, kc, tt * 128:tt * 128 + rows], rhs=wbf[i][:, kc, :],
                                  start=(kc == 0), stop=(kc == 7)),
                                  reads=wkeys + ["xT"], writes=["pp%d" % p])
                          o = osi % 2
                          osi += 1
                          if osi % 2:
                              cx.op("act", lambda e, p=p, o=o, rows=rows: e.activation(
                                  out=ost[o][0:rows, :], in_=pp[p][0:rows, :], func=AF.Copy),
                                  reads=["pp%d" % p], writes=["ost%d" % o])
                          else:
                              cx.op("dve", lambda e, p=p, o=o, rows=rows: e.tensor_copy(
                                  out=ost[o][0:rows, :], in_=pp[p][0:rows, :]),
                                  reads=["pp%d" % p], writes=["ost%d" % o])
                          dstd = dp[tt * 128:(tt + 1) * 128, :] if tt < 16 else dsm[:, :]
                          cx.dma("sp", dstd, ost[o][0:rows, :], reads=["ost%d" % o], writes=["wkv_out"])
              cx.barrier()

        pxu = ExitStack()
        uT = sb("uT", [128, 4, T], BF16, pxu)
        with ExitStack() as px:
            xT = sb("xT", [128, 8, T], BF16, px)
            phase1(xT)
            phase2(xT, [3], None, None, uT)
        TWO_PI = 2.0 * math.pi
        C1 = 6.28125
        C2 = round((TWO_PI - C1) * (1 << 24)) / float(1 << 24)
        C3 = TWO_PI - C1 - C2

        def V(fn, r=(), w=()):
            return cx.op("dve", fn, reads=r, writes=w)

        def A(fn, r=(), w=()):
            return cx.op("act", fn, reads=r, writes=w)

        def G(fn, r=(), w=()):
            return cx.op("pool", fn, reads=r, writes=w)

        def P(fn, r=(), w=()):
            return cx.op("pe", fn, reads=r, writes=w)

        def TT(e, out, in0, in1, op, r, w):
            return cx.op(e, lambda en: en.tensor_tensor(out=out, in0=in0, in1=in1, op=op), reads=r, writes=w)

        def cis(ph, x, n, cos_o, sin_o, tag, xk, ok):
            u = sb(tag + "u", [128, n], F32, ph)
            ki = sb(tag + "ki", [128, n], I32, ph)
            kf = sb(tag + "kf", [128, n], F32, ph)
            r = sb(tag + "r", [128, n], F32, ph)
            m = sb(tag + "m", [128, n], F32, ph)
            k = tag
            V(lambda e: e.tensor_scalar(out=u[:], in0=x, scalar1=1.0 / TWO_PI, scalar2=None, op0=ALU.mult), [xk], [k + "u"])
            V(lambda e: e.tensor_copy(out=ki[:], in_=u[:]), [k + "u"], [k + "ki"])
            V(lambda e: e.tensor_copy(out=kf[:], in_=ki[:]), [k + "ki"], [k + "kf"])
            V(lambda e: e.scalar_tensor_tensor(out=r[:], in0=kf[:], scalar=-C1, in1=x, op0=ALU.mult, op1=ALU.add), [k + "kf", xk], [k + "r"])
            V(lambda e: e.scalar_tensor_tensor(out=r[:], in0=kf[:], scalar=-C2, in1=r[:], op0=ALU.mult, op1=ALU.add), [k + "kf", k + "r"], [k + "r"])
            V(lambda e: e.scalar_tensor_tensor(out=r[:], in0=kf[:], scalar=-C3, in1=r[:], op0=ALU.mult, op1=ALU.add), [k + "kf", k + "r"], [k + "r"])
            V(lambda e: e.tensor_scalar(out=m[:], in0=r[:], scalar1=math.pi, scalar2=-TWO_PI, op0=ALU.is_gt, op1=ALU.mult), [k + "r"], [k + "m"])
            TT("dve", r[:], r[:], m[:], ALU.add, [k + "r", k + "m"], [k + "r"])
            V(lambda e: e.tensor_scalar(out=m[:], in0=r[:], scalar1=-math.pi, scalar2=TWO_PI, op0=ALU.is_lt, op1=ALU.mult), [k + "r"], [k + "m"])
            TT("dve", r[:], r[:], m[:], ALU.add, [k + "r", k + "m"], [k + "r"])
            A(lambda e: e.activation(out=sin_o, in_=r[:], func=AF.Sin), [k + "r"], [ok + "s"])
            V(lambda e: e.tensor_scalar(out=u[:], in0=r[:], scalar1=math.pi / 2, scalar2=None, op0=ALU.add), [k + "r"], [k + "u"])
            V(lambda e: e.tensor_scalar(out=m[:], in0=u[:], scalar1=math.pi, scalar2=-TWO_PI, op0=ALU.is_gt, op1=ALU.mult), [k + "u"], [k + "m"])
            TT("dve", u[:], u[:], m[:], ALU.add, [k + "u", k + "m"], [k + "u"])
            A(lambda e: e.activation(out=cos_o, in_=u[:], func=AF.Sin), [k + "u"], [ok + "c"])

        with ExitStack() as ph:
            T1r = sb("T1r", [128, 4, 16, 128], BF16, ph)
            T1i = sb("T1i", [128, 4, 16, 128], BF16, ph)
            T2r = sb("T2r", [128, 16, 16, 32], BF16, ph)
            T2n = sb("T2n", [128, 16, 16, 32], BF16, ph)
            Kbd = sb("Kbd", [128, 4, 16, 128], BF16, ph)
            Fr = sb("Fr", [128, 16, 17], F32, ph)
            Fi = sb("Fi", [128, 16, 17], F32, ph)
            h0r = sb("h0r", [128, 16, SB], F32, ph)
            h0i = sb("h0i", [128, 16, SB], F32, ph)

            with ExitStack() as p2:
                AR = sb("AR", [128, 16], F32, p2)
                AI = sb("AI", [128, 16], F32, p2)
                DT = sb("DT", [128, 16], F32, p2)
                dtar = sb("dtar", [128, 16], F32, p2)
                dtai = sb("dtai", [128, 16], F32, p2)
                ARG = sb("ARG", [128, 16, 17], F32, p2)
                ANG = sb("ANG", [128, 16, 17], F32, p2)
                MAG = sb("MAG", [128, 16, 17], F32, p2)
                COS = sb("COS", [128, 16, 17], F32, p2)
                SIN = sb("SIN", [128, 16, 17], F32, p2)
                cor = sb("cor", [128, 16], F32, p2)
                coi = sb("coi", [128, 16], F32, p2)
                t1 = sb("t1", [128, 16], F32, p2)
                t2 = sb("t2", [128, 16], F32, p2)
                t3 = sb("t3", [128, 16], F32, p2)
                FBr = sb("FBr", [128, 16, 16], F32, p2)
                FBi = sb("FBi", [128, 16, 16], F32, p2)
                f1 = sb("f1", [128, 16, 16], F32, p2)
                BDr = sb("BDr", [128, 16, 32], F32, p2)
                BDi = sb("BDi", [128, 16, 32], F32, p2)
                CDr = sb("CDr", [128, 16, 32], F32, p2)
                CDi = sb("CDi", [128, 16, 32], F32, p2)
                CDrb = sb("CDrb", [128, 16, 32], BF16, p2)
                CDnb = sb("CDnb", [128, 16, 32], BF16, p2)
                cn = sb("cn", [128, 2, 4, 2, 64], F32, p2)
                dsk = sb("dsk", [128, 4], F32, p2)
                h0n = sb("h0n", [SB, 2048], F32, p2)
                tA = sb("tA", [128, 4, 16, 32], F32, p2)
                tB = sb("tB", [128, 4, 16, 32], F32, p2)
                LBr = sb("LBr", [128, 4, 16, 32], BF16, p2)
                LBi = sb("LBi", [128, 4, 16, 32], BF16, p2)
                pT = ps("pT", [128, 2, 16, 128], BF16, p2)
                pK = ps("pK", [128, 16, 32], F32, p2)
                pC = ps("pC", [128, 4, 128], F32, p2)
                pH = ps("pH", [128, 2, 16, SB], F32, p2)

                cx.dma("sp", AR[:, :], a_re.rearrange("(j q) -> q j", q=128), writes=["AR"], allow_slow_non_contiguous=True)
                cx.dma("sp", AI[:, :], a_im.rearrange("(j q) -> q j", q=128), writes=["AI"], allow_slow_non_contiguous=True)
                ldv = log_dt.rearrange("(j a) -> a j", a=2)
                for a in range(2):
                    cx.dma("act", DT[64 * a:64 * a + 64, :], ldv[a:a + 1, :].broadcast_to([64, 16]), writes=["DT%d" % a],
                           allow_slow_non_contiguous=True)
                V(lambda e: e.memset(BDr[:], 0.0), [], ["BDr"])
                V(lambda e: e.memset(BDi[:], 0.0), [], ["BDi"])
                for (src, dst, key) in ((b_re, BDr, "BDr"), (b_im, BDi, "BDi")):
                    sv = src.rearrange("(j a p) h -> a p j h", a=2, p=64)
                    for a in range(2):
                        cx.dma("sp" if a == 0 else "act", dst[64 * a:64 * a + 64, :, 16 * a:16 * a + 16], sv[a], reads=[], writes=[key],
                               allow_slow_non_contiguous=True)
                for ri, src in enumerate((c_re, c_im)):
                    sv = src.rearrange("(jj r) p -> r jj p", r=128)
                    for dup in range(2):
                        cx.dma("sp" if dup == 0 else "act", cn[:, ri, :, dup, :], sv, writes=["cn"])
                cx.dma("sp", dsk[:, :], d_skip.rearrange("(jj r) -> r jj", r=128), writes=["dsk"], allow_slow_non_contiguous=True)

                if STG < 2:
                    cx.enabled = False
                A(lambda e: e.activation(out=DT[:], in_=DT[:], func=AF.Exp), ["DT0", "DT1"], ["DT"])
                TT("dve", dtar[:], DT[:], AR[:], ALU.mult, ["DT", "AR"], ["dtar"])
                TT("dve", dtai[:], DT[:], AI[:], ALU.mult, ["DT", "AI"], ["dtai"])
                for k in range(17):
                    V(lambda e, k=k: e.tensor_scalar(out=ARG[:, :, k], in0=dtar[:], scalar1=float(k), scalar2=None, op0=ALU.mult), ["dtar"], ["ARG"])
                    V(lambda e, k=k: e.tensor_scalar(out=ANG[:, :, k], in0=dtai[:], scalar1=float(k), scalar2=None, op0=ALU.mult), ["dtai"], ["ANG"])
                A(lambda e: e.activation(out=MAG[:], in_=ARG[:], func=AF.Exp), ["ARG"], ["MAG"])
                cis(p2, ANG[:].rearrange("p a b -> p (a b)"), 16 * 17, COS[:].rearrange("p a b -> p (a b)"),
                    SIN[:].rearrange("p a b -> p (a b)"), "c1", "ANG", "CS")
                TT("dve", Fr[:], MAG[:], COS[:], ALU.mult, ["MAG", "CSc"], ["Fr"])
                TT("dve", Fi[:], MAG[:], SIN[:], ALU.mult, ["MAG", "CSs"], ["Fi"])
                V(lambda e: e.tensor_scalar(out=t1[:], in0=Fr[:, :, 1], scalar1=-1.0, scalar2=None, op0=ALU.add), ["Fr"], ["t1"])
                TT("dve", t2[:], AR[:], AR[:], ALU.mult, ["AR"], ["t2"])
                TT("dve", t3[:], AI[:], AI[:], ALU.mult, ["AI"], ["t3"])
                TT("dve", t2[:], t2[:], t3[:], ALU.add, ["t2", "t3"], ["t2"])
                V(lambda e: e.reciprocal(out=t2[:], in_=t2[:]), ["t2"], ["t2"])
                TT("dve", cor[:], t1[:], AR[:], ALU.mult, ["t1", "AR"], ["cor"])
                TT("dve", t3[:], Fi[:, :, 1], AI[:], ALU.mult, ["Fi", "AI"], ["t3"])
                TT("dve", cor[:], cor[:], t3[:], ALU.add, ["cor", "t3"], ["cor"])
                TT("dve", cor[:], cor[:], t2[:], ALU.mult, ["cor", "t2"], ["cor"])
                TT("dve", coi[:], Fi[:, :, 1], AR[:], ALU.mult, ["Fi", "AR"], ["coi"])
                TT("dve", t3[:], t1[:], AI[:], ALU.mult, ["t1", "AI"], ["t3"])
                TT("dve", coi[:], coi[:], t3[:], ALU.subtract, ["coi", "t3"], ["coi"])
                TT("dve", coi[:], coi[:], t2[:], ALU.mult, ["coi", "t2"], ["coi"])
                corb = cor[:, :].unsqueeze(2).broadcast_to([128, 16, 16])
                coib = coi[:, :].unsqueeze(2).broadcast_to([128, 16, 16])
                TT("dve", FBr[:], Fr[:, :, 0:16], corb, ALU.mult, ["Fr", "cor"], ["FBr"])
                TT("dve", f1[:], Fi[:, :, 0:16], coib, ALU.mult, ["Fi", "coi"], ["f1"])
                TT("dve", FBr[:], FBr[:], f1[:], ALU.subtract, ["FBr", "f1"], ["FBr"])
                TT("dve", FBi[:], Fr[:, :, 0:16], coib, ALU.mult, ["Fr", "coi"], ["FBi"])
                TT("dve", f1[:], Fi[:, :, 0:16], corb, ALU.mult, ["Fi", "cor"], ["f1"])
                TT("dve", FBi[:], FBi[:], f1[:], ALU.add, ["FBi", "f1"], ["FBi"])

                if STG < 3:
                    cx.enabled = False
                V(lambda e: e.memset(CDr[:], 0.0), [], ["CDr"])
                V(lambda e: e.memset(CDi[:], 0.0), [], ["CDi"])
                for jj in range(4):
                    for ri, dst, key in ((0, CDr, "CDr"), (1, CDi, "CDi")):
                        P(lambda pe, jj=jj, ri=ri: pe.transpose(out=pC[:, ri, :], in_=cn[:, ri, jj, :, :].rearrange("p d q -> p (d q)"),
                                                                 identity=ident_f[:, :]), ["cn", "ident_f"], ["pC"])
                        pv = pC[:, ri, :].rearrange("p (m a h) -> p m a h", m=4, a=2)
                        for a in range(2):
                            V(lambda e, jj=jj, a=a, dst=dst, pv=pv: e.tensor_copy(
                                out=dst[64 * a:64 * a + 64, 4 * jj:4 * jj + 4, 16 * a:16 * a + 16], in_=pv[64 * a:64 * a + 64, :, a, :]),
                                [], [key, "pC"])
                if STG < 3.2:
                    cx.enabled = False
                V(lambda e: e.tensor_copy(out=CDrb[:], in_=CDr[:]), ["CDr"], ["CDrb"])
                V(lambda e: e.tensor_scalar(out=CDnb[:], in0=CDi[:], scalar1=-1.0, scalar2=None, op0=ALU.mult), ["CDi"], ["CDnb"])

                if STG < 3.5:
                    cx.enabled = False
                for ri, dst, key in ((0, h0r, "h0r"), (1, h0i, "h0i")):
                    cx.dma("sp", h0n[:, :], (s_re if ri == 0 else s_im)[:, :], writes=["h0n"])
                    for j in range(16):
                        P(lambda pe, ri=ri, j=j: pe.transpose(out=pH[:, ri, j, :], in_=h0n[:, 128 * j:128 * j + 128],
                                                               identity=ident_f[0:SB, 0:SB]), ["h0n", "ident_f"], ["pH"])
                    V(lambda e, ri=ri, dst=dst: e.tensor_copy(out=dst[:], in_=pH[:, ri, :, :]), [], [key, "pH"])

                if STG < 4:
                    cx.enabled = False
                V(lambda e: e.memset(Kbd[:], 0.0), [], ["Kbd"])
                for jj in range(4):
                    js = slice(4 * jj, 4 * jj + 4)
                    fbr = FBr[:, js, :].unsqueeze(3).broadcast_to([128, 4, 16, 32])
                    fbi = FBi[:, js, :].unsqueeze(3).broadcast_to([128, 4, 16, 32])
                    bdr = BDr[:, js, :].unsqueeze(2).broadcast_to([128, 4, 16, 32])
                    bdi = BDi[:, js, :].unsqueeze(2).broadcast_to([128, 4, 16, 32])
                    TT("dve", tA[:], fbr, bdr, ALU.mult, ["FBr", "BDr"], ["tA"])
                    TT("pool", tB[:], fbi, bdi, ALU.mult, ["FBi", "BDi"], ["tB"])
                    TT("dve", LBr[:], tA[:], tB[:], ALU.subtract, ["tA", "tB"], ["LBr"])
                    TT("dve", tA[:], fbr, bdi, ALU.mult, ["FBr", "BDi", "LBr"], ["tA"])
                    TT("pool", tB[:], fbi, bdr, ALU.mult, ["FBi", "BDr", "LBr"], ["tB"])
                    TT("dve", LBi[:], tA[:], tB[:], ALU.add, ["tA", "tB"], ["LBi"])
                    for ri, src, dst, skey, dkey in ((0, LBr, T1r, "LBr", "T1r"), (1, LBi, T1i, "LBi", "T1i")):
                        for k in range(16):
                            for m in range(4):
                                P(lambda pe, ri=ri, k=k, m=m, src=src: pe.transpose(
                                    out=pT[32 * m:32 * m + 32, ri, k, :], in_=src[:, m, k, :], identity=ident_b[:, :], tile_position=(0, 32 * m)),
                                    [skey, "ident_b"], ["pT%d" % ri])
                        A(lambda e, ri=ri, dst=dst, jj=jj: e.activation(out=dst[:, jj, :, :], in_=pT[:, ri, :, :], func=AF.Copy),
                          ["pT%d" % ri], [dkey])
                    for k in range(16):
                        for m in range(4):
                            P(lambda pe, k=k, m=m, jj=jj: pe.matmul(pK[32 * m:32 * m + 32, k, :], lhsT=LBr[:, m, k, :],
                                                                    rhs=CDrb[:, 4 * jj + m, :], start=True, stop=False, tile_position=(0, 32 * m)),
                              ["LBr", "CDrb"], ["pK"])
                            P(lambda pe, k=k, m=m, jj=jj: pe.matmul(pK[32 * m:32 * m + 32, k, :], lhsT=LBi[:, m, k, :],
                                                                    rhs=CDnb[:, 4 * jj + m, :], start=False, stop=True, tile_position=(0, 32 * m)),
                              ["LBi", "CDnb"], ["pK"])
                    for m in range(4):
                        V(lambda e, m=m, jj=jj: e.tensor_copy(out=Kbd[32 * m:32 * m + 32, jj, :, 32 * m:32 * m + 32],
                                                              in_=pK[32 * m:32 * m + 32, :, :]), ["pK"], ["Kbd"])
                    V(lambda e, jj=jj: e.scalar_tensor_tensor(out=Kbd[:, jj, 0, :], in0=ident_f[:, :], scalar=dsk[:, jj:jj + 1],
                                                              in1=Kbd[:, jj, 0, :], op0=ALU.mult, op1=ALU.add),
                      ["Kbd", "dsk", "ident_f"], ["Kbd"])
                    fr = Fr[:, js, 1:17].unsqueeze(3).broadcast_to([128, 4, 16, 32])
                    fi = Fi[:, js, 1:17].unsqueeze(3).broadcast_to([128, 4, 16, 32])
                    cdr = CDr[:, js, :].unsqueeze(2).broadcast_to([128, 4, 16, 32])
                    cdi = CDi[:, js, :].unsqueeze(2).broadcast_to([128, 4, 16, 32])
                    TT("dve", tA[:], fr, cdr, ALU.mult, ["Fr", "CDr", "LBi"], ["tA"])
                    TT("pool", tB[:], fi, cdi, ALU.mult, ["Fi", "CDi", "LBi"], ["tB"])
                    TT("dve", T2r[:, js, :, :], tA[:], tB[:], ALU.subtract, ["tA", "tB"], ["T2r"])
                    TT("dve", tA[:], fr, cdi, ALU.mult, ["Fr", "CDi", "T2r"], ["tA"])
                    TT("pool", tB[:], fi, cdr, ALU.mult, ["Fi", "CDr", "T2r"], ["tB"])
                    V(lambda e, js=js: e.scalar_tensor_tensor(out=T2n[:, js, :, :], in0=tA[:], scalar=-1.0, in1=tB[:],
                                                              op0=ALU.mult, op1=ALU.subtract), ["tA", "tB"], ["T2n"])
                cx.barrier()

            if STG < 5:
                cx.enabled = False
            with ExitStack() as p3:
                Zr = sb("Zr", [128, 16, 128], F32, p3)
                Zi = sb("Zi", [128, 16, 128], F32, p3)
                Yr = sb("Yr", [128, 16, 128], F32, p3)
                Yi = sb("Yi", [128, 16, 128], F32, p3)
                tq = sb("tq", [128, 16, 128], F32, p3)
                tw = sb("tw", [128, 16, 128], F32, p3)
                Mr = sb("Mr", [128, 16], F32, p3)
                Mi = sb("Mi", [128, 16], F32, p3)
                m1 = sb("m1", [128, 16], F32, p3)
                m2 = sb("m2", [128, 16], F32, p3)
                Hxr = sb("Hxr", [128, 16, 128], BF16, p3)
                Hxi = sb("Hxi", [128, 16, 128], BF16, p3)
                h0rb = sb("h0rb", [128, 16, SB], BF16, p3)
                h0ib = sb("h0ib", [128, 16, SB], BF16, p3)
                hsr = sb("hsr", [128, 16, SB], F32, p3)
                hsi = sb("hsi", [128, 16, SB], F32, p3)
                hsn = sb("hsn", [SB, 2048], F32, p3)
                pZ = ps("pZ", [128, 2, 8, 128], F32, p3)
                pY = ps("pY", [128, 16, 128], F32, p3)

                for half in range(2):
                    for j8 in range(8):
                        j = 8 * half + j8
                        jj, m = j // 4, j % 4
                        for ri, Tt, key in ((0, T1r, "T1r"), (1, T1i, "T1i")):
                            for i in range(16):
                                P(lambda pe, ri=ri, j8=j8, jj=jj, m=m, i=i, Tt=Tt: pe.matmul(
                                    pZ[:, ri, j8, :], lhsT=Tt[32 * m:32 * m + 32, jj, 15 - i, :],
                                    rhs=uT[32 * m:32 * m + 32, jj, i:2048:16], start=(i == 0), stop=(i == 15), tile_position=(32 * m, 0)),
                                    [key, "uT"], ["pZ%d" % ri])
                    hs = slice(8 * half, 8 * half + 8)
                    V(lambda e, hs=hs: e.tensor_copy(out=Zr[:, hs, :], in_=pZ[:, 0, :, :]), ["pZ0"], ["Zr"])
                    A(lambda e, hs=hs: e.activation(out=Zi[:, hs, :], in_=pZ[:, 1, :, :], func=AF.Copy), ["pZ1"], ["Zi"])

                V(lambda e: e.tensor_copy(out=Mr[:], in_=Fr[:, :, 16]), ["Fr"], ["Mr"])
                V(lambda e: e.tensor_copy(out=Mi[:], in_=Fi[:, :, 16]), ["Fi"], ["Mi"])
                src_r, src_i, dst_r, dst_i = Zr, Zi, Yr, Yi
                skr, ski, dkr, dki = "Zr", "Zi", "Yr", "Yi"
                for lvl in range(7):
                    s = 1 << lvl
                    n = 128 - s
                    mrb = Mr[:, :].unsqueeze(2).broadcast_to([128, 16, n])
                    mib = Mi[:, :].unsqueeze(2).broadcast_to([128, 16, n])
                    TT("dve", dst_r[:, :, s:], mrb, src_r[:, :, 0:n], ALU.mult, ["Mr", skr], [dkr])
                    TT("dve", tq[:, :, s:], mib, src_i[:, :, 0:n], ALU.mult, ["Mi", ski], ["tq"])
                    TT("dve", dst_r[:, :, s:], dst_r[:, :, s:], tq[:, :, s:], ALU.subtract, [dkr, "tq"], [dkr])
                    TT("dve", dst_r[:, :, s:], dst_r[:, :, s:], src_r[:, :, s:], ALU.add, [dkr, skr], [dkr])
                    V(lambda e, s=s, dst_r=dst_r, src_r=src_r: e.tensor_copy(out=dst_r[:, :, 0:s], in_=src_r[:, :, 0:s]), [skr], [dkr])
                    TT("pool", dst_i[:, :, s:], mrb, src_i[:, :, 0:n], ALU.mult, ["Mr", ski], [dki])
                    TT("pool", tw[:, :, s:], mib, src_r[:, :, 0:n], ALU.mult, ["Mi", skr], ["tw"])
                    TT("pool", dst_i[:, :, s:], dst_i[:, :, s:], tw[:, :, s:], ALU.add, [dki, "tw"], [dki])
                    TT("pool", dst_i[:, :, s:], dst_i[:, :, s:], src_i[:, :, s:], ALU.add, [dki, ski], [dki])
                    G(lambda e, s=s, dst_i=dst_i, src_i=src_i: e.tensor_copy(out=dst_i[:, :, 0:s], in_=src_i[:, :, 0:s]), [ski], [dki])
                    if lvl < 6:
                        TT("dve", m1[:], Mr[:], Mr[:], ALU.mult, ["Mr"], ["m1"])
                        TT("dve", m2[:], Mi[:], Mi[:], ALU.mult, ["Mi"], ["m2"])
                        TT("dve", m1[:], m1[:], m2[:], ALU.subtract, ["m1", "m2"], ["m1"])
                        TT("dve", m2[:], Mr[:], Mi[:], ALU.mult, ["Mr", "Mi", dkr, dki, "tq", "tw"], ["m2"])
                        V(lambda e: e.tensor_scalar(out=Mi[:], in0=m2[:], scalar1=2.0, scalar2=None, op0=ALU.mult), ["m2", dkr, dki, "tq", "tw"], ["Mi"])
                        V(lambda e: e.tensor_copy(out=Mr[:], in_=m1[:]), ["m1", dkr, dki, "tq", "tw"], ["Mr"])
                    src_r, src_i, dst_r, dst_i = dst_r, dst_i, src_r, src_i
                    skr, ski, dkr, dki = dkr, dki, skr, ski
                Hr_, Hi_, hkr, hki = src_r, src_i, skr, ski
                cx.dma("sp", hr_p.rearrange("(j q) -> q j", q=128), Hr_[:, :, 127], reads=[hkr], writes=["o_hrp"], allow_slow_non_contiguous=True)
                cx.dma("sp", hi_p.rearrange("(j q) -> q j", q=128), Hi_[:, :, 127], reads=[hki], writes=["o_hip"], allow_slow_non_contiguous=True)
                V(lambda e: e.memset(Hxr[:, :, 0:1], 0.0), [], ["Hxr"])
                V(lambda e: e.memset(Hxi[:, :, 0:1], 0.0), [], ["Hxi"])
                V(lambda e: e.tensor_copy(out=Hxr[:, :, 1:128], in_=Hr_[:, :, 0:127]), [hkr, "Hxr"], ["Hxr"])
                V(lambda e: e.tensor_copy(out=Hxi[:, :, 1:128], in_=Hi_[:, :, 0:127]), [hki, "Hxi"], ["Hxi"])
                V(lambda e: e.tensor_copy(out=h0rb[:], in_=h0r[:]), ["h0r"], ["h0rb"])
                V(lambda e: e.tensor_copy(out=h0ib[:], in_=h0i[:]), ["h0i"], ["h0ib"])

                for jj in range(4):
                    uv = uT[:, jj, 0:2048].rearrange("p (c i) -> p i c", i=16)
                    for bk in range(4):
                        for tau in range(0, 4 * bk + 4):
                            i0 = max(4 * bk, tau)
                            i1 = 4 * bk + 4
                            P(lambda pe, jj=jj, tau=tau, i0=i0, i1=i1, uv=uv: pe.matmul(
                                pY[:, i0:i1, :], lhsT=Kbd[:, jj, tau, :], rhs=uv[:, i0 - tau:i1 - tau, :],
                                start=(tau == 0), stop=False, skip_group_check=True),
                                ["Kbd", "uT"], ["pY"])
                    for i in range(16):
                        for m in range(4):
                            j = 4 * jj + m
                            P(lambda pe, i=i, m=m, j=j: pe.matmul(pY[32 * m:32 * m + 32, i, :], lhsT=T2r[:, j, i, :], rhs=Hxr[:, j, :],
                                                                  start=False, stop=False, skip_group_check=True, tile_position=(0, 32 * m)), ["T2r", "Hxr"], ["pY"])
                            P(lambda pe, i=i, m=m, j=j: pe.matmul(pY[32 * m:32 * m + 32, i, :], lhsT=T2n[:, j, i, :], rhs=Hxi[:, j, :],
                                                                  start=False, stop=True, skip_group_check=True, tile_position=(0, 32 * m)), ["T2n", "Hxi"], ["pY"])
                    yv = yssm[:, jj, 0:2048].rearrange("p (c i) -> p i c", i=16)
                    V(lambda e, yv=yv: e.tensor_copy(out=yv[:, 0:8, :], in_=pY[:, 0:8, :]), ["pY"], ["yssm"])
                    A(lambda e, yv=yv: e.activation(out=yv[:, 8:16, :], in_=pY[:, 8:16, :], func=AF.Copy), ["pY"], ["yssm"])

                for jj in range(4):
                    uv = uT[:, jj, 2048:T].rearrange("p (b t) -> p t b", t=4)
                    for tau in range(4):
                        P(lambda pe, jj=jj, tau=tau, uv=uv: pe.matmul(
                            pY[:, 0, 0:64].rearrange("p (t b) -> p t b", t=4)[:, tau:4, :], lhsT=Kbd[:, jj, tau, :], rhs=uv[:, 0:4 - tau, :],
                            start=(tau == 0), stop=False, skip_group_check=True), ["Kbd", "uT"], ["pY"])
                    for t in range(4):
                        for m in range(4):
                            j = 4 * jj + m
                            P(lambda pe, t=t, m=m, j=j: pe.matmul(pY[32 * m:32 * m + 32, 0, 16 * t:16 * t + 16], lhsT=T2r[:, j, t, :], rhs=h0rb[:, j, :],
                                                                  start=False, stop=False, skip_group_check=True, tile_position=(0, 32 * m)), ["T2r", "h0rb"], ["pY"])
                            P(lambda pe, t=t, m=m, j=j: pe.matmul(pY[32 * m:32 * m + 32, 0, 16 * t:16 * t + 16], lhsT=T2n[:, j, t, :], rhs=h0ib[:, j, :],
                                                                  start=False, stop=True, skip_group_check=True, tile_position=(0, 32 * m)), ["T2n", "h0ib"], ["pY"])
                    V(lambda e, jj=jj: e.tensor_copy(out=yssm[:, jj, 2048:T].rearrange("p (b t) -> p t b", t=4),
                                                     in_=pY[:, 0, 0:64].rearrange("p (t b) -> p t b", t=4)), ["pY"], ["yssm"])

                for half in range(2):
                    for j8 in range(8):
                        j = 8 * half + j8
                        jj, m = j // 4, j % 4
                        for ri, Tt, key in ((0, T1r, "T1r"), (1, T1i, "T1i")):
                            for t in range(4):
                                P(lambda pe, ri=ri, j8=j8, jj=jj, m=m, t=t, Tt=Tt: pe.matmul(
                                    pZ[:, ri, j8, 0:SB], lhsT=Tt[32 * m:32 * m + 32, jj, 3 - t, :],
                                    rhs=uT[32 * m:32 * m + 32, jj, 2048 + t:T:4], start=(t == 0), stop=(t == 3), tile_position=(32 * m, 0)),
                                    [key, "uT"], ["pZ%d" % ri])
                    hs = slice(8 * half, 8 * half + 8)
                    f4r = Fr[:, hs, 4:5].broadcast_to([128, 8, SB])
                    f4i = Fi[:, hs, 4:5].broadcast_to([128, 8, SB])
                    TT("dve", tq[:, hs, 0:SB], f4r, h0r[:, hs, :], ALU.mult, ["Fr", "h0r"], ["tq"])
                    TT("dve", hsr[:, hs, :], pZ[:, 0, :, 0:SB], tq[:, hs, 0:SB], ALU.add, ["pZ0", "tq"], ["hsr"])
                    TT("dve", tq[:, hs, 0:SB], f4i, h0i[:, hs, :], ALU.mult, ["Fi", "h0i", "hsr"], ["tq"])
                    TT("dve", hsr[:, hs, :], hsr[:, hs, :], tq[:, hs, 0:SB], ALU.subtract, ["hsr", "tq"], ["hsr"])
                    TT("dve", tw[:, hs, 0:SB], f4r, h0i[:, hs, :], ALU.mult, ["Fr", "h0i"], ["tw"])
                    TT("dve", hsi[:, hs, :], pZ[:, 1, :, 0:SB], tw[:, hs, 0:SB], ALU.add, ["pZ1", "tw"], ["hsi"])
                    TT("dve", tw[:, hs, 0:SB], f4i, h0r[:, hs, :], ALU.mult, ["Fi", "h0r", "hsi"], ["tw"])
                    TT("dve", hsi[:, hs, :], hsi[:, hs, :], tw[:, hs, 0:SB], ALU.add, ["hsi", "tw"], ["hsi"])
                for ri, src, key, dstd in ((0, hsr, "hsr", hr_s), (1, hsi, "hsi", hi_s)):
                    for j in range(16):
                        P(lambda pe, src=src, j=j: pe.transpose(out=pY[0:SB, j, :], in_=src[:, j, :], identity=ident_f[:, :]),
                          [key, "ident_f"], ["pY"])
                    V(lambda e, ri=ri: e.tensor_copy(out=hsn[:, :], in_=pY[0:SB, :, :].rearrange("p a b -> p (a b)")), ["pY"], ["hsn"])
                    cx.dma("sp", dstd[:, :], hsn[:, :], reads=["hsn"], writes=["o_hs%d" % ri])
                cx.barrier()

        pxu.close()

        with ExitStack() as ph:
            wg_st = sb("wg_st", [128, 4, 512], F32, ph)
            wg_b = sb("wg_b", [128, 4, 512], BF16, ph)
            bg = sb("bg", [128, 4], F32, ph)
            gs = sb("gs", [128, 4], F32, ph)
            ga = sb("ga", [128, 4, 512], F32, ph)
            gb = sb("gb", [128, 4, 512], BF16, ph)
            t_a = sb("t_a", [128, 4, 512], F32, ph)
            t_b = sb("t_b", [128, 4, 512], F32, ph)
            sq = sb("sq", [128, 4, 512], BF16, ph)
            rstd = sb("rstd", [128, 512], F32, ph)
            pz = [ps("pz%d" % i, [128, 512], F32, ph) for i in range(4)]
            pss = ps("pss", [128, 512], F32, ph)
            cx.dma("sp", wg_st[:, :, :], w_glu.rearrange("(kc p) c -> p kc c", p=128), writes=["wg_st"])
            cx.dma("act", bg[:, :], b_glu.rearrange("(c p) -> p c", p=128), writes=["bg"], allow_slow_non_contiguous=True)
            cx.dma("act", gs[:, :], g_ssm.rearrange("(c p) -> p c", p=128), writes=["gs"], allow_slow_non_contiguous=True)
            V(lambda e: e.tensor_copy(out=wg_b[:], in_=wg_st[:]), ["wg_st"], ["wg_b"])
            for nt in range(5):
                t0 = nt * 512
                n = 512 if nt < 4 else ST
                yv = yssm[:, :, t0:t0 + n]
                A(lambda e, yv=yv, n=n: e.activation(out=t_a[:, :, 0:n], in_=yv, func=AF.Square), ["yssm"], ["t_a"])
                V(lambda e, n=n: e.tensor_scalar(out=t_a[:, :, 0:n], in0=t_a[:, :, 0:n], scalar1=0.044715, scalar2=1.0, op0=ALU.mult, op1=ALU.add), ["t_a"], ["t_a"])
                TT("dve", t_a[:, :, 0:n], t_a[:, :, 0:n], yv, ALU.mult, ["t_a", "yssm"], ["t_a"])
                A(lambda e, n=n: e.activation(out=t_b[:, :, 0:n], in_=t_a[:, :, 0:n], func=AF.Sigmoid, scale=1.5957691216057308), ["t_a"], ["t_b"])
                TT("dve", ga[:, :, 0:n], t_b[:, :, 0:n], yv, ALU.mult, ["t_b", "yssm"], ["ga"])
                G(lambda e, n=n: e.tensor_copy(out=gb[:, :, 0:n], in_=ga[:, :, 0:n]), ["ga"], ["gb"])
                for oc in range(4):
                    for kc in range(4):
                        P(lambda pe, oc=oc, kc=kc, n=n: pe.matmul(pz[oc][:, 0:n], lhsT=wg_b[:, kc, oc * 128:(oc + 1) * 128], rhs=gb[:, kc, 0:n],
                                                                  start=(kc == 0), stop=(kc == 3)), ["wg_b", "gb"], ["pz%d" % oc])
                    A(lambda e, oc=oc, n=n: e.activation(out=t_b[:, oc, 0:n], in_=pz[oc][:, 0:n], func=AF.Sigmoid, bias=bg[:, oc:oc + 1], scale=1.0),
                      ["bg"], ["t_b%d" % oc, "pz%d" % oc])
                    TT("dve", t_a[:, oc, 0:n], ga[:, oc, 0:n], t_b[:, oc, 0:n], ALU.mult, ["ga", "t_b%d" % oc], ["t_a%d" % oc])
                    A(lambda e, oc=oc, n=n: e.activation(out=sq[:, oc, 0:n], in_=t_a[:, oc, 0:n], func=AF.Square), ["t_a%d" % oc], ["sq%d" % oc])
                for oc in range(4):
                    P(lambda pe, oc=oc, n=n: pe.matmul(pss[:, 0:n], lhsT=ones_b[:, :], rhs=sq[:, oc, 0:n], start=(oc == 0), stop=(oc == 3)),
                      ["ones_b", "sq%d" % oc], ["pss"])
                A(lambda e, n=n: e.activation(out=rstd[:, 0:n], in_=pss[:, 0:n], func=AF.Sqrt, bias=eps_t[:, 0:1], scale=1.0 / 512.0), ["eps_t"], ["rstd", "pss"])
                V(lambda e, n=n: e.reciprocal(out=rstd[:, 0:n], in_=rstd[:, 0:n]), ["rstd"], ["rstd"])
                for oc in range(4):
                    V(lambda e, oc=oc, n=n, t0=t0: e.scalar_tensor_tensor(out=yssm[:, oc, t0:t0 + n], in0=t_a[:, oc, 0:n], scalar=gs[:, oc:oc + 1],
                                                                          in1=rstd[:, 0:n], op0=ALU.mult, op1=ALU.mult),
                      ["t_a%d" % oc, "gs", "rstd"], ["yssm"])
                V(lambda e: e.memset(rstd[:, 0:1], 0.0), ["t_a0", "t_a1", "t_a2", "t_a3", "t_b0", "t_b1", "t_b2", "t_b3", "sq0", "sq1", "sq2", "sq3", "yssm"],
                  ["t_a", "t_b", "rstd"])
            cx.barrier()

        attT = sb("attT", [128, 4, T], BF16)
        pqk = ExitStack()
        qT = sb("qT", [128, 4, T], BF16, pqk)
        kT = sb("kT", [128, 4, T], BF16, pqk)
        px = ExitStack()
        xT = sb("xT2", [128, 8, T], BF16, px)
        phase1(xT)
        phase2(xT, [0, 1, 2], qT, kT, None)
        with ExitStack() as ph:
            memT = sb("memT", [128, 8, NMEM], BF16, ph)
            phase1(memT, [(memp[0:128, :], 128), (memp[128:256, :], 128)])
            wst = [sb("mwst%d" % i, [128, 8, 512], F32, ph) for i in range(2)]
            wbf = [sb("mwbf%d" % i, [128, 8, 512], BF16, ph) for i in range(2)]
            ost = [sb("most%d" % i, [128, 512], F32, ph) for i in range(2)]
            pp = [ps("mpp%d" % i, [128, 512], F32, ph) for i in range(2)]
            n = 0
            for wi, (wd, od) in enumerate(((w_mk, mk_p), (w_mv, mv_p))):
                w_v = wd.rearrange("(kc p) c -> p kc c", p=128)
                for cb in range(2):
                    i = n % 2
                    n += 1
                    cx.dma("sp" if i == 0 else "act", wst[i][:, :, :], w_v[:, :, cb * 512:(cb + 1) * 512], writes=["mwst%d" % i])
                    cx.op("pool", lambda e, i=i: e.tensor_copy(out=wbf[i][:, 0:4, :], in_=wst[i][:, 0:4, :]), reads=["mwst%d" % i], writes=["mwbf%d_a" % i])
                    cx.op("dve", lambda e, i=i: e.tensor_copy(out=wbf[i][:, 4:8, :], in_=wst[i][:, 4:8, :]), reads=["mwst%d" % i], writes=["mwbf%d_b" % i])
                    for tt in range(2):
                        p = tt
                        for kc in range(8):
                            cx.op("pe", lambda pe, p=p, i=i, kc=kc, tt=tt: pe.matmul(
                                pp[p][:, :], lhsT=memT[:, kc, tt * 128:(tt + 1) * 128], rhs=wbf[i][:, kc, :], start=(kc == 0), stop=(kc == 7)),
                                reads=["mwbf%d_a" % i, "mwbf%d_b" % i, "xT"], writes=["mpp%d" % p])
                        if tt == 0:
                            cx.op("act", lambda e, p=p: e.activation(out=ost[p][:, :], in_=pp[p][:, :], func=AF.Copy), reads=[], writes=["most%d" % p, "mpp%d" % p])
                        else:
                            cx.op("dve", lambda e, p=p: e.tensor_copy(out=ost[p][:, :], in_=pp[p][:, :]), reads=[], writes=["most%d" % p, "mpp%d" % p])
                        cx.dma("sp", od[tt * 128:(tt + 1) * 128, cb * 512:(cb + 1) * 512], ost[p][:, :], reads=["most%d" % p], writes=["o_mem"])
            cx.barrier()
        px.close()
        with ExitStack() as ph:
            Vall = sb("Vall", [128, 48, 8, 128], BF16, ph)
            vst = [sb("vst%d" % i, [128, 512], F32, ph) for i in range(3)]
            Mpc = sb("Mpc", [128, 4, 128], BF16, ph)
            M16 = sb("M16", [128, 4, 32], BF16, ph)
            mi = sb("mi", [128, 128], I32, ph)
            mf = sb("mf", [128, 128], F32, ph)
            Pt = [sb("Pt%d" % i, [128, 512], BF16, ph) for i in range(4)]
            oatt = sb("oatt", [128, 4, 512], F32, ph)
            osq = sb("osq", [128, 4, 512], BF16, ph)
            rec = sb("rec", [128, 512], F32, ph)
            rstd = sb("a_rstd", [128, 512], F32, ph)
            gatt = sb("gatt", [128, 4], F32, ph)
            Sb = [ps("Sb%d" % i, [128, 512], F32, ph) for i in range(4)]
            acc = [ps("acc%d" % i, [128, 512], F32, ph) for i in range(2)]
            pss = ps("a_pss", [128, 512], F32, ph)

            cx.dma("act", gatt[:, :], g_att.rearrange("(c p) -> p c", p=128), writes=["gatt"], allow_slow_non_contiguous=True)
            G(lambda g: g.iota(mi[:], pattern=[[1, 128]], base=0, channel_multiplier=-1), [], ["mi"])
            V(lambda e: e.tensor_copy(out=mf[:], in_=mi[:]), ["mi"], ["mf"])
            for sl in range(4):
                V(lambda e, sl=sl: e.tensor_scalar(out=Mpc[:, sl, :], in0=mf[:], scalar1=0.0, scalar2=None,
                                                   op0=(ALU.is_le if sl % 2 == 0 else ALU.is_ge)), ["mf"], ["Mpc"])
            for n in range(4):
                V(lambda e, n=n: e.tensor_scalar(out=M16[:, n, :], in0=mf[:, 32 * n:32 * n + 32], scalar1=0.0, scalar2=None, op0=ALU.is_ge),
                  ["mf"], ["M16"])
            V(lambda e: e.memset(Vall[:, 0:24, :, :], 1.0), [], ["Vall"])
            G(lambda e: e.memset(Vall[:, 24:48, :, :], 1.0), [], ["Vall2"])
            vorder = [0, 1, 2, 3, 16, 17, 18, 19] + list(range(32, 48)) + [4, 5, 6, 7, 20, 21, 22, 23, 8, 9, 10, 11, 24, 25, 26, 27, 12, 13, 14, 15, 28, 29, 30, 31]
            for vcnt, tid in enumerate(vorder):
                if tid < 16:
                    src = wv_p[128 * tid:128 * tid + 128, :]
                elif tid < 32:
                    n_, r_ = (tid - 16) // 4, (tid - 16) % 4
                    src = wv_p[512 * n_ + r_:512 * n_ + 512:4, :]
                else:
                    src = wv_p[tid - 32:2048:16, :]
                i = vcnt % 3
                cx.dma("sp", vst[i][:, :], src, writes=["vst%d" % i])
                vk = "Vall" if tid < 24 else "Vall2"
                sv = vst[i][:, :].rearrange("p (a e d) -> p a e d", a=4, e=2)
                dv = Vall[:, tid, :, :].rearrange("p (a e) c -> p a e c", e=2)
                cx.op("pool" if tid % 2 == 0 else "dve", lambda e, sv=sv, dv=dv: e.tensor_copy(out=dv[:, :, 0, 0:64], in_=sv[:, :, 0, :]),
                      reads=["vst%d" % i, vk], writes=["V%d" % tid])
                cx.op("dve" if tid % 2 == 0 else "act", (lambda e, sv=sv, dv=dv: e.tensor_copy(out=dv[:, :, 1, 64:128], in_=sv[:, :, 1, :])) if tid % 2 == 0 else
                      (lambda e, sv=sv, dv=dv: e.activation(out=dv[:, :, 1, 64:128], in_=sv[:, :, 1, :], func=AF.Copy)),
                      reads=["vst%d" % i, vk, "V%d" % tid], writes=["V%d" % tid])
            VK = ["Vall", "Vall2"]

            banks = []
            aidx = [0]
            for n in range(4):
                for hp in range(4):
                    for e_ in range(2):
                        h = 2 * hp + e_
                        pb = 64 * e_
                        ai = aidx[0] % 2
                        aidx[0] += 1
                        hb = []
                        for half in range(2):
                            slots, pvs = [], []
                            for b2 in range(2):
                                qb = 4 * n + 2 * half + b2
                                qap = qT[pb:pb + 64, hp, 128 * qb:128 * qb + 128]
                                for kind in range(2):
                                    kt = qb - 1 + kind
                                    if kt < 0:
                                        continue
                                    sl = 2 * b2 + kind
                                    slots.append((128 * sl, 128, kT[pb:pb + 64, hp, 128 * kt:128 * kt + 128], qap))
                                    pvs.append((slice((qb - 4 * n) * 128, (qb - 4 * n) * 128 + 128), kt, slice(128 * sl, 128 * sl + 128)))
                            hb.append(dict(slots=slots, pvs=pvs, Kn=128, mask=0))
                        for half in range(2):
                            slots, pvs = [], []
                            for b2 in range(2):
                                r4 = 2 * half + b2
                                qap = qT[pb:pb + 64, hp, 512 * n + r4:512 * n + 512:4]
                                for kind in range(2):
                                    kn = n - 1 + kind
                                    if kn < 0:
                                        continue
                                    sl = 2 * b2 + kind
                                    slots.append((128 * sl, 128, kT[pb:pb + 64, hp, 512 * kn + r4:512 * kn + 512:4], qap))
                                    pvs.append((slice(r4, 512, 4), 16 + 4 * kn + r4, slice(128 * sl, 128 * sl + 128)))
                            hb.append(dict(slots=slots, pvs=pvs, Kn=128, mask=0))
                        Kn = 32 * (n + 1)
                        slots, pvs = [], []
                        for r in range(16):
                            slots.append((32 * r, 32, kT[pb:pb + 64, hp, r:16 * Kn:16], qT[pb:pb + 64, hp, 512 * n + r:512 * n + 512:16]))
                            pvs.append((slice(r, 512, 16), 32 + r, slice(32 * r, 32 * r + 32)))
                        hb.append(dict(slots=slots, pvs=pvs, Kn=Kn, mask=1))
                        for bi_, bk in enumerate(hb):
                            bk.update(n=n, hp=hp, e_=e_, h=h, ai=ai, first=(bi_ == 0), last=(bi_ == len(hb) - 1))
                            banks.append(bk)

            def emit_scores(k):
                bk = banks[k]
                si = k % 4
                Kn = bk["Kn"]
                for (c0, ncol, kap, qap) in bk["slots"]:
                    P(lambda pe, si=si, c0=c0, ncol=ncol, kap=kap, qap=qap, Kn=Kn: pe.matmul(Sb[si][0:Kn, c0:c0 + ncol], lhsT=kap, rhs=qap, start=True, stop=True),
                      ["qT", "kT"], ["Sb%d" % si])
                A(lambda e: e.activation(out=Pt[si][0:Kn, :], in_=Sb[si][0:Kn, :], func=AF.Exp, scale=0.125), [], ["Pt%d" % si, "Sb%d" % si])
                if bk["mask"] == 0:
                    pv_, m_ = Pt[si][0:Kn, :], Mpc[:, :, :].rearrange("p a b -> p (a b)")
                else:
                    pv_ = Pt[si][0:Kn, :].rearrange("p (a b) -> p a b", a=16)
                    m_ = M16[0:Kn, bk["n"], :].unsqueeze(1).broadcast_to([Kn, 16, 32])
                V(lambda e: e.tensor_tensor(out=pv_, in0=pv_, in1=m_, op=ALU.mult), ["Mpc", "M16"], ["Pt%d" % si])

            def emit_pv(k):
                bk = banks[k]
                si = k % 4
                Kn = bk["Kn"]
                A_ = acc[bk["ai"]]
                ak = "acc%d" % bk["ai"]
                h, hp, e_, n = bk["h"], bk["hp"], bk["e_"], bk["n"]
                for j_, (cols, vid, pcols) in enumerate(bk["pvs"]):
                    st = bk["first"] and j_ == 0
                    P(lambda pe, cols=cols, vid=vid, pcols=pcols, st=st: pe.matmul(A_[:, cols], lhsT=Vall[0:Kn, vid, h, :], rhs=Pt[si][0:Kn, pcols],
                                                                                 start=st, stop=False, skip_group_check=True), ["V%d" % vid, "Pt%d" % si], [ak])
                if bk["last"]:
                    vo, do = (0, 64) if e_ == 0 else (64, 0)
                    V(lambda e: e.reciprocal(out=rec[do:do + 64, :], in_=A_[do:do + 64, :]), [], ["rec", ak])
                    V(lambda e: e.tensor_tensor(out=oatt[vo:vo + 64, hp, :], in0=A_[vo:vo + 64, :], in1=rec[do:do + 64, :], op=ALU.mult),
                      ["rec"], ["oatt%d" % hp, ak])
                    if hp == 3 and e_ == 1:
                        for hp2 in range(4):
                            A(lambda e, hp2=hp2: e.activation(out=osq[:, hp2, :], in_=oatt[:, hp2, :], func=AF.Square), ["oatt%d" % hp2], ["osq%d" % hp2])
                        for hp2 in range(4):
                            P(lambda pe, hp2=hp2: pe.matmul(pss[:, :], lhsT=ones_b[:, :], rhs=osq[:, hp2, :], start=(hp2 == 0), stop=(hp2 == 3)),
                              ["ones_b", "osq%d" % hp2], ["a_pss"])
                        A(lambda e: e.activation(out=rstd[:, :], in_=pss[:, :], func=AF.Sqrt, bias=eps_t[:, 0:1], scale=1.0 / 512.0), ["eps_t"], ["a_rstd", "a_pss"])
                        V(lambda e: e.reciprocal(out=rstd[:, :], in_=rstd[:, :]), ["a_rstd"], ["a_rstd"])
                        for hp2 in range(4):
                            V(lambda e, hp2=hp2: e.scalar_tensor_tensor(out=attT[:, hp2, 512 * n:512 * n + 512], in0=oatt[:, hp2, :], scalar=gatt[:, hp2:hp2 + 1],
                                                                        in1=rstd[:, :], op0=ALU.mult, op1=ALU.mult), ["oatt%d" % hp2, "gatt", "a_rstd"], ["attT"])

            NBK = len(banks)
            LOOK = 2
            for k in range(NBK + LOOK):
                if k < NBK:
                    emit_scores(k)
                if k >= LOOK:
                    emit_pv(k - LOOK)
            cx.barrier()

        with ExitStack() as ph:
            kst = [sb("kst%d" % i, [128, 512], F32, ph) for i in range(6)]
            vst = [sb("svst%d" % i, [128, 512], F32, ph) for i in range(6)]
            kTs = [sb("kTs%d" % i, [128, 4, 8, 128], BF16, ph) for i in range(2)]
            Vs = [sb("Vs%d" % i, [128, 8, 8, 64], BF16, ph) for i in range(2)]
            vnst = sb("vnst", [4, SB, 512], F32, ph)
            Vn = sb("Vn", [4, SB, 8, 64], BF16, ph)
            mi2 = sb("mi2", [128, 4], I32, ph)
            ma2 = sb("ma2", [128, 4], I32, ph)
            mf2 = sb("mf2", [128, 4], F32, ph)
            mg2 = sb("mg2", [128, 4], F32, ph)
            mh2 = sb("mh2", [128, 4], F32, ph)
            Msf = sb("Msf", [128, 9, 4], F32, ph)
            Msb = sb("Msb", [128, 2, 9, 4], BF16, ph)
            Ps = [sb("Ps%d" % i, [128, 2, 9, 4], BF16, ph) for i in range(2)]
            oas = sb("oas", [128, 4, ST], F32, ph)
            osq = sb("s_osq", [128, 4, ST], BF16, ph)
            rec = sb("s_rec", [128, 4, 4], F32, ph)
            rstd = sb("s_rstd", [128, ST], F32, ph)
            gatt = sb("s_gatt", [128, 4], F32, ph)
            pTs = [ps("pTs%d" % i, [128, 4, 128], F32, ph) for i in range(2)]
            Sps = [ps("Sps%d" % i, [128, 512], F32, ph) for i in range(2)]
            accs = [ps("accs%d" % i, [128, 512], F32, ph) for i in range(2)]
            pss = ps("s_pss", [128, 512], F32, ph)

            cx.dma("act", gatt[:, :], g_att.rearrange("(c p) -> p c", p=128), writes=["gatt"], allow_slow_non_contiguous=True)
            cx.dma("sp", vnst[:, :, :], wv_s.rearrange("(b t) c -> t b c", t=4), writes=["vnst"])
            V(lambda e: e.tensor_copy(out=Vn[:, :, :, :], in_=vnst[:, :, :].rearrange("p b (h d) -> p b h d", h=8)), ["vnst"], ["Vn"])
            G(lambda g: g.iota(mi2[:], pattern=[[-1, 4]], base=4, channel_multiplier=1), [], ["mi2"])
            V(lambda e: e.tensor_scalar(out=ma2[:], in0=mi2[:], scalar1=3, scalar2=None, op0=ALU.bitwise_and), ["mi2"], ["ma2"])
            V(lambda e: e.tensor_copy(out=mf2[:], in_=ma2[:]), ["ma2"], ["mf2"])
            V(lambda e: e.tensor_scalar(out=mf2[:], in0=mf2[:], scalar1=0.0, scalar2=None, op0=ALU.is_equal), ["mf2"], ["mf2"])
            V(lambda e: e.tensor_copy(out=mg2[:], in_=mi2[:]), ["mi2"], ["mg2"])
            for i in range(3):
                V(lambda e, i=i: e.tensor_copy(out=Msf[:, i, :], in_=mf2[:]), ["mf2"], ["Msf"])
            V(lambda e: e.tensor_scalar(out=mh2[:], in0=mg2[:], scalar1=4.0, scalar2=None, op0=ALU.is_ge), ["mg2"], ["mh2"])
            TT("dve", Msf[:, 3, :], mf2[:], mh2[:], ALU.add, ["mf2", "mh2"], ["Msf"])
            for tp in range(4):
                V(lambda e, tp=tp: e.memset(Msf[:, 4 + tp, :], 0.0), [], ["Msf"])
                V(lambda e, tp=tp: e.memset(Msf[:, 4 + tp, tp:tp + 1], 1.0), [], ["Msf"])
            V(lambda e: e.tensor_scalar(out=mh2[:], in0=mg2[:], scalar1=4.0, scalar2=None, op0=ALU.is_le), ["mg2", "Msf"], ["mh2"])
            V(lambda e: e.tensor_scalar(out=mf2[:], in0=mg2[:], scalar1=4.0, scalar2=2.0, op0=ALU.is_equal, op1=ALU.mult), ["mg2", "Msf"], ["mf2"])
            TT("dve", Msf[:, 8, :], mf2[:], mh2[:], ALU.add, ["mf2", "mh2"], ["Msf"])
            for e_ in range(2):
                V(lambda e, e_=e_: e.tensor_copy(out=Msb[:, e_, :, :], in_=Msf[:, :, :]), ["Msf"], ["Msb"])

            ld = [0]
            for b in range(SB):
                bi = b % 2
                for tile in range(8):
                    rows = slice(1536 + 128 * tile, 1536 + 128 * tile + 128) if tile < 4 else slice(tile - 4, 2048, 16)
                    i = ld[0] % 6
                    ld[0] += 1
                    cx.dma("sp", kst[i][:, :], cwk[b, rows, :], writes=["kst%d" % i])
                    cx.dma("sp", vst[i][:, :], cwv[b, rows, :], writes=["svst%d" % i])
                    pi = ld[0] % 2
                    for hp in range(4):
                        P(lambda pe, i=i, hp=hp, pi=pi: pe.transpose(out=pTs[pi][:, hp, :], in_=kst[i][:, 128 * hp:128 * hp + 128], identity=ident_f[:, :]),
                          ["kst%d" % i, "ident_f"], ["pTs%d" % pi])
                    A(lambda e, pi=pi, bi=bi, tile=tile: e.activation(out=kTs[bi][:, :, tile, :], in_=pTs[pi][:, :, :], func=AF.Copy), [], ["kTs%d" % bi, "pTs%d" % pi])
                    cx.op("pool" if tile % 2 == 0 else "dve", lambda e, i=i, bi=bi, tile=tile: e.tensor_copy(
                        out=Vs[bi][:, tile, :, :], in_=vst[i][:, :].rearrange("p (h d) -> p h d", h=8)), reads=["svst%d" % i], writes=["Vs%d_%d" % (bi, tile % 2)])
                tok = slice(2048 + 4 * b, 2048 + 4 * b + 4)
                ai = b % 2
                def sc_fn(hp, b=b, bi=bi, tok=tok):
                    si = (4 * b + hp) % 2
                    Sv = Sps[si][:, 0:72].rearrange("p (e t q) -> p e t q", e=2, t=9)
                    for e_ in range(2):
                        pb = 64 * e_
                        qap = qT[pb:pb + 64, hp, tok]
                        for tile in range(8):
                            P(lambda pe, Sv=Sv, e_=e_, tile=tile, pb=pb, hp=hp, bi=bi, qap=qap: pe.matmul(
                                Sv[:, e_, tile, :], lhsT=kTs[bi][pb:pb + 64, hp, tile, :], rhs=qap, start=True, stop=True), ["kTs%d" % bi, "qT"], ["Sps%d" % si])
                        P(lambda pe, Sv=Sv, e_=e_, pb=pb, hp=hp, qap=qap, tok=tok: pe.matmul(
                            Sv[0:4, e_, 8, :], lhsT=kT[pb:pb + 64, hp, tok], rhs=qap, start=True, stop=True), ["kT", "qT"], ["Sps%d" % si])
                    A(lambda e, si=si: e.activation(out=Ps[si][:, :, :, :].rearrange("p e t q -> p (e t q)"), in_=Sps[si][:, 0:72], func=AF.Exp, scale=0.125),
                      [], ["Ps%d" % si, "Sps%d" % si])
                    TT("dve", Ps[si][:, :, :, :], Ps[si][:, :, :, :], Msb[:, :, :, :], ALU.mult, ["Msb"], ["Ps%d" % si])

                def pv_fn(hp, b=b, bi=bi, ai=ai):
                    si = (4 * b + hp) % 2
                    Av = accs[ai][:, 0:32].rearrange("p (a e q) -> p a e q", a=4, e=2)
                    for e_ in range(2):
                        h = 2 * hp + e_
                        vo, do = (0, 64) if e_ == 0 else (64, 0)
                        for tile in range(9):
                            if tile < 8:
                                vap, oap, pap = Vs[bi][:, tile, h, :], ones_b[:, 0:64], Ps[si][:, e_, tile, :]
                            else:
                                vap, oap, pap = Vn[0:4, b, h, :], ones_b[0:4, 0:64], Ps[si][0:4, e_, 8, :]
                            P(lambda pe, vap=vap, pap=pap, tile=tile, vo=vo, hp=hp, e_=e_, Av=Av: pe.matmul(
                                Av[vo:vo + 64, hp, e_, :], lhsT=vap, rhs=pap, start=(tile == 0), stop=(tile == 8), skip_group_check=True, tile_position=(0, vo)),
                                ["Vs%d_0" % bi, "Vs%d_1" % bi, "Vn", "Ps%d" % si], ["accs%d" % ai])
                            P(lambda pe, oap=oap, pap=pap, tile=tile, do=do, hp=hp, e_=e_, Av=Av: pe.matmul(
                                Av[do:do + 64, hp, e_, :], lhsT=oap, rhs=pap, start=(tile == 0), stop=(tile == 8), skip_group_check=True, tile_position=(0, do)),
                                ["ones_b", "Ps%d" % si], ["accs%d" % ai])

                sc_fn(0)
                for hp in range(4):
                    if hp + 1 < 4:
                        sc_fn(hp + 1)
                    pv_fn(hp)
                Av = accs[ai][:, 0:32].rearrange("p (a e q) -> p a e q", a=4, e=2)
                for e_ in range(2):
                    vo, do = (0, 64) if e_ == 0 else (64, 0)
                    V(lambda e, Av=Av, do=do, e_=e_: e.reciprocal(out=rec[do:do + 64, :, :], in_=Av[do:do + 64, :, e_, :]), [], ["s_rec", "accs%d" % ai])
                    V(lambda e, Av=Av, do=do, vo=vo, e_=e_, tok=tok: e.tensor_tensor(out=oas[vo:vo + 64, :, 4 * (tok.start - 2048) // 4:4 * (tok.start - 2048) // 4 + 4],
                                                                                      in0=Av[vo:vo + 64, :, e_, :], in1=rec[do:do + 64, :, :], op=ALU.mult),
                      ["s_rec"], ["oas", "accs%d" % ai])
            A(lambda e: e.activation(out=osq[:, :, :], in_=oas[:, :, :], func=AF.Square), ["oas"], ["s_osq"])
            for hp in range(4):
                P(lambda pe, hp=hp: pe.matmul(pss[:, 0:ST], lhsT=ones_b[:, :], rhs=osq[:, hp, :], start=(hp == 0), stop=(hp == 3)), ["ones_b", "s_osq"], ["s_pss"])
            A(lambda e: e.activation(out=rstd[:, :], in_=pss[:, 0:ST], func=AF.Sqrt, bias=eps_t[:, 0:1], scale=1.0 / 512.0), ["eps_t"], ["s_rstd", "s_pss"])
            V(lambda e: e.reciprocal(out=rstd[:, :], in_=rstd[:, :]), ["s_rstd"], ["s_rstd"])
            for hp in range(4):
                V(lambda e, hp=hp: e.scalar_tensor_tensor(out=attT[:, hp, 2048:T], in0=oas[:, hp, :], scalar=gatt[:, hp:hp + 1],
                                                          in1=rstd[:, :], op0=ALU.mult, op1=ALU.mult), ["oas", "gatt", "s_rstd"], ["attT"])
            cx.barrier()
        pqk.close()

        TILES = [(128 * i, 128) for i in range(16)] + [(2048, ST)]
        x1s = nc.dram_tensor("x1s", [T, D], F32, kind="Internal").ap()
        x2s = nc.dram_tensor("x2s", [T, D], F32, kind="Internal").ap()

        def xin_rows(t0, rows):
            return x_p[t0:t0 + rows, :] if t0 < 2048 else x_s[:, :]

        def dense_ln(tag, nk, W, lhs_fn, in_keys, prep_fn, xres_fn, g_d, b_d, out_fn, xT_out, xT_key, tiles=None, nbuf=3, wb_pre=None):
            with ExitStack() as ph:
                if wb_pre is not None:
                    wb, wkeys = wb_pre
                else:
                    wb = sb(tag + "wb", [128, nk, 1024], BF16, ph)
                    wst = [sb(tag + "wst%d" % i, [128, 1, 1024], F32, ph) for i in range(2)]
                    Wv = W.rearrange("(kc p) c -> p kc c", p=128)
                    for gi, g0 in enumerate(range(0, nk, 1)):
                        i = gi % 2
                        cx.dma("sp", wst[i][:, :, :], Wv[:, g0:g0 + 1, :], writes=[tag + "wst%d" % i])
                        cx.op("pool" if i == 0 else "dve", lambda e, i=i, g0=g0: e.tensor_copy(out=wb[:, g0:g0 + 1, :], in_=wst[i][:, :, :]),
                              reads=[tag + "wst%d" % i], writes=[tag + "wb%d" % i])
                    wkeys = [tag + "wb0", tag + "wb1"]
                gam = sb(tag + "gam", [128, 1024], F32, ph)
                bet = sb(tag + "bet", [128, 1024], F32, ph)
                cx.dma("sp", gam[:, :], g_d.rearrange("(o c) -> o c", o=1).broadcast_to([128, 1024]), writes=[tag + "gam"])
                cx.dma("act", bet[:, :], b_d.rearrange("(o c) -> o c", o=1).broadcast_to([128, 1024]), writes=[tag + "bet"])
                xr = [sb(tag + "xr%d" % i, [128, 1024], F32, ph) for i in range(nbuf)]
                rr = [sb(tag + "rr%d" % i, [128, 1024], F32, ph) for i in range(nbuf)]
                nn = [sb(tag + "nn%d" % i, [128, 1024], F32, ph) for i in range(nbuf)]
                nb = [sb(tag + "nb%d" % i, [128, 1024], BF16, ph) for i in range(nbuf)] if xT_out is not None else None
                st6 = sb(tag + "st6", [128, 2, 6], F32, ph)
                mv = sb(tag + "mv", [128, 2], F32, ph)
                sd = sb(tag + "sd", [128, 1], F32, ph)
                nbi = sb(tag + "nbi", [128, 1], F32, ph)
                pd = [ps(tag + "pd%d" % i, [128, 1024], F32, ph) for i in range(nbuf)]
                ptr = [ps(tag + "ptr%d" % i, [128, 8, 128], BF16, ph) for i in range(2)]
                pending = []
                for ti, (t0, rows) in enumerate(TILES if tiles is None else tiles):
                    i = ti % nbuf
                    j2 = ti % 2
                    ik = list(in_keys)
                    if prep_fn is not None:
                        ik = ik + prep_fn(ti, t0, rows)
                    cx.dma("sp", xr[i][0:rows, :], xres_fn(t0, rows), writes=[tag + "xr%d" % i])
                    for half in range(2):
                        for kc in range(nk):
                            P(lambda pe, i=i, half=half, kc=kc, t0=t0, rows=rows: pe.matmul(
                                pd[i][0:rows, half * 512:half * 512 + 512], lhsT=lhs_fn(kc, ti, t0, rows), rhs=wb[:, kc, half * 512:half * 512 + 512],
                                start=(kc == 0), stop=(kc == nk - 1)), wkeys + ik, [tag + "pd%d_%d" % (i, half)])
                    for half in range(2):
                        hs = slice(half * 512, half * 512 + 512)
                        V(lambda e, i=i, rows=rows, hs=hs: e.scalar_tensor_tensor(out=rr[i][0:rows, hs], in0=xr[i][0:rows, hs], scalar=ALPHA, in1=pd[i][0:rows, hs],
                                                                                  op0=ALU.mult, op1=ALU.add), [tag + "xr%d" % i], [tag + "rr%d_%d" % (i, half), tag + "pd%d_%d" % (i, half)])
                        V(lambda e, i=i, rows=rows, hs=hs, half=half: e.bn_stats(out=st6[0:rows, half, :], in_=rr[i][0:rows, hs]), [tag + "rr%d_%d" % (i, half)], [tag + "st6_%d" % half])
                    V(lambda e, rows=rows: e.bn_aggr(out=mv[0:rows, :], in_=st6[0:rows, :, :].rearrange("p a b -> p (a b)")), [tag + "st6_0", tag + "st6_1"], [tag + "mv"])
                    A(lambda e, rows=rows: e.activation(out=sd[0:rows, :], in_=mv[0:rows, 1:2], func=AF.Sqrt, bias=eps_t[0:rows, 0:1], scale=1.0), [tag + "mv", "eps_t"], [tag + "sd"])
                    V(lambda e, rows=rows: e.reciprocal(out=sd[0:rows, :], in_=sd[0:rows, :]), [tag + "sd"], [tag + "sd"])
                    V(lambda e, rows=rows: e.scalar_tensor_tensor(out=nbi[0:rows, :], in0=mv[0:rows, 0:1], scalar=-1.0, in1=sd[0:rows, :], op0=ALU.mult, op1=ALU.mult),
                      [tag + "mv", tag + "sd"], [tag + "nbi"])
                    A(lambda e, i=i, rows=rows: e.activation(out=nn[i][0:rows, :], in_=rr[i][0:rows, :], func=AF.Identity, scale=sd[0:rows, 0:1], bias=nbi[0:rows, 0:1]),
                      [tag + "rr%d_0" % i, tag + "rr%d_1" % i, tag + "sd", tag + "nbi"], [tag + "nn%d" % i])
                    TT("dve", nn[i][0:rows, :], nn[i][0:rows, :], gam[0:rows, :], ALU.mult, [tag + "gam", tag + "nn%d" % i], [tag + "nn%d" % i])
                    TT("dve", nn[i][0:rows, :], nn[i][0:rows, :], bet[0:rows, :], ALU.add, [tag + "bet", tag + "nn%d" % i], [tag + "nn%d" % i])
                    def tail_fn(i=i, j2=j2, t0=t0, rows=rows):
                        cx.dma("sp", out_fn(t0, rows), nn[i][0:rows, :], reads=[tag + "nn%d" % i], writes=[tag + "out"])
                        if xT_out is not None:
                            G(lambda e: e.tensor_copy(out=nb[i][0:rows, :], in_=nn[i][0:rows, :]), [tag + "nn%d" % i], [tag + "nb%d" % i])
                            for kc in range(8):
                                P(lambda pe, kc=kc: pe.transpose(out=ptr[j2][:, kc, 0:rows], in_=nb[i][0:rows, kc * 128:(kc + 1) * 128],
                                                                 identity=ident_b[0:rows, 0:rows]), [tag + "nb%d" % i, "ident_b"], [tag + "ptr%d" % j2])
                            A(lambda e: e.activation(out=xT_out[:, :, t0:t0 + rows], in_=ptr[j2][:, :, 0:rows], func=AF.Copy), [], [xT_key, tag + "ptr%d" % j2])
                    if pending:
                        pending.pop(0)()
                    pending.append(tail_fn)
                while pending:
                    pending.pop(0)()
                cx.barrier()

        pxa = ExitStack()
        x1T = sb("x1T", [128, 8, T], BF16, pxa)
        with ExitStack() as ph6:
            mixb = [sb("mixb%d" % i, [128, 4, 128], BF16, ph6) for i in range(2)]

            def prep6(ti, t0, rows):
                i = ti % 2
                G(lambda e: e.tensor_copy(out=mixb[i][:, :, 0:rows], in_=yssm[:, :, t0:t0 + rows]), ["yssm"], ["mixb%d" % i])
                return ["mixb%d" % i]

            def lhs6(kc, ti, t0, rows):
                if kc < 4:
                    return attT[:, kc, t0:t0 + rows]
                return mixb[ti % 2][:, kc - 4, 0:rows]

            dense_ln("l1", 8, w_out, lhs6, ["attT"], prep6, xin_rows, ln_g[0], ln_b[0], lambda t0, rows: x1s[t0:t0 + rows, :], x1T, "x1T")
        if STG < 7:
            cx.enabled = False
        pqm = ExitStack()
        qmT = sb("qmT", [128, 8, T], BF16, pqm)
        QK = ["qm%d" % h for h in range(4)]
        with ExitStack() as ph:
            wqb = sb("wqb", [128, 8, 1024], BF16, ph)
            wqs = [sb("wqs%d" % i, [128, 2, 1024], F32, ph) for i in range(2)]
            Wv = w_mq.rearrange("(kc p) c -> p kc c", p=128)
            for gi in range(4):
                i = gi % 2
                cx.dma("sp" if i == 0 else "act", wqs[i][:, :, :], Wv[:, 2 * gi:2 * gi + 2, :], writes=["wqs%d" % i])
                cx.op("pool" if i == 0 else "dve", lambda e, i=i, gi=gi: e.tensor_copy(out=wqb[:, 2 * gi:2 * gi + 2, :], in_=wqs[i][:, :, :]),
                      reads=["wqs%d" % i], writes=["wqb%d" % i])
            pq = [ps("pq%d" % i, [128, 512], F32, ph) for i in range(4)]
            cnt = 0
            for oc in range(8):
                for nt in range(5):
                    t0 = nt * 512
                    n = 512 if nt < 4 else ST
                    p = cnt % 4
                    cnt += 1
                    for kc in range(8):
                        P(lambda pe, p=p, oc=oc, kc=kc, t0=t0, n=n: pe.matmul(pq[p][:, 0:n], lhsT=wqb[:, kc, oc * 128:(oc + 1) * 128], rhs=x1T[:, kc, t0:t0 + n],
                                                                            start=(kc == 0), stop=(kc == 7)), ["wqb0", "wqb1", "x1T"], ["pq%d" % p])
                    if cnt % 2:
                        A(lambda e, p=p, oc=oc, t0=t0, n=n: e.activation(out=qmT[:, oc, t0:t0 + n], in_=pq[p][:, 0:n], func=AF.Copy), [], ["qm%d" % (oc // 2), "pq%d" % p])
                    else:
                        V(lambda e, p=p, oc=oc, t0=t0, n=n: e.tensor_copy(out=qmT[:, oc, t0:t0 + n], in_=pq[p][:, 0:n]), [], ["qm%d" % (oc // 2), "pq%d" % p])
            cx.barrier()

        with ExitStack() as ph:
            mst = [sb("mst%d" % i, [128, 1024], F32, ph) for i in range(4)]
            mkb = [sb("mkb%d" % i, [128, 1024], BF16, ph) for i in range(4)]
            mkT = [sb("mkT%d" % i, [128, 8, 256], BF16, ph) for i in range(2)]
            mvb = [sb("mvb%d" % i, [128, 2, 1024], BF16, ph) for i in range(2)]
            Pm = [sb("Pm%d" % i, [128, 2, 512], BF16, ph) for i in range(2)]
            recm = sb("recm", [128, 512], F32, ph)
            ptm = [ps("ptm%d" % i, [128, 8, 128], BF16, ph) for i in range(2)]
            Sm = [ps("Sm%d" % i, [128, 512], F32, ph) for i in range(2)]
            Om = [ps("Om%d" % i, [128, 512], F32, ph) for i in range(2)]
            Dm = ps("Dm", [128, 512], F32, ph)
            ldc = [0]

            def load_mem(kd, vd, slot):
                for mt in range(2):
                    i = ldc[0] % 4
                    ldc[0] += 1
                    cx.dma("sp", mst[i][:, :], kd[mt * 128:(mt + 1) * 128, :], writes=["mst%d" % i])
                    V(lambda e, i=i: e.tensor_copy(out=mkb[i][:, :], in_=mst[i][:, :]), ["mst%d" % i], ["mkb%d" % i])
                    for oc in range(8):
                        P(lambda pe, i=i, oc=oc: pe.transpose(out=ptm[i % 2][:, oc, :], in_=mkb[i][:, oc * 128:(oc + 1) * 128], identity=ident_b[:, :]),
                          ["mkb%d" % i, "ident_b"], ["ptm%d" % (i % 2)])
                    A(lambda e, i=i, mt=mt, slot=slot: e.activation(out=mkT[slot][:, :, mt * 128:(mt + 1) * 128], in_=ptm[i % 2][:, :, :], func=AF.Copy), [], ["mkT%d" % slot, "ptm%d" % (i % 2)])
                for mt in range(2):
                    i = ldc[0] % 4
                    ldc[0] += 1
                    cx.dma("sp", mst[i][:, :], vd[mt * 128:(mt + 1) * 128, :], writes=["mst%d" % i])
                    V(lambda e, i=i, mt=mt, slot=slot: e.tensor_copy(out=mvb[slot][:, mt, :], in_=mst[i][:, :]), ["mst%d" % i], ["mvb%d" % slot])

            if STG < 7.2:
                cx.enabled = False
            load_mem(mk_p, mv_p, 0)
            it = 0
            for nt in range(4):
                tok = slice(512 * nt, 512 * nt + 512)
                for h in range(4):
                    pi = it % 2
                    it += 1
                    for mt in range(2):
                        for c in range(2):
                            P(lambda pe, mt=mt, c=c, h=h, tok=tok: pe.matmul(Sm[mt][:, :], lhsT=mkT[0][:, 2 * h + c, mt * 128:(mt + 1) * 128], rhs=qmT[:, 2 * h + c, tok],
                                                                         start=(c == 0), stop=(c == 1)), ["mkT0", "qm%d" % h], ["Sm%d" % mt])
                        A(lambda e, mt=mt, pi=pi: e.activation(out=Pm[pi][:, mt, :], in_=Sm[mt][:, :], func=AF.Exp, scale=1.0 / 16.0), [], ["Pm%d_%d" % (pi, mt), "Sm%d" % mt])
                    pk = ["Pm%d_0" % pi, "Pm%d_1" % pi]
                    for c in range(2):
                        for mt in range(2):
                            P(lambda pe, mt=mt, c=c, h=h, pi=pi: pe.matmul(Om[c][:, :], lhsT=mvb[0][:, mt, h * 256 + c * 128:h * 256 + c * 128 + 128], rhs=Pm[pi][:, mt, :],
                                                                        start=(mt == 0), stop=(mt == 1)), ["mvb0"] + pk, ["Om%d" % c])
                    for mt in range(2):
                        P(lambda pe, mt=mt, pi=pi: pe.matmul(Dm[:, :], lhsT=ones_b[:, :], rhs=Pm[pi][:, mt, :], start=(mt == 0), stop=(mt == 1)), ["ones_b"] + pk, ["Dm"])
                    V(lambda e: e.reciprocal(out=recm[:, :], in_=Dm[:, :]), [], ["recm", "Dm"])
                    for c in range(2):
                        V(lambda e, c=c, h=h, tok=tok: e.tensor_tensor(out=qmT[:, 2 * h + c, tok], in0=Om[c][:, :], in1=recm[:, :], op=ALU.mult),
                          ["recm"], ["qm%d" % h, "Om%d" % c])
            cx.barrier()

            if STG < 7.3:
                cx.enabled = False
            for b in range(SB):
                slot = b % 2
                load_mem(cmk[b], cmv[b], slot)
                tok = slice(2048 + 4 * b, 2048 + 4 * b + 4)
                Sv = Sm[slot][:, 0:32].rearrange("p (h m q) -> p h m q", h=4, m=2)
                Ov = Om[slot][:, 0:32].rearrange("p (h c q) -> p h c q", h=4, c=2)
                Dv = Om[slot][:, 32:48].rearrange("p (h q) -> p h q", h=4)
                Pv = Pm[slot][:, 0, 0:32].rearrange("p (h m q) -> p h m q", h=4, m=2)
                for h in range(4):
                    for mt in range(2):
                        for c in range(2):
                            P(lambda pe, mt=mt, c=c, h=h, tok=tok, Sv=Sv, slot=slot: pe.matmul(
                                Sv[:, h, mt, :], lhsT=mkT[slot][:, 2 * h + c, mt * 128:(mt + 1) * 128], rhs=qmT[:, 2 * h + c, tok], start=(c == 0), stop=(c == 1),
                                skip_group_check=True), ["mkT%d" % slot] + QK, ["Sm%d" % slot])
                A(lambda e, slot=slot: e.activation(out=Pm[slot][:, 0, 0:32], in_=Sm[slot][:, 0:32], func=AF.Exp, scale=1.0 / 16.0), [], ["Pm%d_0" % slot, "Sm%d" % slot])
                for h in range(4):
                    for c in range(2):
                        for mt in range(2):
                            P(lambda pe, mt=mt, c=c, h=h, Ov=Ov, Pv=Pv, slot=slot: pe.matmul(
                                Ov[:, h, c, :], lhsT=mvb[slot][:, mt, h * 256 + c * 128:h * 256 + c * 128 + 128], rhs=Pv[:, h, mt, :], start=(mt == 0), stop=(mt == 1),
                                skip_group_check=True), ["mvb%d" % slot, "Pm%d_0" % slot], ["Om%d" % slot])
                    for mt in range(2):
                        P(lambda pe, mt=mt, h=h, Dv=Dv, Pv=Pv: pe.matmul(Dv[:, h, :], lhsT=ones_b[:, :], rhs=Pv[:, h, mt, :], start=(mt == 0), stop=(mt == 1),
                                                                      skip_group_check=True), ["ones_b", "Pm%d_0" % slot], ["Om%d" % slot])
                V(lambda e, Dv=Dv: e.reciprocal(out=recm[:, 0:16].rearrange("p (h q) -> p h q", h=4), in_=Dv), [], ["recm", "Om%d" % slot])
                V(lambda e, Ov=Ov, tok=tok: e.tensor_tensor(out=qmT[:, :, tok].rearrange("p (h c) q -> p h c q", c=2), in0=Ov,
                                                            in1=recm[:, 0:16].rearrange("p (h q) -> p h q", h=4).unsqueeze(2).broadcast_to([128, 4, 2, 4]), op=ALU.mult),
                  ["recm"], QK + ["Om%d" % slot])
            cx.barrier()

        if STG < 7.4:
            cx.enabled = False
        dense_ln("l2", 8, w_mo, lambda kc, ti, t0, rows: qmT[:, kc, t0:t0 + rows], QK, None,
                 lambda t0, rows: x1s[t0:t0 + rows, :], ln_g[1], ln_b[1], lambda t0, rows: x2s[t0:t0 + rows, :], x1T, "x1T")
        pqm.close()


        if STG < 8:
            cx.enabled = False
        x2T = x1T
        GROUPS = [(0, 768, [(0, 512), (512, 256)]), (768, 768, [(768, 512), (1280, 256)]), (1536, 576, [(1536, 512), (2048, ST)])]
        phT = ExitStack()
        wdb = sb("wdb", [128, 22, 1024], BF16, phT)
        hT = sb("hT", [128, 22, 768], BF16, phT)
        with ExitStack() as ph:
            wdst = [sb("wdst%d" % i, [128, 1, 1024], F32, ph) for i in range(3)]
            Wdv = w_down.rearrange("(kc p) c -> p kc c", p=128)
            for g0 in range(22):
                i = g0 % 3
                cx.dma("sp", wdst[i][:, :, :], Wdv[:, g0:g0 + 1, :], writes=["wdst%d" % i])
                cx.op(("act", "dve", "pool")[i], (lambda e, i=i, g0=g0: e.activation(out=wdb[:, g0:g0 + 1, :], in_=wdst[i][:, :, :], func=AF.Copy)) if i == 0 else
                      (lambda e, i=i, g0=g0: e.tensor_copy(out=wdb[:, g0:g0 + 1, :], in_=wdst[i][:, :, :])), reads=["wdst%d" % i], writes=["wdb%d" % i])
            cx.barrier()
        WDK = ["wdb0", "wdb1", "wdb2"]
        wgv = w_gate.rearrange("(kc p) f -> p kc f", p=128)
        wuv = w_up.rearrange("(kc p) f -> p kc f", p=128)
        for gidx, (g0, glen, subs) in enumerate(GROUPS):
            with ExitStack() as ph:
                gst = [sb("gst%d_%d" % (gidx, i), [128, 2, 8, 128], F32, ph) for i in range(2)]
                gbf = [sb("gbf%d_%d" % (gidx, i), [128, 2, 8, 128], BF16, ph) for i in range(2)]
                sg = [sb("sg%d_%d" % (gidx, i), [128, 512], F32, ph) for i in range(4)]
                tg = [sb("tg%d_%d" % (gidx, i), [128, 512], F32, ph) for i in range(4)]
                pg = [ps("pg%d_%d" % (gidx, i), [128, 512], F32, ph) for i in range(4)]
                pu = [ps("pu%d_%d" % (gidx, i), [128, 512], F32, ph) for i in range(4)]
                it = 0
                def wload(fc_):
                    i_ = fc_ % 2
                    cx.dma("sp", gst[i_][:, 0, :, :], wgv[:, :, fc_ * 128:(fc_ + 1) * 128], writes=["gst%d_0" % i_])
                    cx.dma("sp", gst[i_][:, 1, :, :], wuv[:, :, fc_ * 128:(fc_ + 1) * 128], writes=["gst%d_1" % i_])
                wload(0)
                for fc in range(22):
                    i = fc % 2
                    G(lambda e, i=i: e.tensor_copy(out=gbf[i][:, 0, :, :], in_=gst[i][:, 0, :, :]), ["gst%d_0" % i], ["gbf%d_0" % i])
                    V(lambda e, i=i: e.tensor_copy(out=gbf[i][:, 1, :, :], in_=gst[i][:, 1, :, :]), ["gst%d_1" % i], ["gbf%d_1" % i])
                    if fc + 1 < 22:
                        wload(fc + 1)
                    for (t0, n) in subs:
                        p = it % 4
                        it += 1
                        for kc in range(8):
                            P(lambda pe, p=p, i=i, kc=kc, t0=t0, n=n: pe.matmul(pg[p][:, 0:n], lhsT=gbf[i][:, 0, kc, :], rhs=x2T[:, kc, t0:t0 + n],
                                                                             start=(kc == 0), stop=(kc == 7)), ["gbf%d_0" % i, "x1T"], ["pg%d" % p])
                        for kc in range(8):
                            P(lambda pe, p=p, i=i, kc=kc, t0=t0, n=n: pe.matmul(pu[p][:, 0:n], lhsT=gbf[i][:, 1, kc, :], rhs=x2T[:, kc, t0:t0 + n],
                                                                             start=(kc == 0), stop=(kc == 7)), ["gbf%d_1" % i, "x1T"], ["pu%d" % p])
                        A(lambda e, p=p, n=n: e.activation(out=sg[p][:, 0:n], in_=pg[p][:, 0:n], func=AF.Sigmoid), [], ["sg%d" % p, "pg%d" % p])
                        V(lambda e, p=p, n=n: e.tensor_tensor(out=tg[p][:, 0:n], in0=pg[p][:, 0:n], in1=sg[p][:, 0:n], op=ALU.mult), ["sg%d" % p], ["tg%d" % p, "pg%d" % p])
                        V(lambda e, p=p, n=n, t0=t0, fc=fc, g0=g0: e.tensor_tensor(out=hT[:, fc, t0 - g0:t0 - g0 + n], in0=pu[p][:, 0:n], in1=tg[p][:, 0:n], op=ALU.mult),
                          ["tg%d" % p], ["hT", "pu%d" % p])
                cx.barrier()
            gt = [(t0, rows) for (t0, rows) in TILES if g0 <= t0 < g0 + glen]
            dense_ln("l3_%d" % gidx, 22, w_down, lambda kc, ti, t0, rows, g0=g0: hT[:, kc, t0 - g0:t0 - g0 + rows], ["hT"], None,
                     lambda t0, rows: x2s[t0:t0 + rows, :], ln_g[2], ln_b[2],
                     lambda t0, rows: (y_p[t0:t0 + rows, :] if t0 < 2048 else y_s[:, :]), None, None, tiles=gt, nbuf=3, wb_pre=(wdb, []))
        phT.close()
        cx.enabled = True
        pxa.close()
        cx.finish()
    return nc


_NC_CACHE = {}
_DEV = {}


def kernel(**inp):
    f = lambda a: np.ascontiguousarray(np.asarray(a, dtype=np.float32))
    if "nc" not in _NC_CACHE:
        _NC_CACHE["nc"] = build()
    nc = _NC_CACHE["nc"]
    shared = {
        "w_in": f(inp["w_in"][0]), "g_att": f(inp["g_att"][0]), "g_ssm": f(inp["g_ssm"][0]),
        "a_re": f(inp["ssm_a_re"][0]).reshape(2048), "a_im": f(inp["ssm_a_im"][0]).reshape(2048),
        "log_dt": f(inp["ssm_log_dt"][0]),
        "b_re": f(inp["ssm_b_re"][0]).reshape(2048, 16), "b_im": f(inp["ssm_b_im"][0]).reshape(2048, 16),
        "c_re": f(inp["ssm_c_re"][0]).reshape(512, 64), "c_im": f(inp["ssm_c_im"][0]).reshape(512, 64),
        "d_skip": f(inp["ssm_d"][0]).reshape(512), "w_glu": f(inp["w_glu"][0]), "b_glu": f(inp["b_glu"][0]),
        "w_out": f(inp["w_out"][0]),
        "ln1_g": f(inp["ln1_g"][0]), "ln1_b": f(inp["ln1_b"][0]),
        "ln2_g": f(inp["ln2_g"][0]), "ln2_b": f(inp["ln2_b"][0]),
        "ln3_g": f(inp["ln3_g"][0]), "ln3_b": f(inp["ln3_b"][0]),
        "w_mq": f(inp["w_mem_q"][0]), "w_mk": f(inp["w_mem_k"][0]), "w_mv": f(inp["w_mem_v"][0]),
        "w_mo": f(inp["w_mem_o"][0]),
        "w_gate": f(inp["w_gate"][0]), "w_up": f(inp["w_up"][0]), "w_down": f(inp["w_down"][0]),
    }
    in_maps = []
    for c in range(NCORES):
        b0 = c * SB
        m = dict(shared)
        m["x_p"] = f(inp["x_prompt"][c])
        m["x_s"] = f(inp["x_sample"][b0:b0 + SB]).reshape(ST, D)
        m["cwk"] = f(inp["cache_win_k"][0, b0:b0 + SB]).reshape(SB, 2048, DATT)
        m["cwv"] = f(inp["cache_win_v"][0, b0:b0 + SB]).reshape(SB, 2048, DATT)
        m["s_re"] = f(inp["state_ssm_re"][0, b0:b0 + SB]).reshape(SB, 2048)
        m["s_im"] = f(inp["state_ssm_im"][0, b0:b0 + SB]).reshape(SB, 2048)
        m["cmk"] = f(inp["cache_mem_k"][0, b0:b0 + SB]).reshape(SB, NMEM, D)
        m["cmv"] = f(inp["cache_mem_v"][0, b0:b0 + SB]).reshape(SB, NMEM, D)
        m["memp"] = f(inp["mem_prompt"][c])
        in_maps.append(m)
    nrun = _DEV.get("cores", NCORES)
    res = run_bass_kernel_spmd(nc, in_maps[:nrun], core_ids=list(range(nrun)))
    R = list(res.results)
    _DEV["raw"] = R
    while len(R) < NCORES:
        R.append({k: np.zeros_like(np.asarray(v)) for k, v in R[0].items()})
    cat = lambda k: np.stack([np.asarray(R[c][k], dtype=np.float32) for c in range(NCORES)], 0)
    y_p = cat("y_p")
    y_s = cat("y_s").reshape(128, 4, D)
    wk_p = cat("wk_p").reshape(1, 8, S, 8, 64)
    wv_p = cat("wv_p").reshape(1, 8, S, 8, 64)
    wk_s = cat("wk_s").reshape(1, 128, 4, 8, 64)
    wv_s = cat("wv_s").reshape(1, 128, 4, 8, 64)
    hr_p = cat("hr_p").reshape(1, 8, 32, 64)
    hi_p = cat("hi_p").reshape(1, 8, 32, 64)
    hr_s = cat("hr_s").reshape(1, 128, 32, 64)
    hi_s = cat("hi_s").reshape(1, 128, 32, 64)
    mk_p = cat("mk_p").reshape(1, 8, NMEM, 4, 256)
    mv_p = cat("mv_p").reshape(1, 8, NMEM, 4, 256)
    return (y_p, y_s, wk_p, wv_p, wk_s, wv_s, hr_p, hi_p, hr_s, hi_s, mk_p, mv_p)
```
